# Optimizing a Trainium2 kernel written in Bass

```python
import jax, jax.numpy as jnp
from jax import lax
import numpy as np

D_MODEL = 1024
BATCH = 4
SEQ = 8192
DEPTH = 2

CHUNK = 64
N_MIXERS = 2
N_GLA_LAYERS = (DEPTH + N_MIXERS - 1) // N_MIXERS
N_MLSTM_LAYERS = DEPTH // N_MIXERS
EPS = 1e-6

GLA_HEADS = 4
GLA_HQK = D_MODEL // 2
GLA_HV = D_MODEL
GLA_DK = GLA_HQK // GLA_HEADS
GLA_DV = GLA_HV // GLA_HEADS
GLA_GATE_RANK = 16
GLA_GATE_TAU = 16.0
GLA_IN = 2 * GLA_HQK + 2 * GLA_HV + GLA_GATE_RANK

MLSTM_HEADS = 4
MLSTM_INNER = 2 * D_MODEL
MLSTM_DH = MLSTM_INNER // MLSTM_HEADS
CONV_W = 4
QKV_BLOCK = 4
MLSTM_NB = MLSTM_INNER // QKV_BLOCK
MLSTM_IN = 2 * MLSTM_INNER + 2 * MLSTM_HEADS

PEER_HEADS = 8
N_KEYS = 128
N_EXPERTS = N_KEYS * N_KEYS
PEER_TOPK = 16
PEER_DQ = 256
PEER_BLOCK = 128

kernel_name = "hybrid_gla_mlstm_peer_chunk_causal"


def rmsnorm(x, g):
    xf = x.astype(jnp.float32)
    y = xf * lax.rsqrt(jnp.mean(xf * xf, axis=-1, keepdims=True) + EPS)
    return (y * g.astype(jnp.float32)).astype(x.dtype)


def to_chunks(t, heads):
    b, s, _ = t.shape
    return t.reshape(b, s // CHUNK, CHUNK, heads, -1).transpose(1, 0, 3, 2, 4)


def from_chunks(t):
    nc, b, h, c, d = t.shape
    return t.transpose(1, 0, 3, 2, 4).reshape(b, nc * c, h * d)


def gate_chunks(t):
    b, s, h = t.shape
    return t.reshape(b, s // CHUNK, CHUNK, h).transpose(1, 0, 3, 2)


def gla_mixer(xn, w_in, w_gate_up, b_gate, g_head, w_out):
    dt = xn.dtype
    b, s, _ = xn.shape
    p = xn.astype(jnp.float32) @ w_in
    q, k, v, r, glow = jnp.split(
        p, [GLA_HQK, 2 * GLA_HQK, 2 * GLA_HQK + GLA_HV, 2 * GLA_HQK + 2 * GLA_HV], axis=-1)
    log_a = jax.nn.log_sigmoid(glow @ w_gate_up + b_gate) / GLA_GATE_TAU
    q = q * (GLA_DK ** -0.5)
    qc = to_chunks(q, GLA_HEADS)
    kc = to_chunks(k, GLA_HEADS)
    vc = to_chunks(v, GLA_HEADS)
    cum = jnp.cumsum(to_chunks(log_a, GLA_HEADS), axis=3)
    tot = cum[:, :, :, -1]
    k_dec = kc * jnp.exp(tot[:, :, :, None, :] - cum)

    def step(state, xs):
        q_c, k_c, v_c, tot_c = xs
        state = jnp.exp(tot_c)[..., None] * state + jnp.einsum('bhck,bhcv->bhkv', k_c, v_c)
        o = jnp.einsum('bhck,bhkv->bhcv', q_c, state)
        return state, o

    s0 = jnp.zeros((b, GLA_HEADS, GLA_DK, GLA_DV), jnp.float32)
    _, o = lax.scan(step, s0, (qc, k_dec, vc, tot))
    o = from_chunks(o).reshape(b, s, GLA_HEADS, GLA_DV)
    o = o * lax.rsqrt(jnp.mean(o * o, axis=-1, keepdims=True) + EPS)
    o = o.reshape(b, s, GLA_HV) * g_head * jax.nn.silu(r)
    return (o @ w_out).astype(dt)


def blockdiag(x, w):
    nb, bs, _ = w.shape
    xb = x.reshape(x.shape[:-1] + (nb, bs))
    return jnp.einsum('...ni,nio->...no', xb, w).reshape(x.shape[:-1] + (nb * bs,))


def mlstm_mixer(xn, w_in, b_i, b_f, conv_w, conv_b, w_q, w_k, w_v, skip, g_head, w_out):
    dt = xn.dtype
    b, s, _ = xn.shape
    p = xn.astype(jnp.float32) @ w_in
    x_m, z, i_pre, f_pre = jnp.split(
        p, [MLSTM_INNER, 2 * MLSTM_INNER, 2 * MLSTM_INNER + MLSTM_HEADS], axis=-1)
    x_conv = lax.conv_general_dilated(
        x_m, conv_w.astype(jnp.float32)[:, None, :], window_strides=(1,),
        padding=[(CONV_W - 1, 0)], dimension_numbers=('NWC', 'WIO', 'NWC'),
        feature_group_count=MLSTM_INNER)
    x_c = jax.nn.silu(x_conv + conv_b)
    q = blockdiag(x_c, w_q)
    k = blockdiag(x_c, w_k) * (MLSTM_DH ** -0.5)
    v = blockdiag(x_m, w_v)
    i_log = gate_chunks(i_pre + b_i)
    log_f = gate_chunks(jax.nn.log_sigmoid(f_pre + b_f))
    cumf = jnp.cumsum(log_f, axis=-1)
    tot = cumf[..., -1]
    w_log = tot[..., None] - cumf + i_log
    qc = to_chunks(q, MLSTM_HEADS)
    kc = to_chunks(k, MLSTM_HEADS)
    vc = to_chunks(v, MLSTM_HEADS)

    def step(carry, xs):
        c_st, n_st, m_st = carry
        q_c, k_c, v_c, w_c, tot_c = xs
        m_new = jnp.maximum(tot_c + m_st, jnp.max(w_c, axis=-1))
        a_prev = jnp.exp(tot_c + m_st - m_new)
        kw = k_c * jnp.exp(w_c - m_new[..., None])[..., None]
        c_new = a_prev[..., None, None] * c_st + jnp.einsum('bhck,bhcv->bhkv', kw, v_c)
        n_new = a_prev[..., None] * n_st + jnp.sum(kw, axis=2)
        num = jnp.einsum('bhck,bhkv->bhcv', q_c, c_new)
        den = jnp.maximum(jnp.abs(jnp.einsum('bhck,bhk->bhc', q_c, n_new)),
                          jnp.exp(-m_new)[..., None])
        return (c_new, n_new, m_new), num / den[..., None]

    c0 = jnp.zeros((b, MLSTM_HEADS, MLSTM_DH, MLSTM_DH), jnp.float32)
    n0 = jnp.zeros((b, MLSTM_HEADS, MLSTM_DH), jnp.float32)
    m0 = jnp.zeros((b, MLSTM_HEADS), jnp.float32)
    _, h = lax.scan(step, (c0, n0, m0), (qc, kc, vc, w_log, tot))
    h = from_chunks(h).reshape(b, s, MLSTM_HEADS, MLSTM_DH)
    mu = jnp.mean(h, axis=-1, keepdims=True)
    var = jnp.mean(jnp.square(h - mu), axis=-1, keepdims=True)
    h = ((h - mu) * lax.rsqrt(var + EPS)).reshape(b, s, MLSTM_INNER) * g_head
    h = (h + skip * x_c) * jax.nn.silu(z)
    return (h @ w_out).astype(dt)


def peer_ffn(xn, w_query, sub_keys, u_tab, v_tab):
    dt = xn.dtype
    b, s, d = xn.shape
    t = b * s
    xt = xn.reshape(t, d)
    q = (xt @ w_query).astype(jnp.float32).reshape(t, PEER_HEADS, 2, PEER_DQ // 2)
    sc = jnp.einsum('thpd,hpnd->thpn', q, sub_keys.astype(jnp.float32))
    s_top, i_top = lax.top_k(sc, PEER_TOPK)
    cand = (s_top[:, :, 0, :, None] + s_top[:, :, 1, None, :]).reshape(t, PEER_HEADS, -1)
    cand_idx = (i_top[:, :, 0, :, None] * N_KEYS + i_top[:, :, 1, None, :]).reshape(t, PEER_HEADS, -1)
    best, pos = lax.top_k(cand, PEER_TOPK)
    idx = jnp.take_along_axis(cand_idx, pos, axis=-1)
    gates = jax.nn.softmax(best, axis=-1)

    def block(args):
        x_b, idx_b, g_b = args
        u = jnp.take(u_tab, idx_b, axis=0)
        act = jax.nn.gelu(jnp.einsum('td,thkd->thk', x_b, u).astype(jnp.float32), approximate=False)
        vsel = jnp.take(v_tab, idx_b, axis=0)
        return jnp.einsum('thk,thkd->td', g_b * act, vsel).astype(dt)

    nblk = t // PEER_BLOCK
    y = lax.map(block, (xt.reshape(nblk, PEER_BLOCK, d),
                        idx.reshape(nblk, PEER_BLOCK, PEER_HEADS, PEER_TOPK),
                        gates.reshape(nblk, PEER_BLOCK, PEER_HEADS, PEER_TOPK)))
    return y.reshape(b, s, d)


def setup_inputs(seed: int = 0) -> dict:
    key = jax.random.key(seed)
    ks = jax.random.split(key, 26)

    def nrm(k, shape, scale):
        return jax.random.normal(k, shape, jnp.float32) * scale

    LG, LM = N_GLA_LAYERS, N_MLSTM_LAYERS
    return {
        "x": nrm(ks[0], (BATCH, SEQ, D_MODEL), 1.0),
        "norm_mix_g": 1.0 + nrm(ks[1], (DEPTH, D_MODEL), 0.02),
        "gla_w_in": nrm(ks[2], (LG, D_MODEL, GLA_IN), D_MODEL ** -0.5),
        "gla_w_gate_up": nrm(ks[3], (LG, GLA_GATE_RANK, GLA_HQK), GLA_GATE_RANK ** -0.5),
        "gla_b_gate": nrm(ks[4], (LG, GLA_HQK), 0.1),
        "gla_g_head": 1.0 + nrm(ks[5], (LG, GLA_HV), 0.02),
        "gla_w_out": nrm(ks[6], (LG, GLA_HV, D_MODEL), GLA_HV ** -0.5),
        "mlstm_w_in": nrm(ks[7], (LM, D_MODEL, MLSTM_IN), D_MODEL ** -0.5),
        "mlstm_b_i": nrm(ks[8], (LM, MLSTM_HEADS), 0.1),
        "mlstm_b_f": jnp.broadcast_to(jnp.linspace(3.0, 6.0, MLSTM_HEADS, dtype=jnp.float32), (LM, MLSTM_HEADS))
                     + nrm(ks[9], (LM, MLSTM_HEADS), 0.01),
        "mlstm_conv_w": nrm(ks[10], (LM, CONV_W, MLSTM_INNER), CONV_W ** -0.5),
        "mlstm_conv_b": nrm(ks[11], (LM, MLSTM_INNER), 0.02),
        "mlstm_w_q": nrm(ks[12], (LM, MLSTM_NB, QKV_BLOCK, QKV_BLOCK), QKV_BLOCK ** -0.5),
        "mlstm_w_k": nrm(ks[13], (LM, MLSTM_NB, QKV_BLOCK, QKV_BLOCK), QKV_BLOCK ** -0.5),
        "mlstm_w_v": nrm(ks[14], (LM, MLSTM_NB, QKV_BLOCK, QKV_BLOCK), QKV_BLOCK ** -0.5),
        "mlstm_skip": 1.0 + nrm(ks[15], (LM, MLSTM_INNER), 0.02),
        "mlstm_g_head": 1.0 + nrm(ks[16], (LM, MLSTM_INNER), 0.02),
        "mlstm_w_out": nrm(ks[17], (LM, MLSTM_INNER, D_MODEL), MLSTM_INNER ** -0.5),
        "norm_ffn_g": 1.0 + nrm(ks[18], (DEPTH, D_MODEL), 0.02),
        "peer_w_query": nrm(ks[19], (DEPTH, D_MODEL, PEER_HEADS * PEER_DQ), D_MODEL ** -0.5),
        "peer_sub_keys": nrm(ks[20], (DEPTH, PEER_HEADS, 2, N_KEYS, PEER_DQ // 2), (PEER_DQ // 2) ** -0.5),
        "peer_u": nrm(ks[21], (DEPTH, N_EXPERTS, D_MODEL), D_MODEL ** -0.5),
        "peer_v": nrm(ks[22], (DEPTH, N_EXPERTS, D_MODEL), 0.1),
        "norm_final_g": 1.0 + nrm(ks[23], (D_MODEL,), 0.02),
    }


def reference(x, norm_mix_g, gla_w_in, gla_w_gate_up, gla_b_gate, gla_g_head, gla_w_out,
              mlstm_w_in, mlstm_b_i, mlstm_b_f, mlstm_conv_w, mlstm_conv_b, mlstm_w_q, mlstm_w_k,
              mlstm_w_v, mlstm_skip, mlstm_g_head, mlstm_w_out, norm_ffn_g, peer_w_query,
              peer_sub_keys, peer_u, peer_v, norm_final_g):
    for i in range(DEPTH):
        xn = rmsnorm(x, norm_mix_g[i])
        j = i // N_MIXERS
        if i % N_MIXERS == 0:
            mix = gla_mixer(xn, gla_w_in[j], gla_w_gate_up[j], gla_b_gate[j], gla_g_head[j], gla_w_out[j])
        else:
            mix = mlstm_mixer(xn, mlstm_w_in[j], mlstm_b_i[j], mlstm_b_f[j], mlstm_conv_w[j],
                              mlstm_conv_b[j], mlstm_w_q[j], mlstm_w_k[j], mlstm_w_v[j],
                              mlstm_skip[j], mlstm_g_head[j], mlstm_w_out[j])
        x = x + mix
        x = x + peer_ffn(rmsnorm(x, norm_ffn_g[i]), peer_w_query[i], peer_sub_keys[i],
                         peer_u[i], peer_v[i])
    return rmsnorm(x, norm_final_g)
```

```python
from contextlib import ExitStack
import numpy as np
import concourse.bass as bass
import concourse.mybir as mybir
from concourse.bass_utils import run_bass_kernel_spmd

F32 = mybir.dt.float32
U32 = mybir.dt.uint32
I32 = mybir.dt.int32
ALU = mybir.AluOpType
AF = mybir.ActivationFunctionType
AX = mybir.AxisListType

D = 1024
EPS = 1e-6
ENGS = ("pe", "dve", "act", "pool", "sp")
NEG = -1.0e30


class Prog:
    def __init__(self, nc, stack, n_dma_sems=48):
        self.nc = nc
        self.q = {e: [] for e in ENGS}
        self.sem = {e: stack.enter_context(nc.semaphore("s_" + e)) for e in ENGS}
        self.cnt = {e: 0 for e in ENGS}
        self.dsem = [stack.enter_context(nc.semaphore("d%d" % i)) for i in range(n_dma_sems)]
        self.dcnt = [0] * n_dma_sems
        self.dpool = {"sp": list(range(0, 16)), "act": list(range(0, 16)), "pool": list(range(16, n_dma_sems))}
        self.dnext = {"sp": 0, "act": 0, "pool": 0}
        self.w = {}
        self.r = {}
        self.waited = {e: {} for e in ENGS}
        self.semobj = {}
        for e in ENGS:
            self.semobj[("e", e)] = self.sem[e]
        for i, s in enumerate(self.dsem):
            self.semobj[("d", i)] = s

    def _wait(self, eng, ev):
        sid, val = ev
        if sid == ("e", eng) and eng == "pe":
            return
        if self.waited[eng].get(sid, 0) >= val:
            return
        self.waited[eng][sid] = val
        s = self.semobj[sid]
        self.q[eng].append(lambda e, s=s, val=val: e.wait_ge(s, val))

    def _deps(self, eng, reads, writes):
        for k in reads:
            ev = self.w.get(k)
            if ev is not None:
                self._wait(eng, ev)
        for k in writes:
            ev = self.w.get(k)
            if ev is not None:
                self._wait(eng, ev)
            for ev in self.r.get(k, ()):
                self._wait(eng, ev)

    def _record(self, ev, reads, writes):
        for k in reads:
            if k not in writes:
                self.r.setdefault(k, []).append(ev)
        for k in writes:
            self.w[k] = ev
            self.r[k] = []

    def op(self, eng, fn, reads=(), writes=()):
        self._deps(eng, reads, writes)
        self.cnt[eng] += 1
        s = self.sem[eng]
        self.q[eng].append(lambda e, fn=fn, s=s: fn(e).then_inc(s, 1))
        ev = (("e", eng), self.cnt[eng])
        self._record(ev, reads, writes)
        return ev

    def dma(self, eng, fn, reads=(), writes=()):
        pool_ = self.dpool[eng]
        key_ = "pool" if eng == "pool" else "sp"
        i = pool_[self.dnext[key_] % len(pool_)]
        self.dnext[key_] += 1
        if self.dcnt[i] > 0:
            self._wait(eng, (("d", i), 16 * self.dcnt[i]))
        self._deps(eng, reads, writes)
        self.dcnt[i] += 1
        s = self.dsem[i]
        self.q[eng].append(lambda e, fn=fn, s=s: fn(e).then_inc(s, 16))
        ev = (("d", i), 16 * self.dcnt[i])
        self._record(ev, reads, writes)
        return ev

    def wait_all(self, eng, keys):
        for k in keys:
            ev = self.w.get(k)
            if ev is not None:
                self._wait(eng, ev)

    def barrier(self):
        for eng in ENGS:
            for f_ in ENGS:
                if f_ != eng and self.cnt[f_] > 0:
                    self._wait(eng, (("e", f_), self.cnt[f_]))
            for i, c in enumerate(self.dcnt):
                if c > 0:
                    self._wait(eng, (("d", i), 16 * c))

    def emit(self, block):
        q = self.q
        self.q = {e: [] for e in ENGS}

        @block.tensor
        def _(e):
            for f in q["pe"]:
                f(e)

        @block.vector
        def _(e):
            for f in q["dve"]:
                f(e)

        @block.scalar
        def _(e):
            for f in q["act"]:
                f(e)

        @block.gpsimd
        def _(e):
            for f in q["pool"]:
                f(e)

        @block.sync
        def _(e):
            for f in q["sp"]:
                f(e)


class Ctx:
    def __init__(self, nc, st):
        self.nc = nc
        self.st = st
        self.P = Prog(nc, st)
        self.banks = [st.enter_context(nc.psum_tensor("pb%d" % i, [128, 512], F32)) for i in range(8)]
        self.bi = 0
        self.evi = 0

    def sb(self, name, shape, dt=F32):
        return self.st.enter_context(self.nc.sbuf_tensor(name, shape, dt))

    def bank(self):
        b = self.bi
        self.bi = (self.bi + 1) % 8
        return self.banks[b], ("pb", b)

    def load(self, dst_ap, src_ap, wkeys, eng="sp", reads=()):
        self.P.dma(eng, lambda e: e.dma_start(out=dst_ap, in_=src_ap), reads=reads, writes=wkeys)

    def rmsnorm(self, xt, gb, xn, sq, ms, rstd, kx, kxn, tag):
        P = self.P
        P.op("act", lambda e: e.activation(out=sq[:], in_=xt[:], func=AF.Square, accum_out=ms[:]),
             reads=[kx], writes=["sq" + tag, "ms" + tag])
        P.op("act", lambda e: e.activation(out=rstd[:], in_=ms[:], func=AF.Sqrt, scale=1.0 / D, bias=EPS),
             reads=["ms" + tag], writes=["rstd" + tag])
        P.op("dve", lambda e: e.reciprocal(out=rstd[:], in_=rstd[:]), reads=["rstd" + tag], writes=["rstd" + tag])
        P.op("dve", lambda e: e.scalar_tensor_tensor(out=xn[:], in0=xt[:], scalar=rstd[:, 0:1], in1=gb[:],
                                                     op0=ALU.mult, op1=ALU.mult),
             reads=[kx, "rstd" + tag, "const"], writes=[kxn])

    def transpose_cols(self, src, ksrc, dst, kdst, nchunk, ident, np_in=128):
        P = self.P
        for g0 in range(0, nchunk, 4):
            n = min(4, nchunk - g0)
            pb, kb = self.bank()
            for j in range(n):
                c = g0 + j
                P.op("pe", lambda e, c=c, j=j, pb=pb: e.transpose(out=pb[:, j * 128:j * 128 + np_in],
                                                                   in_=src[0:np_in, c * 128:(c + 1) * 128],
                                                                   identity=ident[0:np_in, 0:np_in]),
                     reads=[ksrc, "const"], writes=[kb])
            self.evi += 1
            if np_in == 128:
                o = dst[:, g0:g0 + n, :]
                i_ = pb[:, 0:n * 128].rearrange("p (a b) -> p a b", b=128)
            else:
                o = dst[:, g0:g0 + n, 0:np_in]
                i_ = pb[:, 0:n * 128].rearrange("p (a b) -> p a b", b=128)[:, :, 0:np_in]
            if self.evi % 2 == 0:
                P.op("act", lambda e, o=o, i_=i_: e.copy(out=o, in_=i_), reads=[kb], writes=[kdst])
            else:
                P.op("dve", lambda e, o=o, i_=i_: e.tensor_copy(out=o, in_=i_), reads=[kb], writes=[kdst])


def _consts():
    c = np.zeros((128, 768), np.float32)
    c[:, 0:128] = np.eye(128, dtype=np.float32)
    s = np.arange(128)
    same = (s[:, None] // 64) == (s[None, :] // 64)
    c[:, 128:256] = (same & (s[:, None] > s[None, :])).astype(np.float32)
    c[:, 256] = (s < 64)
    c[:, 257] = (s >= 64)
    c[:, 258] = 1.0
    c[:, 272:288] = np.arange(16, dtype=np.float32)[None, :]
    c[0:4, 384:512] = 1.0
    c[:, 512:640] = (same & (s[:, None] <= s[None, :])).astype(np.float32)
    return c


def emit_peer(K, NT, io, pfx, xkey, ykey, block, final, NBUF=5):
    nc = K.nc
    x_d = io["x"]
    g_d = io["g"]
    gf_d = io["gf"]
    wq_d = io["wq"]
    sk_d = io["skT"]
    u_d = io["u"]
    v_d = io["v"]
    c_d = io["consts"]
    y_d = io["y"]
    with ExitStack() as st:
        K.st = st
        P = K.P
        sb = lambda n_, s_, dt_=F32: K.sb(pfx + n_, s_, dt_)
        wq = sb("wq_s", [128, 8, 2048])
        skT = sb("skT_s", [128, 16, 128])
        cst = sb("cst", [128, 768])
        gb = sb("gb", [128, D])
        gfb = sb("gfb", [128, D if final else 1])
        xts = [sb("xt%d" % i, [128, D]) for i in range(3)]
        xns = [sb("xn%d" % i, [128, D]) for i in range(2)]
        idxs = [sb("idx%d" % i, [128, 128], I32) for i in range(3)]
        wgts = [sb("wgt%d" % i, [128, 128]) for i in range(2)]
        ms = sb("ms", [128, 1])
        rstd = sb("rstd", [128, 1])
        xnT = sb("xnT", [128, 8, 128])
        qT = sb("qT", [128, 16, 128])
        sc = sb("sc", [128, 16, 128])
        sc2 = sb("sc2", [128, 128])
        stop = sb("stop", [128, 16, 16])
        itop = sb("itop", [128, 16, 16], U32)
        itopf = sb("itopf", [128, 16, 16])
        cand = sb("cand", [128, 8, 256])
        cand2 = sb("cand2", [128, 256])
        best = sb("best", [128, 8, 16])
        pos = sb("pos", [128, 8, 16], U32)
        pa = sb("pa", [128, 8, 16], U32)
        pbb = sb("pbb", [128, 8, 16], U32)
        paf = sb("paf", [128, 8, 16])
        pbf = sb("pbf", [128, 8, 16])
        isel = sb("isel", [128, 8, 16])
        jsel = sb("jsel", [128, 8, 16])
        idxf = sb("idxf", [128, 128])
        gats = [sb("gat%d" % i, [128, 8, 16]) for i in range(2)]
        gsum = sb("gsum", [128, 8])
        apre = sb("apre", [128, 128])
        ub = [sb("ub%d" % i, [128, D]) for i in range(NBUF)]
        vb = [sb("vb%d" % i, [128, D]) for i in range(NBUF)]
        NDG = 8
        dg = [sb("dg%d" % i, [128, 128]) for i in range(NDG)]
        junks = [sb("jnk%d" % i, [128, D]) for i in range(2)]
        junk2 = sb("junk2", [128, D])
        yo = sb("yo", [128, D])
        ms2 = sb("ms2", [128, 1])
        rstd2 = sb("rstd2", [128, 1])

        ident = cst[:, 0:128]
        iota16 = cst[:, 272:288]
        rs_d = io.get("rowsel")
        if rs_d is not None:
            rsel = sb("rsel", [128, NT], I32)
            K.load(rsel[:], rs_d, ["const"])

        K.load(cst[:], c_d, ["const"])
        K.load(gb[:], g_d[0:1, :].broadcast_to([128, D]), ["const"])
        if final:
            K.load(gfb[:], gf_d[0:1, :].broadcast_to([128, D]), ["const"])
        K.load(skT[:].rearrange("p a b -> p (a b)"), sk_d, ["const"])
        for c in range(8):
            K.load(wq[:, c, :], wq_d[c * 128:(c + 1) * 128, :], ["const"])

        abank = [0]

        def bankA():
            b = abank[0]
            abank[0] = (abank[0] + 1) % 6
            return K.banks[b], ("pb", b)

        def top16(src_ap, ksrc, scratch_ap, kscr, vals_ap, idx_ap, kout):
            P.op("dve", lambda e: e.max(out=vals_ap[:, 0:8], in_=src_ap), reads=[ksrc], writes=[kout])
            P.op("dve", lambda e: e.max_index(out=idx_ap[:, 0:8], in_max=vals_ap[:, 0:8], in_values=src_ap),
                 reads=[ksrc, kout], writes=[kout + "i"])
            P.op("dve", lambda e: e.match_replace(out=scratch_ap, in_to_replace=vals_ap[:, 0:8], in_values=src_ap,
                                                  imm_value=NEG), reads=[ksrc, kout], writes=[kscr])
            P.op("dve", lambda e: e.max(out=vals_ap[:, 8:16], in_=scratch_ap), reads=[kscr], writes=[kout])
            P.op("dve", lambda e: e.max_index(out=idx_ap[:, 8:16], in_max=vals_ap[:, 8:16], in_values=scratch_ap),
                 reads=[kscr, kout], writes=[kout + "i"])

        def stageA(it):
            xt = xts[it % 3]
            kxt = "xt%d" % (it % 3)
            xn = xns[it % 2]
            kxn = "xn%d" % (it % 2)
            idx = idxs[it % 3]
            kidx = "idx%d" % (it % 3)
            gat = gats[it % 2]
            kgat = "gat%d" % (it % 2)
            rows = slice(it * 128, (it + 1) * 128)
            if rs_d is None:
                K.load(xt[:], x_d[rows, :], [kxt], reads=[(xkey, it)])
            else:
                P.dma("pool", lambda e, it=it: e.indirect_dma_start(
                    out=xt[:, :], out_offset=None, in_=x_d[:, :],
                    in_offset=bass.IndirectOffsetOnAxis(ap=rsel[:, it:it + 1], axis=0)),
                    reads=["const"] + [(xkey, j_) for j_ in range(x_d.shape[0] // 128)] if it == 0 else ["const"], writes=[kxt])
            yield
            K.rmsnorm(xt, gb, xn, junk2, ms, rstd, kxt, kxn, "A")
            yield
            for g0 in range(0, 8, 4):
                pb, kb = bankA()
                for j in range(4):
                    c = g0 + j
                    P.op("pe", lambda e, c=c, j=j, pb=pb: e.transpose(out=pb[:, j * 128:(j + 1) * 128], in_=xn[:, c * 128:(c + 1) * 128],
                                                                       identity=ident), reads=[kxn, "const"], writes=[kb])
                P.op("act", lambda e, g0=g0, pb=pb: e.copy(out=xnT[:, g0:g0 + 4, :], in_=pb[:].rearrange("p (a b) -> p a b", b=128)),
                     reads=[kb], writes=["xnT"])
                yield
            for g0 in range(0, 16, 4):
                pb, kb = bankA()
                for j in range(4):
                    g = g0 + j
                    for c in range(8):
                        P.op("pe", lambda e, g=g, j=j, c=c, pb=pb: e.matmul(
                            pb[:, j * 128:(j + 1) * 128], lhsT=wq[:, c, g * 128:(g + 1) * 128], rhs=xnT[:, c, :],
                            start=(c == 0), stop=(c == 7)), reads=["const", "xnT"], writes=[kb])
                    yield
                P.op("act", lambda e, g0=g0, pb=pb: e.copy(out=qT[:, g0:g0 + 4, :].rearrange("p a b -> p (a b)"), in_=pb[:]),
                     reads=[kb], writes=["qT"])
            for g0 in range(0, 16, 4):
                pb, kb = bankA()
                for j in range(4):
                    g = g0 + j
                    P.op("pe", lambda e, g=g, j=j, pb=pb: e.matmul(
                        pb[:, j * 128:(j + 1) * 128], lhsT=qT[:, g, :], rhs=skT[:, g, :], start=True, stop=True),
                        reads=["qT", "const"], writes=[kb])
                P.op("act", lambda e, g0=g0, pb=pb: e.copy(out=sc[:, g0:g0 + 4, :].rearrange("p a b -> p (a b)"), in_=pb[:]),
                     reads=[kb], writes=["sc"])
                yield
            for g in range(16):
                top16(sc[:, g, :], "sc", sc2[:], "sc2", stop[:, g, :], itop[:, g, :], "stop")
                yield
            P.op("dve", lambda e: e.tensor_copy(out=itopf[:], in_=itop[:]), reads=["stopi"], writes=["itopf"])
            s4 = stop[:].rearrange("p (h two) k -> p h two k", two=2)
            i4 = itopf[:].rearrange("p (h two) k -> p h two k", two=2)
            P.op("dve", lambda e: e.tensor_tensor(
                out=cand[:].rearrange("p h (a b) -> p h a b", b=16),
                in0=s4[:, :, 0, :].unsqueeze(3).broadcast_to([128, 8, 16, 16]),
                in1=s4[:, :, 1, :].unsqueeze(2).broadcast_to([128, 8, 16, 16]), op=ALU.add),
                reads=["stop"], writes=["cand"])
            yield
            for h in range(8):
                top16(cand[:, h, :], "cand", cand2[:], "cand2", best[:, h, :], pos[:, h, :], "best")
                yield
            P.op("dve", lambda e: e.tensor_single_scalar(out=pa[:], in_=pos[:], scalar=4, op=ALU.logical_shift_right),
                 reads=["besti"], writes=["pa"])
            P.op("dve", lambda e: e.tensor_single_scalar(out=pbb[:], in_=pos[:], scalar=15, op=ALU.bitwise_and),
                 reads=["besti"], writes=["pbb"])
            P.op("dve", lambda e: e.tensor_copy(out=paf[:], in_=pa[:]), reads=["pa"], writes=["paf"])
            P.op("dve", lambda e: e.tensor_copy(out=pbf[:], in_=pbb[:]), reads=["pbb"], writes=["pbf"])
            yield
            io4 = iota16.unsqueeze(1).unsqueeze(1).broadcast_to([128, 8, 16, 16])
            eq = cand[:].rearrange("p h (a b) -> p h a b", b=16)
            for (pf, kpf, two, sel, ksel) in ((paf, "paf", 0, isel, "isel"), (pbf, "pbf", 1, jsel, "jsel")):
                P.op("dve", lambda e, pf=pf: e.tensor_tensor(out=eq, in0=pf[:].unsqueeze(3).broadcast_to([128, 8, 16, 16]),
                                                             in1=io4, op=ALU.is_equal), reads=[kpf, "const"], writes=["cand"])
                P.op("dve", lambda e, two=two: e.tensor_tensor(out=eq, in0=eq,
                                                               in1=i4[:, :, two, :].unsqueeze(2).broadcast_to([128, 8, 16, 16]),
                                                               op=ALU.mult), reads=["cand", "itopf"], writes=["cand"])
                P.op("dve", lambda e, sel=sel: e.tensor_reduce(out=sel[:], in_=eq, axis=AX.X, op=ALU.add),
                     reads=["cand"], writes=[ksel])
                yield
            P.op("dve", lambda e: e.scalar_tensor_tensor(out=idxf[:], in0=isel[:].rearrange("p h k -> p (h k)"), scalar=128.0,
                                                         in1=jsel[:].rearrange("p h k -> p (h k)"), op0=ALU.mult, op1=ALU.add),
                 reads=["isel", "jsel"], writes=["idxf"])
            P.op("dve", lambda e: e.tensor_copy(out=idx[:], in_=idxf[:]), reads=["idxf"], writes=[kidx])
            yield
            P.op("dve", lambda e: e.tensor_tensor(out=gat[:], in0=best[:], in1=best[:, :, 0:1].broadcast_to([128, 8, 16]),
                                                  op=ALU.subtract), reads=["best"], writes=[kgat])
            P.op("act", lambda e: e.activation(out=gat[:], in_=gat[:], func=AF.Exp), reads=[kgat], writes=[kgat])
            P.op("dve", lambda e: e.tensor_reduce(out=gsum[:], in_=gat[:], axis=AX.X, op=ALU.add), reads=[kgat], writes=["gsum"])
            P.op("dve", lambda e: e.reciprocal(out=gsum[:], in_=gsum[:]), reads=["gsum"], writes=["gsum"])
            P.op("dve", lambda e: e.tensor_tensor(out=gat[:], in0=gat[:], in1=gsum[:].unsqueeze(2).broadcast_to([128, 8, 16]),
                                                  op=ALU.mult), reads=[kgat, "gsum"], writes=[kgat])
            yield

        def stageB(it):
            xn = xns[it % 2]
            kxn = "xn%d" % (it % 2)
            idx = idxs[it % 3]
            kidx = "idx%d" % (it % 3)
            gat = gats[it % 2]
            kgat = "gat%d" % (it % 2)
            wgt = wgts[it % 2]
            kwgt = "wgt%d" % (it % 2)
            for s in range(128):
                b = ub[s % NBUF]
                kb_ = "ub%d" % (s % NBUF)
                P.dma("pool", lambda e, b=b, s=s: e.indirect_dma_start(
                    out=b[:, :], out_offset=None, in_=u_d[:, :],
                    in_offset=bass.IndirectOffsetOnAxis(ap=idx[:, s:s + 1], axis=0)), reads=[kidx], writes=[kb_])
                P.op("dve", lambda e, b=b, s=s: e.scalar_tensor_tensor(
                    out=junks[s % 2][:], in0=b[:], scalar=1.0, in1=xn[:], op0=ALU.mult, op1=ALU.mult, accum_out=apre[:, s:s + 1]),
                    reads=[kb_, kxn], writes=[("apre", s), "junk%d" % (s % 2)])
                yield
            P.op("act", lambda e: e.activation(out=wgt[:], in_=apre[:], func=AF.Gelu), reads=[("apre", s_) for s_ in range(128)], writes=[kwgt])
            P.op("dve", lambda e: e.tensor_tensor(out=wgt[:], in0=wgt[:], in1=gat[:].rearrange("p h k -> p (h k)"), op=ALU.mult),
                 reads=[kwgt, kgat], writes=[kwgt])
            yield

        def stageC(it):
            xt = xts[it % 3]
            kxt = "xt%d" % (it % 3)
            idx = idxs[it % 3]
            kidx = "idx%d" % (it % 3)
            wgt = wgts[it % 2]
            kwgt = "wgt%d" % (it % 2)
            py0, ky0 = K.banks[6], ("pb", 6)
            py1, ky1 = K.banks[7], ("pb", 7)
            for s in range(128):
                b = vb[s % NBUF]
                kb_ = "vb%d" % (s % NBUF)
                d_ = dg[s % NDG]
                kd = "dg%d" % (s % NDG)
                P.dma("pool", lambda e, b=b, s=s: e.indirect_dma_start(
                    out=b[:, :], out_offset=None, in_=v_d[:, :],
                    in_offset=bass.IndirectOffsetOnAxis(ap=idx[:, s:s + 1], axis=0)), reads=[kidx], writes=[kb_])
                P.op("act", lambda e, d_=d_, s=s: e.activation(out=d_[:], in_=ident, func=AF.Copy, scale=wgt[:, s:s + 1]),
                     reads=[kwgt, "const"], writes=[kd])
                P.op("pe", lambda e, d_=d_, b=b, s=s: e.matmul(py0[:], lhsT=d_[:], rhs=b[:, 0:512], start=(s == 0), stop=(s == 127)),
                     reads=[kd, kb_], writes=[ky0])
                P.op("pe", lambda e, d_=d_, b=b, s=s: e.matmul(py1[:], lhsT=d_[:], rhs=b[:, 512:1024], start=(s == 0), stop=(s == 127)),
                     reads=[kd, kb_], writes=[ky1])
                yield
            P.op("dve", lambda e: e.tensor_tensor(out=yo[:, 0:512], in0=py0[:], in1=xt[:, 0:512], op=ALU.add),
                 reads=[ky0, kxt], writes=["yo"])
            P.op("dve", lambda e: e.tensor_tensor(out=yo[:, 512:1024], in0=py1[:], in1=xt[:, 512:1024], op=ALU.add),
                 reads=[ky1, kxt], writes=["yo"])
            if final:
                K.rmsnorm(yo, gfb, yo, junk2, ms2, rstd2, "yo", "yo", "C")
            rows = slice(it * 128, (it + 1) * 128)
            P.dma("sp", lambda e, rows=rows: e.dma_start(out=y_d[rows, :], in_=yo[:]), reads=["yo"], writes=[(ykey, it)])
            yield

        def drain(gens, weights):
            alive = [g is not None for g in gens]
            while any(alive):
                for gi, g in enumerate(gens):
                    if not alive[gi]:
                        continue
                    for _ in range(weights[gi]):
                        try:
                            next(g)
                        except StopIteration:
                            alive[gi] = False
                            break

        for k in range(-1, NT + 1):
            gens = [stageB(k) if 0 <= k < NT else None,
                    stageC(k - 1) if 0 <= k - 1 < NT else None,
                    stageA(k + 1) if 0 <= k + 1 < NT else None]
            drain(gens, [1, 1, 1])
        P.barrier()
        P.emit(block)


def peer_inputs(x_rows, g, gf, wq, sub_keys, u, v):
    skT = np.ascontiguousarray(sub_keys.reshape(16, 128, 128).transpose(2, 0, 1)).reshape(128, 2048)
    return {"x": np.ascontiguousarray(x_rows), "g": g.reshape(1, D), "gf": gf.reshape(1, D), "wq": wq, "skT": skT,
            "u": u, "v": v, "consts": _consts()}


def emit_gla(K, NT, io, pfx, xkey, ykey, block):
    nc = K.nc
    x_d = io["x"]
    g_d = io["g"]
    win_d = io["win"]
    wgu_d = io["wgu"]
    bg_d = io["bg"]
    gh_d = io["gh"]
    wout_d = io["wout"]
    c_d = io["consts"]
    y_d = io["y"]
    with ExitStack() as st:
        K.st = st
        P = K.P
        sb = lambda n_, s_, dt_=F32: K.sb(pfx + n_, s_, dt_)
        win = sb("win_s", [128, 8, 3088])
        wout = sb("wout_s", [128, 8, D])
        wgu = sb("wgu_s", [16, 512])
        bg = sb("bg_s", [1, 512])
        cst = sb("cst", [128, 768])
        gb = sb("gb", [128, D])
        ghb = sb("ghb", [128, D])
        xt = sb("xt", [128, D])
        sq = sb("sq", [128, D])
        ms = sb("ms", [128, 1])
        rstd = sb("rstd", [128, 1])
        xn = sb("xn", [128, D])
        xnT = sb("xnT", [128, 8, 128])
        glowT = sb("glowT", [16, 128])
        ta = sb("ta", [128, 512])
        ls = sb("ls", [128, 512])
        expD = sb("expD", [128, 512])
        etot = sb("etot", [128, 8])
        kdec = sb("kdec", [128, 512])
        vv = sb("vv", [128, D])
        gsil = sb("gsil", [128, D])
        qTm = sb("qTm", [128, 2, 512])
        S = sb("S", [128, 4, 256])
        ss = sb("ss", [128, 4])
        og = sb("og", [128, D])
        ogT = sb("ogT", [128, 8, 128])
        yo = sb("yo", [128, D])
        ident = cst[:, 0:128]
        ustrict = cst[:, 128:256]
        chunkind = cst[:, 256:258]
        ones_row = cst[0:1, 384:512]

        K.load(cst[:], c_d, ["const"])
        K.load(gb[:], g_d[0:1, :].broadcast_to([128, D]), ["const"])
        K.load(ghb[:], gh_d[0:1, :].broadcast_to([128, D]), ["const"])
        K.load(wgu[:], wgu_d, ["const"])
        K.load(bg[:], bg_d, ["const"])
        for c in range(8):
            K.load(win[:, c, :], win_d[c * 128:(c + 1) * 128, :], ["const"])
            K.load(wout[:, c, :], wout_d[c * 128:(c + 1) * 128, :], ["const"])
        P.op("dve", lambda e: e.memset(S[:], 0.0), writes=["S"])
        P.op("dve", lambda e: e.memset(qTm[:], 0.0), writes=["qT"])

        def proj(col0, ncol, pb, kb, pcol=0):
            for c in range(8):
                P.op("pe", lambda e, c=c: e.matmul(pb[:, pcol:pcol + ncol], lhsT=xnT[:, c, :], rhs=win[:, c, col0:col0 + ncol],
                                                   start=(c == 0), stop=(c == 7)), reads=["xnT", "const"], writes=[kb])

        for it in range(NT):
            rows = slice(it * 128, (it + 1) * 128)
            K.load(xt[:], x_d[rows, :], ["xt"], reads=[(xkey, it)])
            K.rmsnorm(xt, gb, xn, sq, ms, rstd, "xt", "xn", "")
            K.transpose_cols(xn, "xn", xnT, "xnT", 8, ident)
            pb, kb = K.bank()
            for c in range(8):
                P.op("pe", lambda e, c=c, pb=pb: e.matmul(pb[0:16, 0:128], lhsT=win[:, c, 3072:3088], rhs=xnT[:, c, :],
                                                         start=(c == 0), stop=(c == 7)), reads=["xnT", "const"], writes=[kb])
            P.op("act", lambda e, pb=pb: e.copy(out=glowT[:], in_=pb[0:16, 0:128]), reads=[kb], writes=["glowT"])
            pz, kz = K.bank()
            P.op("pe", lambda e: e.matmul(pz[:], lhsT=glowT[:], rhs=wgu[:], start=True, stop=False),
                 reads=["glowT", "const"], writes=[kz])
            P.op("pe", lambda e: e.matmul(pz[:], lhsT=ones_row, rhs=bg[:], start=False, stop=True),
                 reads=["const"], writes=[kz])
            P.op("act", lambda e: e.copy(out=ls[:], in_=pz[:]), reads=[kz], writes=["ls"])
            P.op("dve", lambda e: e.scalar_tensor_tensor(out=ta[:], in0=pz[:], scalar=-1.0, in1=ls[:], op0=ALU.mult, op1=ALU.min),
                 reads=[kz, "ls"], writes=["ta"])
            P.op("act", lambda e: e.activation(out=ta[:], in_=ta[:], func=AF.Exp), reads=["ta"], writes=["ta"])
            P.op("act", lambda e: e.activation(out=ta[:], in_=ta[:], func=AF.Ln, bias=1.0), reads=["ta"], writes=["ta"])
            P.op("dve", lambda e: e.scalar_tensor_tensor(out=ls[:], in0=pz[:], scalar=0.0, in1=ta[:], op0=ALU.min, op1=ALU.subtract),
                 reads=[kz, "ta"], writes=["ls"])
            pd, kd = K.bank()
            P.op("pe", lambda e: e.matmul(pd[:], lhsT=ustrict, rhs=ls[:], start=True, stop=True), reads=["ls", "const"], writes=[kd])
            P.op("act", lambda e: e.activation(out=expD[:], in_=pd[:], func=AF.Exp, scale=1.0 / 16.0), reads=[kd], writes=["expD"])
            pt, kt = K.bank()
            for h in range(4):
                P.op("pe", lambda e, h=h: e.matmul(pt[:, h * 2:(h + 1) * 2], lhsT=ls[:, h * 128:(h + 1) * 128], rhs=chunkind,
                                                   start=True, stop=True), reads=["ls", "const"], writes=[kt])
            P.op("act", lambda e: e.activation(out=etot[:], in_=pt[:, 0:8], func=AF.Exp, scale=1.0 / 16.0), reads=[kt], writes=["etot"])
            pk, kk = K.bank()
            proj(512, 512, pk, kk)
            P.op("dve", lambda e: e.tensor_tensor(out=kdec[:], in0=pk[:], in1=expD[:], op=ALU.mult), reads=[kk, "expD"], writes=["kdec"])
            for hf in range(2):
                pv, kv = K.bank()
                proj(1024 + hf * 512, 512, pv, kv)
                P.op("act", lambda e, hf=hf, pv=pv: e.copy(out=vv[:, hf * 512:(hf + 1) * 512], in_=pv[:]), reads=[kv], writes=["vv"])
            for hf in range(2):
                pr, kr = K.bank()
                proj(2048 + hf * 512, 512, pr, kr)
                P.op("act", lambda e, hf=hf, pr=pr: e.activation(out=gsil[:, hf * 512:(hf + 1) * 512], in_=pr[:], func=AF.Silu),
                     reads=[kr], writes=["gsil"])
            P.op("dve", lambda e: e.tensor_tensor(out=gsil[:], in0=gsil[:], in1=ghb[:], op=ALU.mult), reads=["gsil", "const"], writes=["gsil"])
            pq, kq = K.bank()
            for h in range(4):
                for c in range(8):
                    P.op("pe", lambda e, h=h, c=c: e.matmul(pq[:, h * 128:(h + 1) * 128], lhsT=win[:, c, h * 128:(h + 1) * 128],
                                                            rhs=xnT[:, c, :], start=(c == 0), stop=(c == 7)),
                         reads=["xnT", "const"], writes=[kq])
            for j in range(2):
                P.op("act", lambda e, j=j: e.mul(out=qTm[:, j, :].rearrange("p (h t) -> p h t", t=128)[:, :, j * 64:(j + 1) * 64],
                                                 in_=pq[:].rearrange("p (h t) -> p h t", t=128)[:, :, j * 64:(j + 1) * 64],
                                                 mul=128.0 ** -0.5), reads=[kq], writes=["qT"])
            po = [K.bank() for _ in range(4)]
            for j in range(2):
                tk = slice(j * 64, (j + 1) * 64)
                for hp in range(2):
                    pS, kS = K.bank()
                    for hh in range(2):
                        h = hp * 2 + hh
                        P.op("pe", lambda e, h=h, hh=hh, pS=pS, tk=tk: e.matmul(pS[:, hh * 256:(hh + 1) * 256], lhsT=kdec[tk, h * 128:(h + 1) * 128],
                                                                         rhs=vv[tk, h * 256:(h + 1) * 256], start=True, stop=True),
                             reads=["kdec", "vv"], writes=[kS])
                    for hh in range(2):
                        h = hp * 2 + hh
                        P.op("dve", lambda e, h=h, hh=hh, pS=pS, j=j: e.scalar_tensor_tensor(
                            out=S[:, h, :], in0=S[:, h, :], scalar=etot[:, h * 2 + j:h * 2 + j + 1], in1=pS[:, hh * 256:(hh + 1) * 256],
                            op0=ALU.mult, op1=ALU.add), reads=[kS, "etot", ("S", h)], writes=[("S", h)])
                for h in range(4):
                    pb_, kb_ = po[h]
                    P.op("pe", lambda e, h=h, pb_=pb_, j=j: e.matmul(pb_[:, 0:256], lhsT=qTm[:, j, h * 128:(h + 1) * 128],
                                                               rhs=S[:, h, :], start=(j == 0), stop=(j == 1)), reads=["qT", ("S", h), "S"], writes=[kb_])
            for h in range(4):
                pb_, kb_ = po[h]
                P.op("act", lambda e, h=h, pb_=pb_: e.activation(out=sq[:, 0:256], in_=pb_[:, 0:256], func=AF.Square,
                                                               accum_out=ss[:, h:h + 1]), reads=[kb_], writes=["sq", ("ss", h)])
            P.op("act", lambda e: e.activation(out=ss[:], in_=ss[:], func=AF.Sqrt, scale=1.0 / 256.0, bias=EPS),
                 reads=[("ss", h) for h in range(4)], writes=[("ss", h) for h in range(4)])
            P.op("dve", lambda e: e.reciprocal(out=ss[:], in_=ss[:]), reads=[("ss", h) for h in range(4)], writes=[("ss", h) for h in range(4)])
            for h in range(4):
                pb_, kb_ = po[h]
                P.op("dve", lambda e, h=h, pb_=pb_: e.scalar_tensor_tensor(
                    out=og[:, h * 256:(h + 1) * 256], in0=pb_[:, 0:256], scalar=ss[:, h:h + 1],
                    in1=gsil[:, h * 256:(h + 1) * 256], op0=ALU.mult, op1=ALU.mult), reads=[kb_, ("ss", h), "gsil"], writes=["og"])
            K.transpose_cols(og, "og", ogT, "ogT", 8, ident)
            for hf in range(2):
                pm, km = K.bank()
                for c in range(8):
                    P.op("pe", lambda e, c=c, hf=hf, pm=pm: e.matmul(pm[:], lhsT=ogT[:, c, :], rhs=wout[:, c, hf * 512:(hf + 1) * 512],
                                                                     start=(c == 0), stop=(c == 7)), reads=["ogT", "const"], writes=[km])
                P.op("dve", lambda e, hf=hf, pm=pm: e.tensor_tensor(out=yo[:, hf * 512:(hf + 1) * 512], in0=pm[:], in1=xt[:, hf * 512:(hf + 1) * 512],
                                                                    op=ALU.add), reads=[km, "xt"], writes=["yo"])
            P.dma("sp", lambda e, rows=rows: e.dma_start(out=y_d[rows, :], in_=yo[:]), reads=["yo"], writes=[(ykey, it)])
        P.barrier()
        P.emit(block)


def gla_inputs(x_rows, g, win, wgu, bg, gh, wout):
    return {"x": np.ascontiguousarray(x_rows), "g": g.reshape(1, D), "win": win, "wgu": wgu, "bg": bg.reshape(1, 512),
            "gh": gh.reshape(1, D), "wout": wout, "consts": _consts()}


DH = 512


def emit_mlstm(K, NT, io, pfx, xkey, ykey, block, half=False):
    nc = K.nc
    x_d = io["x"]
    g_d = io["g"]
    winl_d = io["winl"]
    wg_d = io["wg"]
    bif_d = io["bif"]
    cw_d = io["cw"]
    cb_d = io["cb"]
    bd_d = io["bd"]
    sk_d = io["skip"]
    gh_d = io["gh"]
    wout_d = io["wout"]
    c_d = io["consts"]
    y_d = io["y"]
    with ExitStack() as st:
        K.st = st
        P = K.P
        sb = lambda n_, s_, dt_=F32: K.sb(pfx + n_, s_, dt_)
        cst = sb("cst", [128, 768])
        gb = sb("gb", [128, D])
        wg = sb("wg_s", [128, 8, 8])
        bif = sb("bif_s", [128, 8])
        cw = sb("cw_s", [128, 4, 16])
        cb = sb("cb_s", [128, 16])
        skp = sb("skp_s", [128, 16])
        ghd = sb("ghd_s", [128, 16])
        xt = sb("xt", [128, D])
        sq = sb("sq", [128, D])
        ms = sb("ms", [128, 1])
        rstd = sb("rstd", [128, 1])
        xn = sb("xn", [128, D])
        xnT = sb("xnT", [128, 8, 128])
        wch = [sb("wch%d" % i, [128, 8, 128]) for i in range(4)]
        bdc = [sb("bdc%d" % i, [128, 128]) for i in range(6)]
        woc = [sb("woc%d" % i, [128, D]) for i in range(3)]
        xmT = sb("xmT", [128, 16, 131])
        xcT = sb("xcT", [128, 16, 128])
        tmpc = sb("tmpc", [128, 16, 128])
        szT = sb("szT", [128, 16, 128])
        qTm = sb("qTm", [128, 2, 16, 128])
        kw = sb("kw", [128, 2048])
        vv = sb("vv", [128, 2048])
        hh = sb("hh", [128, 2048])
        fin = sb("fin", [128, 16, 128])
        C = sb("C", [128, 16, 512])
        nst = sb("nst", [128, 16])
        gtk = sb("gtk", [128, 8])
        gta = sb("gta", [128, 4])
        lf = sb("lf", [128, 4])
        sm4 = sb("sm4", [4, 128])
        cumT = sb("cumT", [4, 128])
        dT = sb("dT", [4, 128])
        wlT = sb("wlT", [4, 128])
        wexT = sb("wexT", [4, 128])
        mst = sb("mst", [4, 1])
        t1 = sb("t1", [4, 1])
        mnew = sb("mnew", [4, 1])
        nm = sb("nm", [4, 1])
        Mx = sb("Mx", [4, 1])
        smq = sb("smq", [4, 4])
        msk = sb("msk", [4, 4, 4])
        bc = sb("bc", [128, 16])
        wtok = sb("wtok", [128, 4])
        qn = sb("qn", [128, 4])
        en = sb("en", [128, 4])
        den = sb("den", [128, 4])
        stats = sb("stats", [128, 4, 6])
        mv = sb("mv", [128, 4, 2])
        rs4 = sb("rs4", [128, 4])
        yo = sb("yo", [128, D])
        ident = cst[:, 0:128]
        ones_col = cst[:, 258:259]
        ones4 = cst[0:4, 384:512]
        tincl = cst[:, 512:640]
        I4 = cst[0:4, 0:4]

        K.load(cst[:], c_d, ["const"])
        K.load(gb[:], g_d[0:1, :].broadcast_to([128, D]), ["const"])
        K.load(wg[:].rearrange("p a b -> p (a b)"), wg_d, ["const"])
        K.load(bif[:], bif_d[0:1, :].broadcast_to([128, 8]), ["const"])
        K.load(cw[:].rearrange("p a b -> p (a b)"), cw_d, ["const"])
        K.load(cb[:], cb_d, ["const"])
        K.load(skp[:], sk_d, ["const"])
        K.load(ghd[:], gh_d, ["const"])
        P.op("dve", lambda e: e.memset(C[:], 0.0), writes=["C"])
        P.op("dve", lambda e: e.memset(nst[:], 0.0), writes=["nst"])
        P.op("dve", lambda e: e.memset(mst[:], 0.0), writes=["mst"])
        P.op("dve", lambda e: e.memset(qTm[:], 0.0), writes=["qTm"])
        P.op("dve", lambda e: e.memset(xmT[:], 0.0), writes=["xmT"])
        nw = [0]
        nb = [0]
        nwo = [0]

        fl_d = io.get("flag")
        rs_d = io.get("rowsel")
        if half:
            rsel = sb("rsel", [128, NT // 2], I32)
            flg = sb("flg", [128, 1])
            K.load(rsel[:], rs_d, ["const"])
            K.load(flg[:], fl_d, ["const"])
            plan = [(it, False, None, None) for it in range(NT // 2)] + [None] + [(None, True, j, j) for j in range(NT // 2)]
        else:
            plan = [(it, True, it, None) for it in range(NT)]
        for step in plan:
            if step is None:
                P.op("dve", lambda e: e.tensor_scalar(out=C[:], in0=C[:], scalar1=flg[:, 0:1], scalar2=None, op0=ALU.mult),
                     reads=["const", "C"] + [("C", i_) for i_ in range(16)], writes=["C"] + [("C", i_) for i_ in range(16)])
                P.op("dve", lambda e: e.tensor_scalar(out=nst[:], in0=nst[:], scalar1=flg[:, 0:1], scalar2=None, op0=ALU.mult),
                     reads=["const", "nst"], writes=["nst"])
                P.op("dve", lambda e: e.tensor_scalar(out=mst[:], in0=mst[:], scalar1=flg[0:4, 0:1], scalar2=None, op0=ALU.mult),
                     reads=["const", "mst"], writes=["mst"])
                P.op("dve", lambda e: e.tensor_scalar(out=xmT[:, :, 0:3], in0=xmT[:, :, 0:3], scalar1=flg[:, 0:1], scalar2=None, op0=ALU.mult),
                     reads=["const", "xmT"], writes=["xmT"])
                continue
            it, full, slot, rcol = step
            if rcol is None:
                K.load(xt[:], x_d[it * 128:(it + 1) * 128, :], ["xt"], reads=[(xkey, it)])
            else:
                P.dma("pool", lambda e, rcol=rcol: e.indirect_dma_start(
                    out=xt[:, :], out_offset=None, in_=x_d[:, :],
                    in_offset=bass.IndirectOffsetOnAxis(ap=rsel[:, rcol:rcol + 1], axis=0)),
                    reads=["const"] + [(xkey, j_) for j_ in range(NT)], writes=["xt"])
            if full:
                rows = slice(slot * 128, (slot + 1) * 128)
            K.rmsnorm(xt, gb, xn, sq, ms, rstd, "xt", "xn", "")
            K.transpose_cols(xn, "xn", xnT, "xnT", 8, ident)
            pg, kg = K.bank()
            for c in range(8):
                P.op("pe", lambda e, c=c, pg=pg: e.matmul(pg[:, 0:8], lhsT=xnT[:, c, :], rhs=wg[:, c, :], start=(c == 0), stop=(c == 7)),
                     reads=["xnT", "const"], writes=[kg])
            P.op("dve", lambda e, pg=pg: e.tensor_tensor(out=gtk[:], in0=pg[:, 0:8], in1=bif[:], op=ALU.add), reads=[kg, "const"], writes=["gtk"])
            P.op("dve", lambda e: e.scalar_tensor_tensor(out=gta[:], in0=gtk[:, 4:8], scalar=-1.0, in1=gtk[:, 4:8], op0=ALU.mult, op1=ALU.min),
                 reads=["gtk"], writes=["gta"])
            P.op("act", lambda e: e.activation(out=gta[:], in_=gta[:], func=AF.Exp), reads=["gta"], writes=["gta"])
            P.op("act", lambda e: e.activation(out=gta[:], in_=gta[:], func=AF.Ln, bias=1.0), reads=["gta"], writes=["gta"])
            P.op("dve", lambda e: e.scalar_tensor_tensor(out=lf[:], in0=gtk[:, 4:8], scalar=0.0, in1=gta[:], op0=ALU.min, op1=ALU.subtract),
                 reads=["gtk", "gta"], writes=["lf"])
            pc, kc_ = K.bank()
            P.op("pe", lambda e, pc=pc: e.matmul(pc[0:4, 0:128], lhsT=lf[:], rhs=tincl, start=True, stop=True), reads=["lf", "const"], writes=[kc_])
            P.op("pe", lambda e, pc=pc: e.matmul(pc[0:4, 128:256], lhsT=gtk[:, 0:4], rhs=ident, start=True, stop=True), reads=["gtk", "const"], writes=[kc_])
            P.op("act", lambda e, pc=pc: e.copy(out=cumT[:], in_=pc[0:4, 0:128]), reads=[kc_], writes=["cumT"])
            P.op("dve", lambda e, pc=pc: e.tensor_tensor(out=dT[:], in0=pc[0:4, 128:256], in1=cumT[:], op=ALU.subtract), reads=[kc_, "cumT"], writes=["dT"])
            for j in range(2):
                tk = slice(j * 64, (j + 1) * 64)
                tot = cumT[:, j * 64 + 63:j * 64 + 64]
                P.op("dve", lambda e, tk=tk, tot=tot: e.tensor_scalar(out=wlT[:, tk], in0=dT[:, tk], scalar1=tot, scalar2=None, op0=ALU.add),
                     reads=["dT", "cumT"], writes=["wlT"])
                P.op("dve", lambda e, tk=tk: e.tensor_reduce(out=Mx[:], in_=wlT[:, tk], axis=AX.X, op=ALU.max), reads=["wlT"], writes=["Mx"])
                P.op("dve", lambda e, tot=tot: e.tensor_tensor(out=t1[:], in0=mst[:], in1=tot, op=ALU.add), reads=["mst", "cumT"], writes=["t1"])
                P.op("dve", lambda e: e.tensor_tensor(out=mnew[:], in0=t1[:], in1=Mx[:], op=ALU.max), reads=["t1", "Mx"], writes=["mnew"])
                P.op("dve", lambda e: e.tensor_scalar(out=nm[:], in0=mnew[:], scalar1=-1.0, scalar2=None, op0=ALU.mult), reads=["mnew"], writes=["nm"])
                P.op("act", lambda e, j=j: e.activation(out=smq[:, j:j + 1], in_=t1[:], func=AF.Exp, bias=nm[:, 0:1]), reads=["t1", "nm"], writes=["smq"])
                P.op("act", lambda e, j=j: e.activation(out=smq[:, 2 + j:3 + j], in_=nm[:], func=AF.Exp), reads=["nm"], writes=["smq"])
                P.op("act", lambda e, tk=tk: e.activation(out=wexT[:, tk], in_=wlT[:, tk], func=AF.Exp, bias=nm[:, 0:1]), reads=["wlT", "nm"], writes=["wexT"])
                P.op("dve", lambda e: e.tensor_copy(out=mst[:], in_=mnew[:]), reads=["mnew"], writes=["mst"])
            P.op("dve", lambda e: e.tensor_tensor(out=msk[:], in0=smq[:].unsqueeze(2).broadcast_to([4, 4, 4]),
                                                  in1=I4.unsqueeze(1).broadcast_to([4, 4, 4]), op=ALU.mult), reads=["smq", "const"], writes=["msk"])
            pbc, kbc = K.bank()
            P.op("pe", lambda e, pbc=pbc: e.matmul(pbc[:, 0:16], lhsT=ones4, rhs=msk[:].rearrange("p a b -> p (a b)"), start=True, stop=True),
                 reads=["msk", "const"], writes=[kbc])
            P.op("pe", lambda e, pbc=pbc: e.transpose(out=pbc[:, 16:20], in_=wexT[:], identity=I4), reads=["wexT", "const"], writes=[kbc])
            P.op("act", lambda e, pbc=pbc: e.copy(out=bc[:], in_=pbc[:, 0:16]), reads=[kbc], writes=["bc"])
            P.op("act", lambda e, pbc=pbc: e.copy(out=wtok[:], in_=pbc[:, 16:20]), reads=[kbc], writes=["wtok"])
            if full:
                for j in range(2):
                    tk = slice(j * 64, (j + 1) * 64)
                    P.op("act", lambda e, j=j, tk=tk: e.copy(out=en[tk, :], in_=bc[tk, (2 + j) * 4:(3 + j) * 4]), reads=["bc"], writes=["en"])
            for g0 in range(0, 16, 4):
                pb, kb = K.bank()
                for jj in range(4):
                    cc = g0 + jj
                    wb = wch[nw[0] % 4]
                    kw_ = "wch%d" % (nw[0] % 4)
                    nw[0] += 1
                    K.load(wb[:].rearrange("p a b -> p (a b)"), winl_d[cc, :, :], [kw_])
                    for c in range(8):
                        P.op("pe", lambda e, c=c, jj=jj, pb=pb, wb=wb: e.matmul(pb[:, jj * 128:(jj + 1) * 128], lhsT=wb[:, c, :], rhs=xnT[:, c, :],
                                                                                  start=(c == 0), stop=(c == 7)), reads=["xnT", kw_], writes=[kb])
                P.op("act", lambda e, g0=g0, pb=pb: e.copy(out=xmT[:, g0:g0 + 4, 3:131], in_=pb[:].rearrange("p (a b) -> p a b", b=128)),
                     reads=[kb], writes=["xmT"])
            for w_ in range(4):
                cwb = cw[:, w_, :].unsqueeze(2).broadcast_to([128, 16, 128])
                if w_ == 0:
                    P.op("dve", lambda e, cwb=cwb: e.tensor_tensor(out=xcT[:], in0=xmT[:, :, 0:128], in1=cwb, op=ALU.mult),
                         reads=["xmT", "const"], writes=["xcT"])
                else:
                    P.op("dve", lambda e, cwb=cwb, w_=w_: e.tensor_tensor(out=tmpc[:], in0=xmT[:, :, w_:w_ + 128], in1=cwb, op=ALU.mult),
                         reads=["xmT", "const"], writes=["tmpc"])
                    P.op("dve", lambda e: e.tensor_tensor(out=xcT[:], in0=xcT[:], in1=tmpc[:], op=ALU.add), reads=["xcT", "tmpc"], writes=["xcT"])
            P.op("dve", lambda e: e.tensor_tensor(out=xcT[:], in0=xcT[:], in1=cb[:].unsqueeze(2).broadcast_to([128, 16, 128]), op=ALU.add),
                 reads=["xcT", "const"], writes=["xcT"])
            P.op("act", lambda e: e.activation(out=xcT[:], in_=xcT[:], func=AF.Silu), reads=["xcT"], writes=["xcT"])
            if full:
                for g0 in range(0, 16, 4):
                    pb, kb = K.bank()
                    for jj in range(4):
                        cc = g0 + jj
                        wb = wch[nw[0] % 4]
                        kw_ = "wch%d" % (nw[0] % 4)
                        nw[0] += 1
                        K.load(wb[:].rearrange("p a b -> p (a b)"), winl_d[16 + cc, :, :], [kw_])
                        for c in range(8):
                            P.op("pe", lambda e, c=c, jj=jj, pb=pb, wb=wb: e.matmul(pb[:, jj * 128:(jj + 1) * 128], lhsT=wb[:, c, :], rhs=xnT[:, c, :],
                                                                                      start=(c == 0), stop=(c == 7)), reads=["xnT", kw_], writes=[kb])
                    P.op("act", lambda e, g0=g0, pb=pb: e.activation(out=szT[:, g0:g0 + 4, :], in_=pb[:].rearrange("p (a b) -> p a b", b=128), func=AF.Silu),
                         reads=[kb], writes=["szT"])
            def bdload(which, cc):
                b_ = bdc[nb[0] % 6]
                k_ = "bdc%d" % (nb[0] % 6)
                nb[0] += 1
                K.load(b_[:], bd_d[which, cc, :, :], [k_])
                return b_, k_
            if full:
                for g0 in range(0, 16, 4):
                    pb, kb = K.bank()
                    for jj in range(4):
                        cc = g0 + jj
                        b_, k_ = bdload(0, cc)
                        P.op("pe", lambda e, jj=jj, cc=cc, pb=pb, b_=b_: e.matmul(pb[:, jj * 128:(jj + 1) * 128], lhsT=b_[:], rhs=xcT[:, cc, :], start=True, stop=True),
                             reads=["xcT", k_], writes=[kb])
                    for j in range(2):
                        P.op("act", lambda e, j=j, g0=g0, pb=pb: e.copy(out=qTm[:, j, g0:g0 + 4, j * 64:(j + 1) * 64],
                                                                        in_=pb[:].rearrange("p (a b) -> p a b", b=128)[:, :, j * 64:(j + 1) * 64]),
                             reads=[kb], writes=["qTm"])
            for h in range(4):
                pb, kb = K.bank()
                for jj in range(4):
                    cc = h * 4 + jj
                    b_, k_ = bdload(1, cc)
                    P.op("pe", lambda e, jj=jj, cc=cc, pb=pb, b_=b_: e.matmul(pb[:, jj * 128:(jj + 1) * 128], lhsT=xcT[:, cc, :], rhs=b_[:], start=True, stop=True),
                         reads=["xcT", k_], writes=[kb])
                P.op("dve", lambda e, h=h, pb=pb: e.tensor_scalar(out=kw[:, h * 512:(h + 1) * 512], in0=pb[:], scalar1=wtok[:, h:h + 1], scalar2=float(DH) ** -0.5,
                                                                  op0=ALU.mult, op1=ALU.mult), reads=[kb, "wtok"], writes=["kw"])
            for h in range(4):
                pb, kb = K.bank()
                for jj in range(4):
                    cc = h * 4 + jj
                    b_, k_ = bdload(2, cc)
                    P.op("pe", lambda e, jj=jj, cc=cc, pb=pb, b_=b_: e.matmul(pb[:, jj * 128:(jj + 1) * 128], lhsT=xmT[:, cc, 3:131], rhs=b_[:], start=True, stop=True),
                         reads=["xmT", k_], writes=[kb])
                P.op("act", lambda e, h=h, pb=pb: e.copy(out=vv[:, h * 512:(h + 1) * 512], in_=pb[:]), reads=[kb], writes=["vv"])
            P.op("dve", lambda e: e.tensor_copy(out=xmT[:, :, 0:3], in_=xmT[:, :, 128:131]), reads=["xmT"], writes=["xmT"])
            K.bi = 0
            pnum = [(K.banks[h], ("pb", h)) for h in range(4)]
            pqn, kqn = K.banks[4], ("pb", 4)
            pdn, kdn = K.banks[5], ("pb", 5)
            ci = 0
            for j in range(2):
                tk = slice(j * 64, (j + 1) * 64)
                for h in range(4):
                    abc = bc[:, j * 4 + h:j * 4 + h + 1]
                    for dk in range(4):
                        pC, kC = K.banks[6 + ci % 2], ("pb", 6 + ci % 2)
                        ci += 1
                        P.op("pe", lambda e, h=h, dk=dk, tk=tk, pC=pC: e.matmul(pC[:], lhsT=kw[tk, h * 512 + dk * 128:h * 512 + (dk + 1) * 128],
                                                                             rhs=vv[tk, h * 512:(h + 1) * 512], start=True, stop=True),
                             reads=["kw", "vv"], writes=[kC])
                        P.op("dve", lambda e, h=h, dk=dk, pC=pC, abc=abc: e.scalar_tensor_tensor(
                            out=C[:, h * 4 + dk, :], in0=C[:, h * 4 + dk, :], scalar=abc, in1=pC[:], op0=ALU.mult, op1=ALU.add),
                            reads=[kC, "bc", ("C", h * 4 + dk), "C"], writes=[("C", h * 4 + dk)])
                for h in range(4):
                    for dk in range(4):
                        P.op("pe", lambda e, h=h, dk=dk, tk=tk: e.matmul(pdn[:, h * 4 + dk:h * 4 + dk + 1], lhsT=kw[tk, h * 512 + dk * 128:h * 512 + (dk + 1) * 128],
                                                                        rhs=ones_col[tk, :], start=True, stop=True), reads=["kw", "const"], writes=[kdn])
                for h in range(4):
                    abc = bc[:, j * 4 + h:j * 4 + h + 1]
                    P.op("dve", lambda e, h=h, abc=abc: e.scalar_tensor_tensor(out=nst[:, h * 4:(h + 1) * 4], in0=nst[:, h * 4:(h + 1) * 4], scalar=abc,
                                                                              in1=pdn[:, h * 4:(h + 1) * 4], op0=ALU.mult, op1=ALU.add),
                         reads=[kdn, "bc", "nst"], writes=["nst"])
                if full:
                    for h in range(4):
                        pb_, kb_ = pnum[h]
                        for dk in range(4):
                            P.op("pe", lambda e, h=h, dk=dk, j=j, pb_=pb_: e.matmul(pb_[:], lhsT=qTm[:, j, h * 4 + dk, :], rhs=C[:, h * 4 + dk, :],
                                                                                 start=(j == 0 and dk == 0), stop=(j == 1 and dk == 3)),
                                 reads=["qTm", ("C", h * 4 + dk), "C"], writes=[kb_])
                    for h in range(4):
                        for dk in range(4):
                            P.op("pe", lambda e, h=h, dk=dk, j=j: e.matmul(pqn[:, j * 4 + h:j * 4 + h + 1], lhsT=qTm[:, j, h * 4 + dk, :], rhs=nst[:, h * 4 + dk:h * 4 + dk + 1],
                                                                          start=(dk == 0), stop=(dk == 3)), reads=["qTm", "nst"], writes=[kqn])
            if full:
                P.op("act", lambda e: e.copy(out=qn[:], in_=pqn[:, 0:4]), reads=[kqn], writes=["qn"])
                P.op("dve", lambda e: e.tensor_tensor(out=qn[:], in0=qn[:], in1=pqn[:, 4:8], op=ALU.add), reads=[kqn, "qn"], writes=["qn"])
                P.op("dve", lambda e: e.scalar_tensor_tensor(out=den[:], in0=qn[:], scalar=-1.0, in1=qn[:], op0=ALU.mult, op1=ALU.max), reads=["qn"], writes=["den"])
                P.op("dve", lambda e: e.tensor_tensor(out=den[:], in0=den[:], in1=en[:], op=ALU.max), reads=["den", "en"], writes=["den"])
                P.op("dve", lambda e: e.reciprocal(out=den[:], in_=den[:]), reads=["den"], writes=["den"])
                for h in range(4):
                    pb_, kb_ = pnum[h]
                    P.op("dve", lambda e, h=h, pb_=pb_: e.tensor_scalar(out=hh[:, h * 512:(h + 1) * 512], in0=pb_[:], scalar1=den[:, h:h + 1], scalar2=None, op0=ALU.mult),
                         reads=[kb_, "den"], writes=[("hh", h)])
                    P.op("dve", lambda e, h=h: e.bn_stats(out=stats[:, h, :], in_=hh[:, h * 512:(h + 1) * 512]), reads=[("hh", h)], writes=[("stats", h)])
                    P.op("dve", lambda e, h=h: e.bn_aggr(out=mv[:, h, :], in_=stats[:, h, :]), reads=[("stats", h)], writes=[("mv", h)])
                P.op("act", lambda e: e.activation(out=rs4[:], in_=mv[:, :, 1], func=AF.Sqrt, bias=EPS), reads=[("mv", h) for h in range(4)], writes=["rs4"])
                P.op("dve", lambda e: e.reciprocal(out=rs4[:], in_=rs4[:]), reads=["rs4"], writes=["rs4"])
                for h in range(4):
                    P.op("dve", lambda e, h=h: e.tensor_scalar(out=hh[:, h * 512:(h + 1) * 512], in0=hh[:, h * 512:(h + 1) * 512], scalar1=mv[:, h, 0:1],
                                                               scalar2=rs4[:, h:h + 1], op0=ALU.subtract, op1=ALU.mult),
                         reads=[("hh", h), ("mv", h), "rs4"], writes=[("hh", h)])
                for g0 in range(0, 16, 4):
                    pb, kb = K.bank()
                    for jj in range(4):
                        cc = g0 + jj
                        P.op("pe", lambda e, jj=jj, cc=cc, pb=pb: e.transpose(out=pb[:, jj * 128:(jj + 1) * 128], in_=hh[:, cc * 128:(cc + 1) * 128], identity=ident),
                             reads=[("hh", cc // 4), "const"], writes=[kb])
                    gsl = slice(g0, g0 + 4)
                    P.op("dve", lambda e, gsl=gsl, pb=pb: e.tensor_tensor(out=fin[:, gsl, :], in0=pb[:].rearrange("p (a b) -> p a b", b=128),
                                                                           in1=ghd[:, gsl].unsqueeze(2).broadcast_to([128, 4, 128]), op=ALU.mult),
                         reads=[kb, "const"], writes=[("fin", g0)])
                    P.op("dve", lambda e, gsl=gsl: e.tensor_tensor(out=tmpc[:, gsl, :], in0=xcT[:, gsl, :], in1=skp[:, gsl].unsqueeze(2).broadcast_to([128, 4, 128]), op=ALU.mult),
                         reads=["xcT", "const"], writes=["tmpc"])
                    P.op("dve", lambda e, gsl=gsl: e.tensor_tensor(out=fin[:, gsl, :], in0=fin[:, gsl, :], in1=tmpc[:, gsl, :], op=ALU.add),
                         reads=[("fin", g0), "tmpc"], writes=[("fin", g0)])
                    P.op("dve", lambda e, gsl=gsl: e.tensor_tensor(out=fin[:, gsl, :], in0=fin[:, gsl, :], in1=szT[:, gsl, :], op=ALU.mult),
                         reads=[("fin", g0), "szT"], writes=[("fin", g0)])
                pm0, km0 = K.bank()
                pm1, km1 = K.bank()
                for cc in range(16):
                    wb = woc[nwo[0] % 3]
                    kwo = "woc%d" % (nwo[0] % 3)
                    nwo[0] += 1
                    K.load(wb[:], wout_d[cc * 128:(cc + 1) * 128, :], [kwo])
                    P.op("pe", lambda e, cc=cc, wb=wb: e.matmul(pm0[:], lhsT=fin[:, cc, :], rhs=wb[:, 0:512], start=(cc == 0), stop=(cc == 15)),
                         reads=[("fin", (cc // 4) * 4), kwo], writes=[km0])
                    P.op("pe", lambda e, cc=cc, wb=wb: e.matmul(pm1[:], lhsT=fin[:, cc, :], rhs=wb[:, 512:1024], start=(cc == 0), stop=(cc == 15)),
                         reads=[("fin", (cc // 4) * 4), kwo], writes=[km1])
                P.op("dve", lambda e: e.tensor_tensor(out=yo[:, 0:512], in0=pm0[:], in1=xt[:, 0:512], op=ALU.add), reads=[km0, "xt"], writes=["yo"])
                P.op("dve", lambda e: e.tensor_tensor(out=yo[:, 512:1024], in0=pm1[:], in1=xt[:, 512:1024], op=ALU.add), reads=[km1, "xt"], writes=["yo"])
                P.dma("sp", lambda e, rows=rows: e.dma_start(out=y_d[rows, :], in_=yo[:]), reads=["yo"], writes=[(ykey, slot)])
        P.barrier()
        P.emit(block)


def mlstm_inputs(x_rows, g, win, b_i, b_f, conv_w, conv_b, w_q, w_k, w_v, skip, gh, wout):
    winl = np.ascontiguousarray(win[:, 0:4096].reshape(8, 128, 32, 128).transpose(2, 1, 0, 3)).reshape(32, 128, 1024)
    wg = np.ascontiguousarray(win[:, 4096:4104].reshape(8, 128, 8).transpose(1, 0, 2)).reshape(128, 64)
    bif = np.concatenate([b_i.reshape(-1), b_f.reshape(-1)]).reshape(1, 8).astype(np.float32)
    cw = np.ascontiguousarray(conv_w.reshape(4, 16, 128).transpose(2, 0, 1)).reshape(128, 64)
    cb = np.ascontiguousarray(conv_b.reshape(16, 128).T)
    bd = np.zeros((3, 16, 128, 128), np.float32)
    for a, w_ in enumerate((w_q, w_k, w_v)):
        wr = w_.reshape(16, 32, 4, 4)
        for n in range(32):
            bd[a, :, n * 4:(n + 1) * 4, n * 4:(n + 1) * 4] = wr[:, n]
    return {"x": np.ascontiguousarray(x_rows), "g": g.reshape(1, D), "winl": winl, "wg": wg, "bif": bif, "cw": cw, "cb": cb, "bd": bd,
            "skip": np.ascontiguousarray(skip.reshape(16, 128).T), "gh": np.ascontiguousarray(gh.reshape(16, 128).T), "wout": wout,
            "consts": _consts()}


_IO = {
    "peer": [("g", [1, D]), ("gf", [1, D]), ("wq", [D, 2048]), ("skT", [128, 2048]), ("u", [16384, D]), ("v", [16384, D])],
    "gla": [("g", [1, D]), ("win", [D, 3088]), ("wgu", [16, 512]), ("bg", [1, 512]), ("gh", [1, D]), ("wout", [D, D])],
    "mlstm": [("g", [1, D]), ("winl", [32, 128, 1024]), ("wg", [128, 64]), ("bif", [1, 8]), ("cw", [128, 64]), ("cb", [128, 16]),
              ("bd", [3, 16, 128, 128]), ("skip", [128, 16]), ("gh", [128, 16]), ("wout", [2048, D])],
}
_EMIT = {"peer": emit_peer, "gla": emit_gla, "mlstm": emit_mlstm}


def build_program(NT, phases, split_last=False):
    nc = bass.Bass("TRN2", target_bir_lowering=False)
    dr = lambda n, s, kind="ExternalInput": nc.dram_tensor(n, s, F32, kind=kind).ap()
    x_d = dr("x", [NT * 128, D])
    c_d = dr("consts", [128, 768])
    NTO = NT // 2 if split_last else NT
    y_d = dr("y", [NTO * 128, D], kind="ExternalOutput")
    scratch = [dr("xa", [NT * 128, D], kind="Internal"), dr("xb", [NT * 128, D], kind="Internal")] if len(phases) > 1 else []
    ios = []
    for kind, pfx, extra in phases:
        io = {n: dr(pfx + n, shp) for n, shp in _IO[kind]}
        io["consts"] = c_d
        ios.append(io)
    if split_last:
        assert phases[-2][0] == "mlstm" and phases[-1][0] == "peer"
        ios[-2]["rowsel"] = nc.dram_tensor("rowsel", [128, NT // 2], I32, kind="ExternalInput").ap()
        ios[-2]["flag"] = nc.dram_tensor("flag", [128, 1], F32, kind="ExternalInput").ap()
    with ExitStack() as st0:
        K = Ctx(nc, st0)
        block = st0.enter_context(nc.Block())
        src, skey = x_d, "x_in"
        for i, (kind, pfx, extra) in enumerate(phases):
            last = i == len(phases) - 1
            dst, dkey = (y_d, "y_out") if last else (scratch[i % 2], "xs%d" % i)
            ios[i]["x"] = src
            ios[i]["y"] = dst
            if split_last and i == len(phases) - 2:
                extra = extra + (True,)
            _EMIT[kind](K, NTO if (last and split_last) else NT, ios[i], pfx, skey, dkey, block, *extra)
            src, skey = dst, dkey
        K.P.wait_all("sp", [("y_out", it) for it in range(NTO)])
        K.P.emit(block)
    return nc


def build_peer(NT, final):
    return build_program(NT, [("peer", "", (final,))])


def build_gla(NT):
    return build_program(NT, [("gla", "", ())])


def build_mlstm(NT):
    return build_program(NT, [("mlstm", "", ())])


_PHASES = [("gla", "g0_", ()), ("peer", "p0_", (False,)), ("mlstm", "m1_", ()), ("peer", "p1_", (True,))]
_NC_CACHE = {}


def kernel(x, norm_mix_g, gla_w_in, gla_w_gate_up, gla_b_gate, gla_g_head, gla_w_out,
           mlstm_w_in, mlstm_b_i, mlstm_b_f, mlstm_conv_w, mlstm_conv_b, mlstm_w_q, mlstm_w_k,
           mlstm_w_v, mlstm_skip, mlstm_g_head, mlstm_w_out, norm_ffn_g, peer_w_query,
           peer_sub_keys, peer_u, peer_v, norm_final_g):
    f = lambda a: np.ascontiguousarray(np.asarray(a, dtype=np.float32))
    x = f(x)
    B, S, _ = x.shape
    NT = S // 128
    dummy = np.zeros((1, D), np.float32)
    parts = [
        ("g0_", gla_inputs(dummy, f(norm_mix_g)[0], f(gla_w_in)[0], f(gla_w_gate_up)[0], f(gla_b_gate)[0], f(gla_g_head)[0], f(gla_w_out)[0])),
        ("p0_", peer_inputs(dummy, f(norm_ffn_g)[0], f(norm_final_g), f(peer_w_query)[0], f(peer_sub_keys)[0], f(peer_u)[0], f(peer_v)[0])),
        ("m1_", mlstm_inputs(dummy, f(norm_mix_g)[1], f(mlstm_w_in)[0], f(mlstm_b_i)[0], f(mlstm_b_f)[0], f(mlstm_conv_w)[0], f(mlstm_conv_b)[0],
                             f(mlstm_w_q)[0], f(mlstm_w_k)[0], f(mlstm_w_v)[0], f(mlstm_skip)[0], f(mlstm_g_head)[0], f(mlstm_w_out)[0])),
        ("p1_", peer_inputs(dummy, f(norm_ffn_g)[1], f(norm_final_g), f(peer_w_query)[1], f(peer_sub_keys)[1], f(peer_u)[1], f(peer_v)[1])),
    ]
    shared = {"consts": _consts()}
    for pfx, d in parts:
        for k, v in d.items():
            if k not in ("x", "consts"):
                shared[pfx + k] = v
    def rowsel(half):
        t = np.arange(NT // 2)[None, :] + half * (NT // 2)
        return np.ascontiguousarray((t * 128 + np.arange(128)[:, None]).astype(np.int32))
    in_maps = [dict(shared, x=x[c % B], rowsel=rowsel(c // B), flag=np.full((128, 1), float(c // B), np.float32)) for c in range(8)]
    if NT not in _NC_CACHE:
        _NC_CACHE[NT] = build_program(NT, _PHASES, split_last=True)
    res = run_bass_kernel_spmd(_NC_CACHE[NT], in_maps, core_ids=list(range(8)))
    out = np.empty((B, S, D), np.float32)
    h = S // 2
    for c in range(8):
        out[c % B, (c // B) * h:(c // B + 1) * h] = res.results[c]["y"]
    return out
```

```python
from contextlib import ExitStack
import numpy as np
import concourse.bass as bass
import concourse.mybir as mybir
from concourse.bass_utils import run_bass_kernel_spmd

F32 = mybir.dt.float32
U32 = mybir.dt.uint32
I32 = mybir.dt.int32
ALU = mybir.AluOpType
AF = mybir.ActivationFunctionType
AX = mybir.AxisListType

D = 1024
EPS = 1e-6
ENGS = ("pe", "dve", "act", "pool", "sp")
NEG = -1.0e30


class Prog:
    def __init__(self, nc, stack, n_dma_sems=48):
        self.nc = nc
        self.q = {e: [] for e in ENGS}
        self.sem = {e: stack.enter_context(nc.semaphore("s_" + e)) for e in ENGS}
        self.cnt = {e: 0 for e in ENGS}
        self.dsem = [stack.enter_context(nc.semaphore("d%d" % i)) for i in range(n_dma_sems)]
        self.dcnt = [0] * n_dma_sems
        self.dpool = {"sp": list(range(0, 16)), "act": list(range(0, 16)), "pool": list(range(16, n_dma_sems))}
        self.dnext = {"sp": 0, "act": 0, "pool": 0}
        self.w = {}
        self.r = {}
        self.waited = {e: {} for e in ENGS}
        self.semobj = {}
        for e in ENGS:
            self.semobj[("e", e)] = self.sem[e]
        for i, s in enumerate(self.dsem):
            self.semobj[("d", i)] = s

    def _wait(self, eng, ev):
        sid, val = ev
        if sid == ("e", eng) and eng == "pe":
            return
        if self.waited[eng].get(sid, 0) >= val:
            return
        self.waited[eng][sid] = val
        s = self.semobj[sid]
        self.q[eng].append(lambda e, s=s, val=val: e.wait_ge(s, val))

    def _deps(self, eng, reads, writes):
        for k in reads:
            ev = self.w.get(k)
            if ev is not None:
                self._wait(eng, ev)
        for k in writes:
            ev = self.w.get(k)
            if ev is not None:
                self._wait(eng, ev)
            for ev in self.r.get(k, ()):
                self._wait(eng, ev)

    def _record(self, ev, reads, writes):
        for k in reads:
            if k not in writes:
                self.r.setdefault(k, []).append(ev)
        for k in writes:
            self.w[k] = ev
            self.r[k] = []

    def op(self, eng, fn, reads=(), writes=()):
        self._deps(eng, reads, writes)
        self.cnt[eng] += 1
        s = self.sem[eng]
        self.q[eng].append(lambda e, fn=fn, s=s: fn(e).then_inc(s, 1))
        ev = (("e", eng), self.cnt[eng])
        self._record(ev, reads, writes)
        return ev

    def dma(self, eng, fn, reads=(), writes=()):
        pool_ = self.dpool[eng]
        key_ = "pool" if eng == "pool" else "sp"
        i = pool_[self.dnext[key_] % len(pool_)]
        self.dnext[key_] += 1
        if self.dcnt[i] > 0:
            self._wait(eng, (("d", i), 16 * self.dcnt[i]))
        self._deps(eng, reads, writes)
        self.dcnt[i] += 1
        s = self.dsem[i]
        self.q[eng].append(lambda e, fn=fn, s=s: fn(e).then_inc(s, 16))
        ev = (("d", i), 16 * self.dcnt[i])
        self._record(ev, reads, writes)
        return ev

    def wait_all(self, eng, keys):
        for k in keys:
            ev = self.w.get(k)
            if ev is not None:
                self._wait(eng, ev)

    def barrier(self):
        for eng in ENGS:
            for f_ in ENGS:
                if f_ != eng and self.cnt[f_] > 0:
                    self._wait(eng, (("e", f_), self.cnt[f_]))
            for i, c in enumerate(self.dcnt):
                if c > 0:
                    self._wait(eng, (("d", i), 16 * c))

    def emit(self, block):
        q = self.q
        self.q = {e: [] for e in ENGS}

        @block.tensor
        def _(e):
            for f in q["pe"]:
                f(e)

        @block.vector
        def _(e):
            for f in q["dve"]:
                f(e)

        @block.scalar
        def _(e):
            for f in q["act"]:
                f(e)

        @block.gpsimd
        def _(e):
            for f in q["pool"]:
                f(e)

        @block.sync
        def _(e):
            for f in q["sp"]:
                f(e)


class Ctx:
    def __init__(self, nc, st):
        self.nc = nc
        self.st = st
        self.P = Prog(nc, st)
        self.banks = [st.enter_context(nc.psum_tensor("pb%d" % i, [128, 512], F32)) for i in range(8)]
        self.bi = 0
        self.evi = 0

    def sb(self, name, shape, dt=F32):
        return self.st.enter_context(self.nc.sbuf_tensor(name, shape, dt))

    def bank(self):
        b = self.bi
        self.bi = (self.bi + 1) % 8
        return self.banks[b], ("pb", b)

    def load(self, dst_ap, src_ap, wkeys, eng="sp", reads=()):
        self.P.dma(eng, lambda e: e.dma_start(out=dst_ap, in_=src_ap), reads=reads, writes=wkeys)

    def rmsnorm(self, xt, gb, xn, sq, ms, rstd, kx, kxn, tag):
        P = self.P
        P.op("act", lambda e: e.activation(out=sq[:], in_=xt[:], func=AF.Square, accum_out=ms[:]),
             reads=[kx], writes=["sq" + tag, "ms" + tag])
        P.op("act", lambda e: e.activation(out=rstd[:], in_=ms[:], func=AF.Sqrt, scale=1.0 / D, bias=EPS),
             reads=["ms" + tag], writes=["rstd" + tag])
        P.op("dve", lambda e: e.reciprocal(out=rstd[:], in_=rstd[:]), reads=["rstd" + tag], writes=["rstd" + tag])
        P.op("dve", lambda e: e.scalar_tensor_tensor(out=xn[:], in0=xt[:], scalar=rstd[:, 0:1], in1=gb[:],
                                                     op0=ALU.mult, op1=ALU.mult),
             reads=[kx, "rstd" + tag, "const"], writes=[kxn])

    def transpose_cols(self, src, ksrc, dst, kdst, nchunk, ident, np_in=128):
        P = self.P
        for g0 in range(0, nchunk, 4):
            n = min(4, nchunk - g0)
            pb, kb = self.bank()
            for j in range(n):
                c = g0 + j
                P.op("pe", lambda e, c=c, j=j, pb=pb: e.transpose(out=pb[:, j * 128:j * 128 + np_in],
                                                                   in_=src[0:np_in, c * 128:(c + 1) * 128],
                                                                   identity=ident[0:np_in, 0:np_in]),
                     reads=[ksrc, "const"], writes=[kb])
            self.evi += 1
            if np_in == 128:
                o = dst[:, g0:g0 + n, :]
                i_ = pb[:, 0:n * 128].rearrange("p (a b) -> p a b", b=128)
            else:
                o = dst[:, g0:g0 + n, 0:np_in]
                i_ = pb[:, 0:n * 128].rearrange("p (a b) -> p a b", b=128)[:, :, 0:np_in]
            if self.evi % 2 == 0:
                P.op("act", lambda e, o=o, i_=i_: e.copy(out=o, in_=i_), reads=[kb], writes=[kdst])
            else:
                P.op("dve", lambda e, o=o, i_=i_: e.tensor_copy(out=o, in_=i_), reads=[kb], writes=[kdst])


def _consts():
    c = np.zeros((128, 768), np.float32)
    c[:, 0:128] = np.eye(128, dtype=np.float32)
    s = np.arange(128)
    same = (s[:, None] // 64) == (s[None, :] // 64)
    c[:, 128:256] = (same & (s[:, None] > s[None, :])).astype(np.float32)
    c[:, 256] = (s < 64)
    c[:, 257] = (s >= 64)
    c[:, 258] = 1.0
    c[:, 272:288] = np.arange(16, dtype=np.float32)[None, :]
    c[0:4, 384:512] = 1.0
    c[:, 512:640] = (same & (s[:, None] <= s[None, :])).astype(np.float32)
    return c


def emit_peer(K, NT, io, pfx, xkey, ykey, block, final, NBUF=5):
    nc = K.nc
    x_d = io["x"]
    g_d = io["g"]
    gf_d = io["gf"]
    wq_d = io["wq"]
    sk_d = io["skT"]
    uv_d = io["uv"]
    c_d = io["consts"]
    y_d = io["y"]
    with ExitStack() as st:
        K.st = st
        P = K.P
        sb = lambda n_, s_, dt_=F32: K.sb(pfx + n_, s_, dt_)
        wq = sb("wq_s", [128, 8, 2048])
        skT = sb("skT_s", [128, 16, 128])
        cst = sb("cst", [128, 768])
        gb = sb("gb", [128, D])
        gfb = sb("gfb", [128, D if final else 1])
        xts = [sb("xt%d" % i, [128, D]) for i in range(3)]
        xns = [sb("xn%d" % i, [128, D]) for i in range(2)]
        idxs = [sb("idx%d" % i, [128, 128], I32) for i in range(3)]
        wgts = [sb("wgt%d" % i, [128, 128]) for i in range(2)]
        ms = sb("ms", [128, 1])
        rstd = sb("rstd", [128, 1])
        xnT = sb("xnT", [128, 8, 128])
        qT = sb("qT", [128, 16, 128])
        sc = sb("sc", [128, 16, 128])
        sc2 = sb("sc2", [128, 128])
        stop = sb("stop", [128, 16, 16])
        itop = sb("itop", [128, 16, 16], U32)
        itopf = sb("itopf", [128, 16, 16])
        cand = sb("cand", [128, 8, 256])
        cand2 = sb("cand2", [128, 256])
        best = sb("best", [128, 8, 16])
        pos = sb("pos", [128, 8, 16], U32)
        pa = sb("pa", [128, 8, 16], U32)
        pbb = sb("pbb", [128, 8, 16], U32)
        paf = sb("paf", [128, 8, 16])
        pbf = sb("pbf", [128, 8, 16])
        isel = sb("isel", [128, 8, 16])
        jsel = sb("jsel", [128, 8, 16])
        idxf = sb("idxf", [128, 128])
        gats = [sb("gat%d" % i, [128, 8, 16]) for i in range(2)]
        gsum = sb("gsum", [128, 8])
        apre = sb("apre", [128, 128])
        uvb = [sb("uvb%d" % i, [128, 2 * D]) for i in range(NBUF)]
        gl = sb("gl", [128, 128])
        NDG = 8
        dg = [sb("dg%d" % i, [128, 128]) for i in range(NDG)]
        junks = [sb("jnk%d" % i, [128, D]) for i in range(2)]
        junk2 = sb("junk2", [128, D])
        yo = sb("yo", [128, D])
        ms2 = sb("ms2", [128, 1])
        rstd2 = sb("rstd2", [128, 1])

        ident = cst[:, 0:128]
        iota16 = cst[:, 272:288]
        rs_d = io.get("rowsel")
        if rs_d is not None:
            rsel = sb("rsel", [128, NT], I32)
            K.load(rsel[:], rs_d, ["const"])

        K.load(cst[:], c_d, ["const"])
        K.load(gb[:], g_d[0:1, :].broadcast_to([128, D]), ["const"])
        if final:
            K.load(gfb[:], gf_d[0:1, :].broadcast_to([128, D]), ["const"])
        K.load(skT[:].rearrange("p a b -> p (a b)"), sk_d, ["const"])
        for c in range(8):
            K.load(wq[:, c, :], wq_d[c * 128:(c + 1) * 128, :], ["const"])

        abank = [0]

        def bankA():
            b = abank[0]
            abank[0] = (abank[0] + 1) % 6
            return K.banks[b], ("pb", b)

        def top16(src_ap, ksrc, scratch_ap, kscr, vals_ap, idx_ap, kout):
            P.op("dve", lambda e: e.max(out=vals_ap[:, 0:8], in_=src_ap), reads=[ksrc], writes=[kout])
            P.op("dve", lambda e: e.max_index(out=idx_ap[:, 0:8], in_max=vals_ap[:, 0:8], in_values=src_ap),
                 reads=[ksrc, kout], writes=[kout + "i"])
            P.op("dve", lambda e: e.match_replace(out=scratch_ap, in_to_replace=vals_ap[:, 0:8], in_values=src_ap,
                                                  imm_value=NEG), reads=[ksrc, kout], writes=[kscr])
            P.op("dve", lambda e: e.max(out=vals_ap[:, 8:16], in_=scratch_ap), reads=[kscr], writes=[kout])
            P.op("dve", lambda e: e.max_index(out=idx_ap[:, 8:16], in_max=vals_ap[:, 8:16], in_values=scratch_ap),
                 reads=[kscr, kout], writes=[kout + "i"])

        def stageA(it):
            xt = xts[it % 3]
            kxt = "xt%d" % (it % 3)
            xn = xns[it % 2]
            kxn = "xn%d" % (it % 2)
            idx = idxs[it % 3]
            kidx = "idx%d" % (it % 3)
            gat = gats[it % 2]
            kgat = "gat%d" % (it % 2)
            rows = slice(it * 128, (it + 1) * 128)
            if rs_d is None:
                K.load(xt[:], x_d[rows, :], [kxt], reads=[(xkey, it)])
            else:
                P.dma("pool", lambda e, it=it: e.indirect_dma_start(
                    out=xt[:, :], out_offset=None, in_=x_d[:, :],
                    in_offset=bass.IndirectOffsetOnAxis(ap=rsel[:, it:it + 1], axis=0)),
                    reads=["const"] + [(xkey, j_) for j_ in range(x_d.shape[0] // 128)] if it == 0 else ["const"], writes=[kxt])
            yield
            K.rmsnorm(xt, gb, xn, junk2, ms, rstd, kxt, kxn, "A")
            yield
            for g0 in range(0, 8, 4):
                pb, kb = bankA()
                for j in range(4):
                    c = g0 + j
                    P.op("pe", lambda e, c=c, j=j, pb=pb: e.transpose(out=pb[:, j * 128:(j + 1) * 128], in_=xn[:, c * 128:(c + 1) * 128],
                                                                       identity=ident), reads=[kxn, "const"], writes=[kb])
                P.op("act", lambda e, g0=g0, pb=pb: e.copy(out=xnT[:, g0:g0 + 4, :], in_=pb[:].rearrange("p (a b) -> p a b", b=128)),
                     reads=[kb], writes=["xnT"])
                yield
            for g0 in range(0, 16, 4):
                pb, kb = bankA()
                for j in range(4):
                    g = g0 + j
                    for c in range(8):
                        P.op("pe", lambda e, g=g, j=j, c=c, pb=pb: e.matmul(
                            pb[:, j * 128:(j + 1) * 128], lhsT=wq[:, c, g * 128:(g + 1) * 128], rhs=xnT[:, c, :],
                            start=(c == 0), stop=(c == 7)), reads=["const", "xnT"], writes=[kb])
                    yield
                P.op("act", lambda e, g0=g0, pb=pb: e.copy(out=qT[:, g0:g0 + 4, :].rearrange("p a b -> p (a b)"), in_=pb[:]),
                     reads=[kb], writes=["qT"])
            for g0 in range(0, 16, 4):
                pb, kb = bankA()
                for j in range(4):
                    g = g0 + j
                    P.op("pe", lambda e, g=g, j=j, pb=pb: e.matmul(
                        pb[:, j * 128:(j + 1) * 128], lhsT=qT[:, g, :], rhs=skT[:, g, :], start=True, stop=True),
                        reads=["qT", "const"], writes=[kb])
                P.op("act", lambda e, g0=g0, pb=pb: e.copy(out=sc[:, g0:g0 + 4, :].rearrange("p a b -> p (a b)"), in_=pb[:]),
                     reads=[kb], writes=["sc"])
                yield
            for g in range(16):
                top16(sc[:, g, :], "sc", sc2[:], "sc2", stop[:, g, :], itop[:, g, :], "stop")
                yield
            P.op("dve", lambda e: e.tensor_copy(out=itopf[:], in_=itop[:]), reads=["stopi"], writes=["itopf"])
            s4 = stop[:].rearrange("p (h two) k -> p h two k", two=2)
            i4 = itopf[:].rearrange("p (h two) k -> p h two k", two=2)
            P.op("dve", lambda e: e.tensor_tensor(
                out=cand[:].rearrange("p h (a b) -> p h a b", b=16),
                in0=s4[:, :, 0, :].unsqueeze(3).broadcast_to([128, 8, 16, 16]),
                in1=s4[:, :, 1, :].unsqueeze(2).broadcast_to([128, 8, 16, 16]), op=ALU.add),
                reads=["stop"], writes=["cand"])
            yield
            for h in range(8):
                top16(cand[:, h, :], "cand", cand2[:], "cand2", best[:, h, :], pos[:, h, :], "best")
                yield
            P.op("dve", lambda e: e.tensor_single_scalar(out=pa[:], in_=pos[:], scalar=4, op=ALU.logical_shift_right),
                 reads=["besti"], writes=["pa"])
            P.op("dve", lambda e: e.tensor_single_scalar(out=pbb[:], in_=pos[:], scalar=15, op=ALU.bitwise_and),
                 reads=["besti"], writes=["pbb"])
            P.op("dve", lambda e: e.tensor_copy(out=paf[:], in_=pa[:]), reads=["pa"], writes=["paf"])
            P.op("dve", lambda e: e.tensor_copy(out=pbf[:], in_=pbb[:]), reads=["pbb"], writes=["pbf"])
            yield
            io4 = iota16.unsqueeze(1).unsqueeze(1).broadcast_to([128, 8, 16, 16])
            eq = cand[:].rearrange("p h (a b) -> p h a b", b=16)
            for (pf, kpf, two, sel, ksel) in ((paf, "paf", 0, isel, "isel"), (pbf, "pbf", 1, jsel, "jsel")):
                P.op("dve", lambda e, pf=pf: e.tensor_tensor(out=eq, in0=pf[:].unsqueeze(3).broadcast_to([128, 8, 16, 16]),
                                                             in1=io4, op=ALU.is_equal), reads=[kpf, "const"], writes=["cand"])
                P.op("dve", lambda e, two=two: e.tensor_tensor(out=eq, in0=eq,
                                                               in1=i4[:, :, two, :].unsqueeze(2).broadcast_to([128, 8, 16, 16]),
                                                               op=ALU.mult), reads=["cand", "itopf"], writes=["cand"])
                P.op("dve", lambda e, sel=sel: e.tensor_reduce(out=sel[:], in_=eq, axis=AX.X, op=ALU.add),
                     reads=["cand"], writes=[ksel])
                yield
            P.op("dve", lambda e: e.scalar_tensor_tensor(out=idxf[:], in0=isel[:].rearrange("p h k -> p (h k)"), scalar=128.0,
                                                         in1=jsel[:].rearrange("p h k -> p (h k)"), op0=ALU.mult, op1=ALU.add),
                 reads=["isel", "jsel"], writes=["idxf"])
            P.op("dve", lambda e: e.tensor_copy(out=idx[:], in_=idxf[:]), reads=["idxf"], writes=[kidx])
            yield
            P.op("dve", lambda e: e.tensor_tensor(out=gat[:], in0=best[:], in1=best[:, :, 0:1].broadcast_to([128, 8, 16]),
                                                  op=ALU.subtract), reads=["best"], writes=[kgat])
            P.op("act", lambda e: e.activation(out=gat[:], in_=gat[:], func=AF.Exp), reads=[kgat], writes=[kgat])
            P.op("dve", lambda e: e.tensor_reduce(out=gsum[:], in_=gat[:], axis=AX.X, op=ALU.add), reads=[kgat], writes=["gsum"])
            P.op("dve", lambda e: e.reciprocal(out=gsum[:], in_=gsum[:]), reads=["gsum"], writes=["gsum"])
            P.op("dve", lambda e: e.tensor_tensor(out=gat[:], in0=gat[:], in1=gsum[:].unsqueeze(2).broadcast_to([128, 8, 16]),
                                                  op=ALU.mult), reads=[kgat, "gsum"], writes=[kgat])
            yield

        def stageBC(it):
            xt = xts[it % 3]
            kxt = "xt%d" % (it % 3)
            xn = xns[it % 2]
            kxn = "xn%d" % (it % 2)
            idx = idxs[it % 3]
            kidx = "idx%d" % (it % 3)
            gat = gats[it % 2]
            kgat = "gat%d" % (it % 2)
            gat2 = gat[:].rearrange("p h k -> p (h k)")
            py0, ky0 = K.banks[6], ("pb", 6)
            py1, ky1 = K.banks[7], ("pb", 7)
            for s in range(128):
                b = uvb[s % NBUF]
                kb_ = "uvb%d" % (s % NBUF)
                d_ = dg[s % NDG]
                kd = "dg%d" % (s % NDG)
                P.dma("pool", lambda e, b=b, s=s: e.indirect_dma_start(
                    out=b[:, :], out_offset=None, in_=uv_d[:, :],
                    in_offset=bass.IndirectOffsetOnAxis(ap=idx[:, s:s + 1], axis=0)), reads=[kidx], writes=[kb_])
                P.op("dve", lambda e, b=b, s=s: e.scalar_tensor_tensor(
                    out=junks[s % 2][:], in0=b[:, 0:D], scalar=1.0, in1=xn[:], op0=ALU.mult, op1=ALU.mult, accum_out=apre[:, s:s + 1]),
                    reads=[kb_, kxn], writes=[("apre", s), "junk%d" % (s % 2)])
                P.op("act", lambda e, s=s: e.activation(out=gl[:, s:s + 1], in_=apre[:, s:s + 1], func=AF.Gelu),
                     reads=[("apre", s)], writes=[("gl", s)])
                P.op("dve", lambda e, d_=d_, s=s: e.tensor_scalar(out=d_[:], in0=ident, scalar1=gl[:, s:s + 1], scalar2=gat2[:, s:s + 1],
                                                                  op0=ALU.mult, op1=ALU.mult),
                     reads=[("gl", s), kgat, "const"], writes=[kd])
                P.op("pe", lambda e, d_=d_, b=b, s=s: e.matmul(py0[:], lhsT=d_[:], rhs=b[:, D:D + 512], start=(s == 0), stop=(s == 127)),
                     reads=[kd, kb_], writes=[ky0])
                P.op("pe", lambda e, d_=d_, b=b, s=s: e.matmul(py1[:], lhsT=d_[:], rhs=b[:, D + 512:2 * D], start=(s == 0), stop=(s == 127)),
                     reads=[kd, kb_], writes=[ky1])
                yield
            P.op("dve", lambda e: e.tensor_tensor(out=yo[:, 0:512], in0=py0[:], in1=xt[:, 0:512], op=ALU.add),
                 reads=[ky0, kxt], writes=["yo"])
            P.op("dve", lambda e: e.tensor_tensor(out=yo[:, 512:1024], in0=py1[:], in1=xt[:, 512:1024], op=ALU.add),
                 reads=[ky1, kxt], writes=["yo"])
            if final:
                K.rmsnorm(yo, gfb, yo, junk2, ms2, rstd2, "yo", "yo", "C")
            rows = slice(it * 128, (it + 1) * 128)
            P.dma("sp", lambda e, rows=rows: e.dma_start(out=y_d[rows, :], in_=yo[:]), reads=["yo"], writes=[(ykey, it)])
            yield

        def drain(gens, weights):
            alive = [g is not None for g in gens]
            while any(alive):
                for gi, g in enumerate(gens):
                    if not alive[gi]:
                        continue
                    for _ in range(weights[gi]):
                        try:
                            next(g)
                        except StopIteration:
                            alive[gi] = False
                            break

        for k in range(-1, NT):
            gens = [stageBC(k) if 0 <= k < NT else None,
                    stageA(k + 1) if 0 <= k + 1 < NT else None]
            drain(gens, [2, 1])
        P.barrier()
        P.emit(block)


def peer_inputs(x_rows, g, gf, wq, sub_keys, u, v):
    skT = np.ascontiguousarray(sub_keys.reshape(16, 128, 128).transpose(2, 0, 1)).reshape(128, 2048)
    return {"x": np.ascontiguousarray(x_rows), "g": g.reshape(1, D), "gf": gf.reshape(1, D), "wq": wq, "skT": skT,
            "uv": np.concatenate([u, v], axis=1), "consts": _consts()}


def emit_gla(K, NT, io, pfx, xkey, ykey, block):
    nc = K.nc
    x_d = io["x"]
    g_d = io["g"]
    win_d = io["win"]
    wgu_d = io["wgu"]
    bg_d = io["bg"]
    gh_d = io["gh"]
    wout_d = io["wout"]
    c_d = io["consts"]
    y_d = io["y"]
    with ExitStack() as st:
        K.st = st
        P = K.P
        sb = lambda n_, s_, dt_=F32: K.sb(pfx + n_, s_, dt_)
        win = sb("win_s", [128, 8, 3088])
        wout = sb("wout_s", [128, 8, D])
        wgu = sb("wgu_s", [16, 512])
        bg = sb("bg_s", [1, 512])
        cst = sb("cst", [128, 768])
        gb = sb("gb", [128, D])
        ghb = sb("ghb", [128, D])
        xt = sb("xt", [128, D])
        sq = sb("sq", [128, D])
        ms = sb("ms", [128, 1])
        rstd = sb("rstd", [128, 1])
        xn = sb("xn", [128, D])
        xnT = sb("xnT", [128, 8, 128])
        glowT = sb("glowT", [16, 128])
        ta = sb("ta", [128, 512])
        ls = sb("ls", [128, 512])
        expD = sb("expD", [128, 512])
        etot = sb("etot", [128, 8])
        kdec = sb("kdec", [128, 512])
        vv = sb("vv", [128, D])
        gsil = sb("gsil", [128, D])
        qTm = sb("qTm", [128, 2, 512])
        S = sb("S", [128, 4, 256])
        ss = sb("ss", [128, 4])
        og = sb("og", [128, D])
        ogT = sb("ogT", [128, 8, 128])
        yo = sb("yo", [128, D])
        ident = cst[:, 0:128]
        ustrict = cst[:, 128:256]
        chunkind = cst[:, 256:258]
        ones_row = cst[0:1, 384:512]

        K.load(cst[:], c_d, ["const"])
        K.load(gb[:], g_d[0:1, :].broadcast_to([128, D]), ["const"])
        K.load(ghb[:], gh_d[0:1, :].broadcast_to([128, D]), ["const"])
        K.load(wgu[:], wgu_d, ["const"])
        K.load(bg[:], bg_d, ["const"])
        for c in range(8):
            K.load(win[:, c, :], win_d[c * 128:(c + 1) * 128, :], ["const"])
            K.load(wout[:, c, :], wout_d[c * 128:(c + 1) * 128, :], ["const"])
        P.op("dve", lambda e: e.memset(S[:], 0.0), writes=["S"])
        P.op("dve", lambda e: e.memset(qTm[:], 0.0), writes=["qT"])

        def proj(col0, ncol, pb, kb, pcol=0):
            for c in range(8):
                P.op("pe", lambda e, c=c: e.matmul(pb[:, pcol:pcol + ncol], lhsT=xnT[:, c, :], rhs=win[:, c, col0:col0 + ncol],
                                                   start=(c == 0), stop=(c == 7)), reads=["xnT", "const"], writes=[kb])

        for it in range(NT):
            rows = slice(it * 128, (it + 1) * 128)
            K.load(xt[:], x_d[rows, :], ["xt"], reads=[(xkey, it)])
            K.rmsnorm(xt, gb, xn, sq, ms, rstd, "xt", "xn", "")
            K.transpose_cols(xn, "xn", xnT, "xnT", 8, ident)
            pb, kb = K.bank()
            for c in range(8):
                P.op("pe", lambda e, c=c, pb=pb: e.matmul(pb[0:16, 0:128], lhsT=win[:, c, 3072:3088], rhs=xnT[:, c, :],
                                                         start=(c == 0), stop=(c == 7)), reads=["xnT", "const"], writes=[kb])
            P.op("act", lambda e, pb=pb: e.copy(out=glowT[:], in_=pb[0:16, 0:128]), reads=[kb], writes=["glowT"])
            pz, kz = K.bank()
            P.op("pe", lambda e: e.matmul(pz[:], lhsT=glowT[:], rhs=wgu[:], start=True, stop=False),
                 reads=["glowT", "const"], writes=[kz])
            P.op("pe", lambda e: e.matmul(pz[:], lhsT=ones_row, rhs=bg[:], start=False, stop=True),
                 reads=["const"], writes=[kz])
            P.op("act", lambda e: e.copy(out=ls[:], in_=pz[:]), reads=[kz], writes=["ls"])
            P.op("dve", lambda e: e.scalar_tensor_tensor(out=ta[:], in0=pz[:], scalar=-1.0, in1=ls[:], op0=ALU.mult, op1=ALU.min),
                 reads=[kz, "ls"], writes=["ta"])
            P.op("act", lambda e: e.activation(out=ta[:], in_=ta[:], func=AF.Exp), reads=["ta"], writes=["ta"])
            P.op("act", lambda e: e.activation(out=ta[:], in_=ta[:], func=AF.Ln, bias=1.0), reads=["ta"], writes=["ta"])
            P.op("dve", lambda e: e.scalar_tensor_tensor(out=ls[:], in0=pz[:], scalar=0.0, in1=ta[:], op0=ALU.min, op1=ALU.subtract),
                 reads=[kz, "ta"], writes=["ls"])
            pd, kd = K.bank()
            P.op("pe", lambda e: e.matmul(pd[:], lhsT=ustrict, rhs=ls[:], start=True, stop=True), reads=["ls", "const"], writes=[kd])
            P.op("act", lambda e: e.activation(out=expD[:], in_=pd[:], func=AF.Exp, scale=1.0 / 16.0), reads=[kd], writes=["expD"])
            pt, kt = K.bank()
            for h in range(4):
                P.op("pe", lambda e, h=h: e.matmul(pt[:, h * 2:(h + 1) * 2], lhsT=ls[:, h * 128:(h + 1) * 128], rhs=chunkind,
                                                   start=True, stop=True), reads=["ls", "const"], writes=[kt])
            P.op("act", lambda e: e.activation(out=etot[:], in_=pt[:, 0:8], func=AF.Exp, scale=1.0 / 16.0), reads=[kt], writes=["etot"])
            pk, kk = K.bank()
            proj(512, 512, pk, kk)
            P.op("dve", lambda e: e.tensor_tensor(out=kdec[:], in0=pk[:], in1=expD[:], op=ALU.mult), reads=[kk, "expD"], writes=["kdec"])
            for hf in range(2):
                pv, kv = K.bank()
                proj(1024 + hf * 512, 512, pv, kv)
                P.op("act", lambda e, hf=hf, pv=pv: e.copy(out=vv[:, hf * 512:(hf + 1) * 512], in_=pv[:]), reads=[kv], writes=["vv"])
            for hf in range(2):
                pr, kr = K.bank()
                proj(2048 + hf * 512, 512, pr, kr)
                P.op("act", lambda e, hf=hf, pr=pr: e.activation(out=gsil[:, hf * 512:(hf + 1) * 512], in_=pr[:], func=AF.Silu),
                     reads=[kr], writes=["gsil"])
            P.op("dve", lambda e: e.tensor_tensor(out=gsil[:], in0=gsil[:], in1=ghb[:], op=ALU.mult), reads=["gsil", "const"], writes=["gsil"])
            pq, kq = K.bank()
            for h in range(4):
                for c in range(8):
                    P.op("pe", lambda e, h=h, c=c: e.matmul(pq[:, h * 128:(h + 1) * 128], lhsT=win[:, c, h * 128:(h + 1) * 128],
                                                            rhs=xnT[:, c, :], start=(c == 0), stop=(c == 7)),
                         reads=["xnT", "const"], writes=[kq])
            for j in range(2):
                P.op("act", lambda e, j=j: e.mul(out=qTm[:, j, :].rearrange("p (h t) -> p h t", t=128)[:, :, j * 64:(j + 1) * 64],
                                                 in_=pq[:].rearrange("p (h t) -> p h t", t=128)[:, :, j * 64:(j + 1) * 64],
                                                 mul=128.0 ** -0.5), reads=[kq], writes=["qT"])
            po = [K.bank() for _ in range(4)]
            for j in range(2):
                tk = slice(j * 64, (j + 1) * 64)
                for hp in range(2):
                    pS, kS = K.bank()
                    for hh in range(2):
                        h = hp * 2 + hh
                        P.op("pe", lambda e, h=h, hh=hh, pS=pS, tk=tk: e.matmul(pS[:, hh * 256:(hh + 1) * 256], lhsT=kdec[tk, h * 128:(h + 1) * 128],
                                                                         rhs=vv[tk, h * 256:(h + 1) * 256], start=True, stop=True),
                             reads=["kdec", "vv"], writes=[kS])
                    for hh in range(2):
                        h = hp * 2 + hh
                        P.op("dve", lambda e, h=h, hh=hh, pS=pS, j=j: e.scalar_tensor_tensor(
                            out=S[:, h, :], in0=S[:, h, :], scalar=etot[:, h * 2 + j:h * 2 + j + 1], in1=pS[:, hh * 256:(hh + 1) * 256],
                            op0=ALU.mult, op1=ALU.add), reads=[kS, "etot", ("S", h)], writes=[("S", h)])
                for h in range(4):
                    pb_, kb_ = po[h]
                    P.op("pe", lambda e, h=h, pb_=pb_, j=j: e.matmul(pb_[:, 0:256], lhsT=qTm[:, j, h * 128:(h + 1) * 128],
                                                               rhs=S[:, h, :], start=(j == 0), stop=(j == 1)), reads=["qT", ("S", h), "S"], writes=[kb_])
            for h in range(4):
                pb_, kb_ = po[h]
                P.op("act", lambda e, h=h, pb_=pb_: e.activation(out=sq[:, 0:256], in_=pb_[:, 0:256], func=AF.Square,
                                                               accum_out=ss[:, h:h + 1]), reads=[kb_], writes=["sq", ("ss", h)])
            P.op("act", lambda e: e.activation(out=ss[:], in_=ss[:], func=AF.Sqrt, scale=1.0 / 256.0, bias=EPS),
                 reads=[("ss", h) for h in range(4)], writes=[("ss", h) for h in range(4)])
            P.op("dve", lambda e: e.reciprocal(out=ss[:], in_=ss[:]), reads=[("ss", h) for h in range(4)], writes=[("ss", h) for h in range(4)])
            for h in range(4):
                pb_, kb_ = po[h]
                P.op("dve", lambda e, h=h, pb_=pb_: e.scalar_tensor_tensor(
                    out=og[:, h * 256:(h + 1) * 256], in0=pb_[:, 0:256], scalar=ss[:, h:h + 1],
                    in1=gsil[:, h * 256:(h + 1) * 256], op0=ALU.mult, op1=ALU.mult), reads=[kb_, ("ss", h), "gsil"], writes=["og"])
            K.transpose_cols(og, "og", ogT, "ogT", 8, ident)
            for hf in range(2):
                pm, km = K.bank()
                for c in range(8):
                    P.op("pe", lambda e, c=c, hf=hf, pm=pm: e.matmul(pm[:], lhsT=ogT[:, c, :], rhs=wout[:, c, hf * 512:(hf + 1) * 512],
                                                                     start=(c == 0), stop=(c == 7)), reads=["ogT", "const"], writes=[km])
                P.op("dve", lambda e, hf=hf, pm=pm: e.tensor_tensor(out=yo[:, hf * 512:(hf + 1) * 512], in0=pm[:], in1=xt[:, hf * 512:(hf + 1) * 512],
                                                                    op=ALU.add), reads=[km, "xt"], writes=["yo"])
            P.dma("sp", lambda e, rows=rows: e.dma_start(out=y_d[rows, :], in_=yo[:]), reads=["yo"], writes=[(ykey, it)])
        P.barrier()
        P.emit(block)


def gla_inputs(x_rows, g, win, wgu, bg, gh, wout):
    return {"x": np.ascontiguousarray(x_rows), "g": g.reshape(1, D), "win": win, "wgu": wgu, "bg": bg.reshape(1, 512),
            "gh": gh.reshape(1, D), "wout": wout, "consts": _consts()}


DH = 512


def emit_mlstm(K, NT, io, pfx, xkey, ykey, block, half=False):
    nc = K.nc
    x_d = io["x"]
    g_d = io["g"]
    winl_d = io["winl"]
    wg_d = io["wg"]
    bif_d = io["bif"]
    cw_d = io["cw"]
    cb_d = io["cb"]
    bd_d = io["bd"]
    sk_d = io["skip"]
    gh_d = io["gh"]
    wout_d = io["wout"]
    c_d = io["consts"]
    y_d = io["y"]
    with ExitStack() as st:
        K.st = st
        P = K.P
        sb = lambda n_, s_, dt_=F32: K.sb(pfx + n_, s_, dt_)
        cst = sb("cst", [128, 768])
        gb = sb("gb", [128, D])
        wg = sb("wg_s", [128, 8, 8])
        bif = sb("bif_s", [128, 8])
        cw = sb("cw_s", [128, 4, 16])
        cb = sb("cb_s", [128, 16])
        skp = sb("skp_s", [128, 16])
        ghd = sb("ghd_s", [128, 16])
        xt = sb("xt", [128, D])
        sq = sb("sq", [128, D])
        ms = sb("ms", [128, 1])
        rstd = sb("rstd", [128, 1])
        xn = sb("xn", [128, D])
        xnT = sb("xnT", [128, 8, 128])
        wch = [sb("wch%d" % i, [128, 8, 128]) for i in range(4)]
        bdc = [sb("bdc%d" % i, [128, 128]) for i in range(6)]
        woc = [sb("woc%d" % i, [128, D]) for i in range(3)]
        xmT = sb("xmT", [128, 16, 131])
        xcT = sb("xcT", [128, 16, 128])
        tmpc = sb("tmpc", [128, 16, 128])
        szT = sb("szT", [128, 16, 128])
        qTm = sb("qTm", [128, 2, 16, 128])
        kw = sb("kw", [128, 2048])
        vv = sb("vv", [128, 2048])
        hh = sb("hh", [128, 2048])
        fin = sb("fin", [128, 16, 128])
        C = sb("C", [128, 16, 512])
        nst = sb("nst", [128, 16])
        gtk = sb("gtk", [128, 8])
        gta = sb("gta", [128, 4])
        lf = sb("lf", [128, 4])
        sm4 = sb("sm4", [4, 128])
        cumT = sb("cumT", [4, 128])
        dT = sb("dT", [4, 128])
        wlT = sb("wlT", [4, 128])
        wexT = sb("wexT", [4, 128])
        mst = sb("mst", [4, 1])
        t1 = sb("t1", [4, 1])
        mnew = sb("mnew", [4, 1])
        nm = sb("nm", [4, 1])
        Mx = sb("Mx", [4, 1])
        smq = sb("smq", [4, 4])
        msk = sb("msk", [4, 4, 4])
        bc = sb("bc", [128, 16])
        wtok = sb("wtok", [128, 4])
        qn = sb("qn", [128, 4])
        en = sb("en", [128, 4])
        den = sb("den", [128, 4])
        stats = sb("stats", [128, 4, 6])
        mv = sb("mv", [128, 4, 2])
        rs4 = sb("rs4", [128, 4])
        yo = sb("yo", [128, D])
        ident = cst[:, 0:128]
        ones_col = cst[:, 258:259]
        ones4 = cst[0:4, 384:512]
        tincl = cst[:, 512:640]
        I4 = cst[0:4, 0:4]

        K.load(cst[:], c_d, ["const"])
        K.load(gb[:], g_d[0:1, :].broadcast_to([128, D]), ["const"])
        K.load(wg[:].rearrange("p a b -> p (a b)"), wg_d, ["const"])
        K.load(bif[:], bif_d[0:1, :].broadcast_to([128, 8]), ["const"])
        K.load(cw[:].rearrange("p a b -> p (a b)"), cw_d, ["const"])
        K.load(cb[:], cb_d, ["const"])
        K.load(skp[:], sk_d, ["const"])
        K.load(ghd[:], gh_d, ["const"])
        P.op("dve", lambda e: e.memset(C[:], 0.0), writes=["C"])
        P.op("dve", lambda e: e.memset(nst[:], 0.0), writes=["nst"])
        P.op("dve", lambda e: e.memset(mst[:], 0.0), writes=["mst"])
        P.op("dve", lambda e: e.memset(qTm[:], 0.0), writes=["qTm"])
        P.op("dve", lambda e: e.memset(xmT[:], 0.0), writes=["xmT"])
        nw = [0]
        nb = [0]
        nwo = [0]

        fl_d = io.get("flag")
        rs_d = io.get("rowsel")
        if half:
            rsel = sb("rsel", [128, NT // 2], I32)
            flg = sb("flg", [128, 1])
            K.load(rsel[:], rs_d, ["const"])
            K.load(flg[:], fl_d, ["const"])
            plan = [(it, False, None, None) for it in range(NT // 2)] + [None] + [(None, True, j, j) for j in range(NT // 2)]
        else:
            plan = [(it, True, it, None) for it in range(NT)]
        for step in plan:
            if step is None:
                P.op("dve", lambda e: e.tensor_scalar(out=C[:], in0=C[:], scalar1=flg[:, 0:1], scalar2=None, op0=ALU.mult),
                     reads=["const", "C"] + [("C", i_) for i_ in range(16)], writes=["C"] + [("C", i_) for i_ in range(16)])
                P.op("dve", lambda e: e.tensor_scalar(out=nst[:], in0=nst[:], scalar1=flg[:, 0:1], scalar2=None, op0=ALU.mult),
                     reads=["const", "nst"], writes=["nst"])
                P.op("dve", lambda e: e.tensor_scalar(out=mst[:], in0=mst[:], scalar1=flg[0:4, 0:1], scalar2=None, op0=ALU.mult),
                     reads=["const", "mst"], writes=["mst"])
                P.op("dve", lambda e: e.tensor_scalar(out=xmT[:, :, 0:3], in0=xmT[:, :, 0:3], scalar1=flg[:, 0:1], scalar2=None, op0=ALU.mult),
                     reads=["const", "xmT"], writes=["xmT"])
                continue
            it, full, slot, rcol = step
            if rcol is None:
                K.load(xt[:], x_d[it * 128:(it + 1) * 128, :], ["xt"], reads=[(xkey, it)])
            else:
                P.dma("pool", lambda e, rcol=rcol: e.indirect_dma_start(
                    out=xt[:, :], out_offset=None, in_=x_d[:, :],
                    in_offset=bass.IndirectOffsetOnAxis(ap=rsel[:, rcol:rcol + 1], axis=0)),
                    reads=["const"] + [(xkey, j_) for j_ in range(NT)], writes=["xt"])
            if full:
                rows = slice(slot * 128, (slot + 1) * 128)
            K.rmsnorm(xt, gb, xn, sq, ms, rstd, "xt", "xn", "")
            K.transpose_cols(xn, "xn", xnT, "xnT", 8, ident)
            pg, kg = K.bank()
            for c in range(8):
                P.op("pe", lambda e, c=c, pg=pg: e.matmul(pg[:, 0:8], lhsT=xnT[:, c, :], rhs=wg[:, c, :], start=(c == 0), stop=(c == 7)),
                     reads=["xnT", "const"], writes=[kg])
            P.op("dve", lambda e, pg=pg: e.tensor_tensor(out=gtk[:], in0=pg[:, 0:8], in1=bif[:], op=ALU.add), reads=[kg, "const"], writes=["gtk"])
            P.op("dve", lambda e: e.scalar_tensor_tensor(out=gta[:], in0=gtk[:, 4:8], scalar=-1.0, in1=gtk[:, 4:8], op0=ALU.mult, op1=ALU.min),
                 reads=["gtk"], writes=["gta"])
            P.op("act", lambda e: e.activation(out=gta[:], in_=gta[:], func=AF.Exp), reads=["gta"], writes=["gta"])
            P.op("act", lambda e: e.activation(out=gta[:], in_=gta[:], func=AF.Ln, bias=1.0), reads=["gta"], writes=["gta"])
            P.op("dve", lambda e: e.scalar_tensor_tensor(out=lf[:], in0=gtk[:, 4:8], scalar=0.0, in1=gta[:], op0=ALU.min, op1=ALU.subtract),
                 reads=["gtk", "gta"], writes=["lf"])
            pc, kc_ = K.bank()
            P.op("pe", lambda e, pc=pc: e.matmul(pc[0:4, 0:128], lhsT=lf[:], rhs=tincl, start=True, stop=True), reads=["lf", "const"], writes=[kc_])
            P.op("pe", lambda e, pc=pc: e.matmul(pc[0:4, 128:256], lhsT=gtk[:, 0:4], rhs=ident, start=True, stop=True), reads=["gtk", "const"], writes=[kc_])
            P.op("act", lambda e, pc=pc: e.copy(out=cumT[:], in_=pc[0:4, 0:128]), reads=[kc_], writes=["cumT"])
            P.op("dve", lambda e, pc=pc: e.tensor_tensor(out=dT[:], in0=pc[0:4, 128:256], in1=cumT[:], op=ALU.subtract), reads=[kc_, "cumT"], writes=["dT"])
            for j in range(2):
                tk = slice(j * 64, (j + 1) * 64)
                tot = cumT[:, j * 64 + 63:j * 64 + 64]
                P.op("dve", lambda e, tk=tk, tot=tot: e.tensor_scalar(out=wlT[:, tk], in0=dT[:, tk], scalar1=tot, scalar2=None, op0=ALU.add),
                     reads=["dT", "cumT"], writes=["wlT"])
                P.op("dve", lambda e, tk=tk: e.tensor_reduce(out=Mx[:], in_=wlT[:, tk], axis=AX.X, op=ALU.max), reads=["wlT"], writes=["Mx"])
                P.op("dve", lambda e, tot=tot: e.tensor_tensor(out=t1[:], in0=mst[:], in1=tot, op=ALU.add), reads=["mst", "cumT"], writes=["t1"])
                P.op("dve", lambda e: e.tensor_tensor(out=mnew[:], in0=t1[:], in1=Mx[:], op=ALU.max), reads=["t1", "Mx"], writes=["mnew"])
                P.op("dve", lambda e: e.tensor_scalar(out=nm[:], in0=mnew[:], scalar1=-1.0, scalar2=None, op0=ALU.mult), reads=["mnew"], writes=["nm"])
                P.op("act", lambda e, j=j: e.activation(out=smq[:, j:j + 1], in_=t1[:], func=AF.Exp, bias=nm[:, 0:1]), reads=["t1", "nm"], writes=["smq"])
                P.op("act", lambda e, j=j: e.activation(out=smq[:, 2 + j:3 + j], in_=nm[:], func=AF.Exp), reads=["nm"], writes=["smq"])
                P.op("act", lambda e, tk=tk: e.activation(out=wexT[:, tk], in_=wlT[:, tk], func=AF.Exp, bias=nm[:, 0:1]), reads=["wlT", "nm"], writes=["wexT"])
                P.op("dve", lambda e: e.tensor_copy(out=mst[:], in_=mnew[:]), reads=["mnew"], writes=["mst"])
            P.op("dve", lambda e: e.tensor_tensor(out=msk[:], in0=smq[:].unsqueeze(2).broadcast_to([4, 4, 4]),
                                                  in1=I4.unsqueeze(1).broadcast_to([4, 4, 4]), op=ALU.mult), reads=["smq", "const"], writes=["msk"])
            pbc, kbc = K.bank()
            P.op("pe", lambda e, pbc=pbc: e.matmul(pbc[:, 0:16], lhsT=ones4, rhs=msk[:].rearrange("p a b -> p (a b)"), start=True, stop=True),
                 reads=["msk", "const"], writes=[kbc])
            P.op("pe", lambda e, pbc=pbc: e.transpose(out=pbc[:, 16:20], in_=wexT[:], identity=I4), reads=["wexT", "const"], writes=[kbc])
            P.op("act", lambda e, pbc=pbc: e.copy(out=bc[:], in_=pbc[:, 0:16]), reads=[kbc], writes=["bc"])
            P.op("act", lambda e, pbc=pbc: e.copy(out=wtok[:], in_=pbc[:, 16:20]), reads=[kbc], writes=["wtok"])
            if full:
                for j in range(2):
                    tk = slice(j * 64, (j + 1) * 64)
                    P.op("act", lambda e, j=j, tk=tk: e.copy(out=en[tk, :], in_=bc[tk, (2 + j) * 4:(3 + j) * 4]), reads=["bc"], writes=["en"])
            for g0 in range(0, 16, 4):
                pb, kb = K.bank()
                for jj in range(4):
                    cc = g0 + jj
                    wb = wch[nw[0] % 4]
                    kw_ = "wch%d" % (nw[0] % 4)
                    nw[0] += 1
                    K.load(wb[:].rearrange("p a b -> p (a b)"), winl_d[cc, :, :], [kw_])
                    for c in range(8):
                        P.op("pe", lambda e, c=c, jj=jj, pb=pb, wb=wb: e.matmul(pb[:, jj * 128:(jj + 1) * 128], lhsT=wb[:, c, :], rhs=xnT[:, c, :],
                                                                                  start=(c == 0), stop=(c == 7)), reads=["xnT", kw_], writes=[kb])
                P.op("act", lambda e, g0=g0, pb=pb: e.copy(out=xmT[:, g0:g0 + 4, 3:131], in_=pb[:].rearrange("p (a b) -> p a b", b=128)),
                     reads=[kb], writes=["xmT"])
            for w_ in range(4):
                cwb = cw[:, w_, :].unsqueeze(2).broadcast_to([128, 16, 128])
                if w_ == 0:
                    P.op("dve", lambda e, cwb=cwb: e.tensor_tensor(out=xcT[:], in0=xmT[:, :, 0:128], in1=cwb, op=ALU.mult),
                         reads=["xmT", "const"], writes=["xcT"])
                else:
                    P.op("dve", lambda e, cwb=cwb, w_=w_: e.tensor_tensor(out=tmpc[:], in0=xmT[:, :, w_:w_ + 128], in1=cwb, op=ALU.mult),
                         reads=["xmT", "const"], writes=["tmpc"])
                    P.op("dve", lambda e: e.tensor_tensor(out=xcT[:], in0=xcT[:], in1=tmpc[:], op=ALU.add), reads=["xcT", "tmpc"], writes=["xcT"])
            P.op("dve", lambda e: e.tensor_tensor(out=xcT[:], in0=xcT[:], in1=cb[:].unsqueeze(2).broadcast_to([128, 16, 128]), op=ALU.add),
                 reads=["xcT", "const"], writes=["xcT"])
            P.op("act", lambda e: e.activation(out=xcT[:], in_=xcT[:], func=AF.Silu), reads=["xcT"], writes=["xcT"])
            if full:
                for g0 in range(0, 16, 4):
                    pb, kb = K.bank()
                    for jj in range(4):
                        cc = g0 + jj
                        wb = wch[nw[0] % 4]
                        kw_ = "wch%d" % (nw[0] % 4)
                        nw[0] += 1
                        K.load(wb[:].rearrange("p a b -> p (a b)"), winl_d[16 + cc, :, :], [kw_])
                        for c in range(8):
                            P.op("pe", lambda e, c=c, jj=jj, pb=pb, wb=wb: e.matmul(pb[:, jj * 128:(jj + 1) * 128], lhsT=wb[:, c, :], rhs=xnT[:, c, :],
                                                                                      start=(c == 0), stop=(c == 7)), reads=["xnT", kw_], writes=[kb])
                    P.op("act", lambda e, g0=g0, pb=pb: e.activation(out=szT[:, g0:g0 + 4, :], in_=pb[:].rearrange("p (a b) -> p a b", b=128), func=AF.Silu),
                         reads=[kb], writes=["szT"])
            def bdload(which, cc):
                b_ = bdc[nb[0] % 6]
                k_ = "bdc%d" % (nb[0] % 6)
                nb[0] += 1
                K.load(b_[:], bd_d[which, cc, :, :], [k_])
                return b_, k_
            if full:
                for g0 in range(0, 16, 4):
                    pb, kb = K.bank()
                    for jj in range(4):
                        cc = g0 + jj
                        b_, k_ = bdload(0, cc)
                        P.op("pe", lambda e, jj=jj, cc=cc, pb=pb, b_=b_: e.matmul(pb[:, jj * 128:(jj + 1) * 128], lhsT=b_[:], rhs=xcT[:, cc, :], start=True, stop=True),
                             reads=["xcT", k_], writes=[kb])
                    for j in range(2):
                        P.op("act", lambda e, j=j, g0=g0, pb=pb: e.copy(out=qTm[:, j, g0:g0 + 4, j * 64:(j + 1) * 64],
                                                                        in_=pb[:].rearrange("p (a b) -> p a b", b=128)[:, :, j * 64:(j + 1) * 64]),
                             reads=[kb], writes=["qTm"])
            for h in range(4):
                pb, kb = K.bank()
                for jj in range(4):
                    cc = h * 4 + jj
                    b_, k_ = bdload(1, cc)
                    P.op("pe", lambda e, jj=jj, cc=cc, pb=pb, b_=b_: e.matmul(pb[:, jj * 128:(jj + 1) * 128], lhsT=xcT[:, cc, :], rhs=b_[:], start=True, stop=True),
                         reads=["xcT", k_], writes=[kb])
                P.op("dve", lambda e, h=h, pb=pb: e.tensor_scalar(out=kw[:, h * 512:(h + 1) * 512], in0=pb[:], scalar1=wtok[:, h:h + 1], scalar2=float(DH) ** -0.5,
                                                                  op0=ALU.mult, op1=ALU.mult), reads=[kb, "wtok"], writes=["kw"])
            for h in range(4):
                pb, kb = K.bank()
                for jj in range(4):
                    cc = h * 4 + jj
                    b_, k_ = bdload(2, cc)
                    P.op("pe", lambda e, jj=jj, cc=cc, pb=pb, b_=b_: e.matmul(pb[:, jj * 128:(jj + 1) * 128], lhsT=xmT[:, cc, 3:131], rhs=b_[:], start=True, stop=True),
                         reads=["xmT", k_], writes=[kb])
                P.op("act", lambda e, h=h, pb=pb: e.copy(out=vv[:, h * 512:(h + 1) * 512], in_=pb[:]), reads=[kb], writes=["vv"])
            P.op("dve", lambda e: e.tensor_copy(out=xmT[:, :, 0:3], in_=xmT[:, :, 128:131]), reads=["xmT"], writes=["xmT"])
            K.bi = 0
            pnum = [(K.banks[h], ("pb", h)) for h in range(4)]
            pqn, kqn = K.banks[4], ("pb", 4)
            pdn, kdn = K.banks[5], ("pb", 5)
            ci = 0
            for j in range(2):
                tk = slice(j * 64, (j + 1) * 64)
                for h in range(4):
                    abc = bc[:, j * 4 + h:j * 4 + h + 1]
                    for dk in range(4):
                        pC, kC = K.banks[6 + ci % 2], ("pb", 6 + ci % 2)
                        ci += 1
                        P.op("pe", lambda e, h=h, dk=dk, tk=tk, pC=pC: e.matmul(pC[:], lhsT=kw[tk, h * 512 + dk * 128:h * 512 + (dk + 1) * 128],
                                                                             rhs=vv[tk, h * 512:(h + 1) * 512], start=True, stop=True),
                             reads=["kw", "vv"], writes=[kC])
                        P.op("dve", lambda e, h=h, dk=dk, pC=pC, abc=abc: e.scalar_tensor_tensor(
                            out=C[:, h * 4 + dk, :], in0=C[:, h * 4 + dk, :], scalar=abc, in1=pC[:], op0=ALU.mult, op1=ALU.add),
                            reads=[kC, "bc", ("C", h * 4 + dk), "C"], writes=[("C", h * 4 + dk)])
                for h in range(4):
                    for dk in range(4):
                        P.op("pe", lambda e, h=h, dk=dk, tk=tk: e.matmul(pdn[:, h * 4 + dk:h * 4 + dk + 1], lhsT=kw[tk, h * 512 + dk * 128:h * 512 + (dk + 1) * 128],
                                                                        rhs=ones_col[tk, :], start=True, stop=True), reads=["kw", "const"], writes=[kdn])
                for h in range(4):
                    abc = bc[:, j * 4 + h:j * 4 + h + 1]
                    P.op("dve", lambda e, h=h, abc=abc: e.scalar_tensor_tensor(out=nst[:, h * 4:(h + 1) * 4], in0=nst[:, h * 4:(h + 1) * 4], scalar=abc,
                                                                              in1=pdn[:, h * 4:(h + 1) * 4], op0=ALU.mult, op1=ALU.add),
                         reads=[kdn, "bc", "nst"], writes=["nst"])
                if full:
                    for h in range(4):
                        pb_, kb_ = pnum[h]
                        for dk in range(4):
                            P.op("pe", lambda e, h=h, dk=dk, j=j, pb_=pb_: e.matmul(pb_[:], lhsT=qTm[:, j, h * 4 + dk, :], rhs=C[:, h * 4 + dk, :],
                                                                                 start=(j == 0 and dk == 0), stop=(j == 1 and dk == 3)),
                                 reads=["qTm", ("C", h * 4 + dk), "C"], writes=[kb_])
                    for h in range(4):
                        for dk in range(4):
                            P.op("pe", lambda e, h=h, dk=dk, j=j: e.matmul(pqn[:, j * 4 + h:j * 4 + h + 1], lhsT=qTm[:, j, h * 4 + dk, :], rhs=nst[:, h * 4 + dk:h * 4 + dk + 1],
                                                                          start=(dk == 0), stop=(dk == 3)), reads=["qTm", "nst"], writes=[kqn])
            if full:
                P.op("act", lambda e: e.copy(out=qn[:], in_=pqn[:, 0:4]), reads=[kqn], writes=["qn"])
                P.op("dve", lambda e: e.tensor_tensor(out=qn[:], in0=qn[:], in1=pqn[:, 4:8], op=ALU.add), reads=[kqn, "qn"], writes=["qn"])
                P.op("dve", lambda e: e.scalar_tensor_tensor(out=den[:], in0=qn[:], scalar=-1.0, in1=qn[:], op0=ALU.mult, op1=ALU.max), reads=["qn"], writes=["den"])
                P.op("dve", lambda e: e.tensor_tensor(out=den[:], in0=den[:], in1=en[:], op=ALU.max), reads=["den", "en"], writes=["den"])
                P.op("dve", lambda e: e.reciprocal(out=den[:], in_=den[:]), reads=["den"], writes=["den"])
                for h in range(4):
                    pb_, kb_ = pnum[h]
                    P.op("dve", lambda e, h=h, pb_=pb_: e.tensor_scalar(out=hh[:, h * 512:(h + 1) * 512], in0=pb_[:], scalar1=den[:, h:h + 1], scalar2=None, op0=ALU.mult),
                         reads=[kb_, "den"], writes=[("hh", h)])
                    P.op("dve", lambda e, h=h: e.bn_stats(out=stats[:, h, :], in_=hh[:, h * 512:(h + 1) * 512]), reads=[("hh", h)], writes=[("stats", h)])
                    P.op("dve", lambda e, h=h: e.bn_aggr(out=mv[:, h, :], in_=stats[:, h, :]), reads=[("stats", h)], writes=[("mv", h)])
                P.op("act", lambda e: e.activation(out=rs4[:], in_=mv[:, :, 1], func=AF.Sqrt, bias=EPS), reads=[("mv", h) for h in range(4)], writes=["rs4"])
                P.op("dve", lambda e: e.reciprocal(out=rs4[:], in_=rs4[:]), reads=["rs4"], writes=["rs4"])
                for h in range(4):
                    P.op("dve", lambda e, h=h: e.tensor_scalar(out=hh[:, h * 512:(h + 1) * 512], in0=hh[:, h * 512:(h + 1) * 512], scalar1=mv[:, h, 0:1],
                                                               scalar2=rs4[:, h:h + 1], op0=ALU.subtract, op1=ALU.mult),
                         reads=[("hh", h), ("mv", h), "rs4"], writes=[("hh", h)])
                for g0 in range(0, 16, 4):
                    pb, kb = K.bank()
                    for jj in range(4):
                        cc = g0 + jj
                        P.op("pe", lambda e, jj=jj, cc=cc, pb=pb: e.transpose(out=pb[:, jj * 128:(jj + 1) * 128], in_=hh[:, cc * 128:(cc + 1) * 128], identity=ident),
                             reads=[("hh", cc // 4), "const"], writes=[kb])
                    gsl = slice(g0, g0 + 4)
                    P.op("dve", lambda e, gsl=gsl, pb=pb: e.tensor_tensor(out=fin[:, gsl, :], in0=pb[:].rearrange("p (a b) -> p a b", b=128),
                                                                           in1=ghd[:, gsl].unsqueeze(2).broadcast_to([128, 4, 128]), op=ALU.mult),
                         reads=[kb, "const"], writes=[("fin", g0)])
                    P.op("dve", lambda e, gsl=gsl: e.tensor_tensor(out=tmpc[:, gsl, :], in0=xcT[:, gsl, :], in1=skp[:, gsl].unsqueeze(2).broadcast_to([128, 4, 128]), op=ALU.mult),
                         reads=["xcT", "const"], writes=["tmpc"])
                    P.op("dve", lambda e, gsl=gsl: e.tensor_tensor(out=fin[:, gsl, :], in0=fin[:, gsl, :], in1=tmpc[:, gsl, :], op=ALU.add),
                         reads=[("fin", g0), "tmpc"], writes=[("fin", g0)])
                    P.op("dve", lambda e, gsl=gsl: e.tensor_tensor(out=fin[:, gsl, :], in0=fin[:, gsl, :], in1=szT[:, gsl, :], op=ALU.mult),
                         reads=[("fin", g0), "szT"], writes=[("fin", g0)])
                pm0, km0 = K.bank()
                pm1, km1 = K.bank()
                for cc in range(16):
                    wb = woc[nwo[0] % 3]
                    kwo = "woc%d" % (nwo[0] % 3)
                    nwo[0] += 1
                    K.load(wb[:], wout_d[cc * 128:(cc + 1) * 128, :], [kwo])
                    P.op("pe", lambda e, cc=cc, wb=wb: e.matmul(pm0[:], lhsT=fin[:, cc, :], rhs=wb[:, 0:512], start=(cc == 0), stop=(cc == 15)),
                         reads=[("fin", (cc // 4) * 4), kwo], writes=[km0])
                    P.op("pe", lambda e, cc=cc, wb=wb: e.matmul(pm1[:], lhsT=fin[:, cc, :], rhs=wb[:, 512:1024], start=(cc == 0), stop=(cc == 15)),
                         reads=[("fin", (cc // 4) * 4), kwo], writes=[km1])
                P.op("dve", lambda e: e.tensor_tensor(out=yo[:, 0:512], in0=pm0[:], in1=xt[:, 0:512], op=ALU.add), reads=[km0, "xt"], writes=["yo"])
                P.op("dve", lambda e: e.tensor_tensor(out=yo[:, 512:1024], in0=pm1[:], in1=xt[:, 512:1024], op=ALU.add), reads=[km1, "xt"], writes=["yo"])
                P.dma("sp", lambda e, rows=rows: e.dma_start(out=y_d[rows, :], in_=yo[:]), reads=["yo"], writes=[(ykey, slot)])
        P.barrier()
        P.emit(block)


def mlstm_inputs(x_rows, g, win, b_i, b_f, conv_w, conv_b, w_q, w_k, w_v, skip, gh, wout):
    winl = np.ascontiguousarray(win[:, 0:4096].reshape(8, 128, 32, 128).transpose(2, 1, 0, 3)).reshape(32, 128, 1024)
    wg = np.ascontiguousarray(win[:, 4096:4104].reshape(8, 128, 8).transpose(1, 0, 2)).reshape(128, 64)
    bif = np.concatenate([b_i.reshape(-1), b_f.reshape(-1)]).reshape(1, 8).astype(np.float32)
    cw = np.ascontiguousarray(conv_w.reshape(4, 16, 128).transpose(2, 0, 1)).reshape(128, 64)
    cb = np.ascontiguousarray(conv_b.reshape(16, 128).T)
    bd = np.zeros((3, 16, 128, 128), np.float32)
    for a, w_ in enumerate((w_q, w_k, w_v)):
        wr = w_.reshape(16, 32, 4, 4)
        for n in range(32):
            bd[a, :, n * 4:(n + 1) * 4, n * 4:(n + 1) * 4] = wr[:, n]
    return {"x": np.ascontiguousarray(x_rows), "g": g.reshape(1, D), "winl": winl, "wg": wg, "bif": bif, "cw": cw, "cb": cb, "bd": bd,
            "skip": np.ascontiguousarray(skip.reshape(16, 128).T), "gh": np.ascontiguousarray(gh.reshape(16, 128).T), "wout": wout,
            "consts": _consts()}


_IO = {
    "peer": [("g", [1, D]), ("gf", [1, D]), ("wq", [D, 2048]), ("skT", [128, 2048]), ("uv", [16384, 2 * D])],
    "gla": [("g", [1, D]), ("win", [D, 3088]), ("wgu", [16, 512]), ("bg", [1, 512]), ("gh", [1, D]), ("wout", [D, D])],
    "mlstm": [("g", [1, D]), ("winl", [32, 128, 1024]), ("wg", [128, 64]), ("bif", [1, 8]), ("cw", [128, 64]), ("cb", [128, 16]),
              ("bd", [3, 16, 128, 128]), ("skip", [128, 16]), ("gh", [128, 16]), ("wout", [2048, D])],
}
_EMIT = {"peer": emit_peer, "gla": emit_gla, "mlstm": emit_mlstm}


def build_program(NT, phases, split_last=False):
    nc = bass.Bass("TRN2", target_bir_lowering=False)
    dr = lambda n, s, kind="ExternalInput": nc.dram_tensor(n, s, F32, kind=kind).ap()
    x_d = dr("x", [NT * 128, D])
    c_d = dr("consts", [128, 768])
    NTO = NT // 2 if split_last else NT
    y_d = dr("y", [NTO * 128, D], kind="ExternalOutput")
    scratch = [dr("xa", [NT * 128, D], kind="Internal"), dr("xb", [NT * 128, D], kind="Internal")] if len(phases) > 1 else []
    ios = []
    for kind, pfx, extra in phases:
        io = {n: dr(pfx + n, shp) for n, shp in _IO[kind]}
        io["consts"] = c_d
        ios.append(io)
    if split_last:
        assert phases[-2][0] == "mlstm" and phases[-1][0] == "peer"
        ios[-2]["rowsel"] = nc.dram_tensor("rowsel", [128, NT // 2], I32, kind="ExternalInput").ap()
        ios[-2]["flag"] = nc.dram_tensor("flag", [128, 1], F32, kind="ExternalInput").ap()
    with ExitStack() as st0:
        K = Ctx(nc, st0)
        block = st0.enter_context(nc.Block())
        src, skey = x_d, "x_in"
        for i, (kind, pfx, extra) in enumerate(phases):
            last = i == len(phases) - 1
            dst, dkey = (y_d, "y_out") if last else (scratch[i % 2], "xs%d" % i)
            ios[i]["x"] = src
            ios[i]["y"] = dst
            if split_last and i == len(phases) - 2:
                extra = extra + (True,)
            _EMIT[kind](K, NTO if (last and split_last) else NT, ios[i], pfx, skey, dkey, block, *extra)
            src, skey = dst, dkey
        K.P.wait_all("sp", [("y_out", it) for it in range(NTO)])
        K.P.emit(block)
    return nc


def build_peer(NT, final):
    return build_program(NT, [("peer", "", (final,))])


def build_gla(NT):
    return build_program(NT, [("gla", "", ())])


def build_mlstm(NT):
    return build_program(NT, [("mlstm", "", ())])


_PHASES = [("gla", "g0_", ()), ("peer", "p0_", (False,)), ("mlstm", "m1_", ()), ("peer", "p1_", (True,))]
_NC_CACHE = {}


def kernel(x, norm_mix_g, gla_w_in, gla_w_gate_up, gla_b_gate, gla_g_head, gla_w_out,
           mlstm_w_in, mlstm_b_i, mlstm_b_f, mlstm_conv_w, mlstm_conv_b, mlstm_w_q, mlstm_w_k,
           mlstm_w_v, mlstm_skip, mlstm_g_head, mlstm_w_out, norm_ffn_g, peer_w_query,
           peer_sub_keys, peer_u, peer_v, norm_final_g):
    f = lambda a: np.ascontiguousarray(np.asarray(a, dtype=np.float32))
    x = f(x)
    B, S, _ = x.shape
    NT = S // 128
    dummy = np.zeros((1, D), np.float32)
    parts = [
        ("g0_", gla_inputs(dummy, f(norm_mix_g)[0], f(gla_w_in)[0], f(gla_w_gate_up)[0], f(gla_b_gate)[0], f(gla_g_head)[0], f(gla_w_out)[0])),
        ("p0_", peer_inputs(dummy, f(norm_ffn_g)[0], f(norm_final_g), f(peer_w_query)[0], f(peer_sub_keys)[0], f(peer_u)[0], f(peer_v)[0])),
        ("m1_", mlstm_inputs(dummy, f(norm_mix_g)[1], f(mlstm_w_in)[0], f(mlstm_b_i)[0], f(mlstm_b_f)[0], f(mlstm_conv_w)[0], f(mlstm_conv_b)[0],
                             f(mlstm_w_q)[0], f(mlstm_w_k)[0], f(mlstm_w_v)[0], f(mlstm_skip)[0], f(mlstm_g_head)[0], f(mlstm_w_out)[0])),
        ("p1_", peer_inputs(dummy, f(norm_ffn_g)[1], f(norm_final_g), f(peer_w_query)[1], f(peer_sub_keys)[1], f(peer_u)[1], f(peer_v)[1])),
    ]
    shared = {"consts": _consts()}
    for pfx, d in parts:
        for k, v in d.items():
            if k not in ("x", "consts"):
                shared[pfx + k] = v
    def rowsel(half):
        t = np.arange(NT // 2)[None, :] + half * (NT // 2)
        return np.ascontiguousarray((t * 128 + np.arange(128)[:, None]).astype(np.int32))
    in_maps = [dict(shared, x=x[c % B], rowsel=rowsel(c // B), flag=np.full((128, 1), float(c // B), np.float32)) for c in range(8)]
    if NT not in _NC_CACHE:
        _NC_CACHE[NT] = build_program(NT, _PHASES, split_last=True)
    res = run_bass_kernel_spmd(_NC_CACHE[NT], in_maps, core_ids=list(range(8)))
    out = np.empty((B, S, D), np.float32)
    h = S // 2
    for c in range(8):
        out[c % B, (c // B) * h:(c // B + 1) * h] = res.results[c]["y"]
    return out
```

```python
from contextlib import ExitStack
import numpy as np
import concourse.bass as bass
import concourse.mybir as mybir
from concourse.bass_utils import run_bass_kernel_spmd

F32 = mybir.dt.float32
U32 = mybir.dt.uint32
I32 = mybir.dt.int32
ALU = mybir.AluOpType
AF = mybir.ActivationFunctionType
AX = mybir.AxisListType

D = 1024
EPS = 1e-6
ENGS = ("pe", "dve", "act", "pool", "sp")
NEG = -1.0e30


class Prog:
    def __init__(self, nc, stack, n_dma_sems=48):
        self.nc = nc
        self.q = {e: [] for e in ENGS}
        self.sem = {e: stack.enter_context(nc.semaphore("s_" + e)) for e in ENGS}
        self.cnt = {e: 0 for e in ENGS}
        self.dsem = [stack.enter_context(nc.semaphore("d%d" % i)) for i in range(n_dma_sems)]
        self.dcnt = [0] * n_dma_sems
        self.dpool = {"sp": list(range(0, 16)), "act": list(range(0, 16)), "pool": list(range(16, n_dma_sems))}
        self.dnext = {"sp": 0, "act": 0, "pool": 0}
        self.w = {}
        self.r = {}
        self.waited = {e: {} for e in ENGS}
        self.semobj = {}
        for e in ENGS:
            self.semobj[("e", e)] = self.sem[e]
        for i, s in enumerate(self.dsem):
            self.semobj[("d", i)] = s

    def _wait(self, eng, ev):
        sid, val = ev
        if sid == ("e", eng) and eng == "pe":
            return
        if self.waited[eng].get(sid, 0) >= val:
            return
        self.waited[eng][sid] = val
        s = self.semobj[sid]
        self.q[eng].append(lambda e, s=s, val=val: e.wait_ge(s, val))

    def _deps(self, eng, reads, writes):
        for k in reads:
            ev = self.w.get(k)
            if ev is not None:
                self._wait(eng, ev)
        for k in writes:
            ev = self.w.get(k)
            if ev is not None:
                self._wait(eng, ev)
            for ev in self.r.get(k, ()):
                self._wait(eng, ev)

    def _record(self, ev, reads, writes):
        for k in reads:
            if k not in writes:
                self.r.setdefault(k, []).append(ev)
        for k in writes:
            self.w[k] = ev
            self.r[k] = []

    def op(self, eng, fn, reads=(), writes=()):
        self._deps(eng, reads, writes)
        self.cnt[eng] += 1
        s = self.sem[eng]
        self.q[eng].append(lambda e, fn=fn, s=s: fn(e).then_inc(s, 1))
        ev = (("e", eng), self.cnt[eng])
        self._record(ev, reads, writes)
        return ev

    def dma(self, eng, fn, reads=(), writes=()):
        pool_ = self.dpool[eng]
        key_ = "pool" if eng == "pool" else "sp"
        i = pool_[self.dnext[key_] % len(pool_)]
        self.dnext[key_] += 1
        if self.dcnt[i] > 0:
            self._wait(eng, (("d", i), 16 * self.dcnt[i]))
        self._deps(eng, reads, writes)
        self.dcnt[i] += 1
        s = self.dsem[i]
        self.q[eng].append(lambda e, fn=fn, s=s: fn(e).then_inc(s, 16))
        ev = (("d", i), 16 * self.dcnt[i])
        self._record(ev, reads, writes)
        return ev

    def wait_all(self, eng, keys):
        for k in keys:
            ev = self.w.get(k)
            if ev is not None:
                self._wait(eng, ev)

    def barrier(self):
        for eng in ENGS:
            for f_ in ENGS:
                if f_ != eng and self.cnt[f_] > 0:
                    self._wait(eng, (("e", f_), self.cnt[f_]))
            for i, c in enumerate(self.dcnt):
                if c > 0:
                    self._wait(eng, (("d", i), 16 * c))

    def emit(self, block):
        q = self.q
        self.q = {e: [] for e in ENGS}

        @block.tensor
        def _(e):
            for f in q["pe"]:
                f(e)

        @block.vector
        def _(e):
            for f in q["dve"]:
                f(e)

        @block.scalar
        def _(e):
            for f in q["act"]:
                f(e)

        @block.gpsimd
        def _(e):
            for f in q["pool"]:
                f(e)

        @block.sync
        def _(e):
            for f in q["sp"]:
                f(e)


class Ctx:
    def __init__(self, nc, st):
        self.nc = nc
        self.st = st
        self.P = Prog(nc, st)
        self.banks = [st.enter_context(nc.psum_tensor("pb%d" % i, [128, 512], F32)) for i in range(8)]
        self.bi = 0
        self.evi = 0

    def sb(self, name, shape, dt=F32):
        return self.st.enter_context(self.nc.sbuf_tensor(name, shape, dt))

    def bank(self):
        b = self.bi
        self.bi = (self.bi + 1) % 8
        return self.banks[b], ("pb", b)

    def load(self, dst_ap, src_ap, wkeys, eng="sp", reads=()):
        self.P.dma(eng, lambda e: e.dma_start(out=dst_ap, in_=src_ap), reads=reads, writes=wkeys)

    def rmsnorm(self, xt, gb, xn, sq, ms, rstd, kx, kxn, tag):
        P = self.P
        P.op("act", lambda e: e.activation(out=sq[:], in_=xt[:], func=AF.Square, accum_out=ms[:]),
             reads=[kx], writes=["sq" + tag, "ms" + tag])
        P.op("act", lambda e: e.activation(out=rstd[:], in_=ms[:], func=AF.Sqrt, scale=1.0 / D, bias=EPS),
             reads=["ms" + tag], writes=["rstd" + tag])
        P.op("dve", lambda e: e.reciprocal(out=rstd[:], in_=rstd[:]), reads=["rstd" + tag], writes=["rstd" + tag])
        P.op("dve", lambda e: e.scalar_tensor_tensor(out=xn[:], in0=xt[:], scalar=rstd[:, 0:1], in1=gb[:],
                                                     op0=ALU.mult, op1=ALU.mult),
             reads=[kx, "rstd" + tag, "const"], writes=[kxn])

    def transpose_cols(self, src, ksrc, dst, kdst, nchunk, ident, np_in=128):
        P = self.P
        for g0 in range(0, nchunk, 4):
            n = min(4, nchunk - g0)
            pb, kb = self.bank()
            for j in range(n):
                c = g0 + j
                P.op("pe", lambda e, c=c, j=j, pb=pb: e.transpose(out=pb[:, j * 128:j * 128 + np_in],
                                                                   in_=src[0:np_in, c * 128:(c + 1) * 128],
                                                                   identity=ident[0:np_in, 0:np_in]),
                     reads=[ksrc, "const"], writes=[kb])
            self.evi += 1
            if np_in == 128:
                o = dst[:, g0:g0 + n, :]
                i_ = pb[:, 0:n * 128].rearrange("p (a b) -> p a b", b=128)
            else:
                o = dst[:, g0:g0 + n, 0:np_in]
                i_ = pb[:, 0:n * 128].rearrange("p (a b) -> p a b", b=128)[:, :, 0:np_in]
            if self.evi % 2 == 0:
                P.op("act", lambda e, o=o, i_=i_: e.copy(out=o, in_=i_), reads=[kb], writes=[kdst])
            else:
                P.op("dve", lambda e, o=o, i_=i_: e.tensor_copy(out=o, in_=i_), reads=[kb], writes=[kdst])


def _consts():
    c = np.zeros((128, 768), np.float32)
    c[:, 0:128] = np.eye(128, dtype=np.float32)
    s = np.arange(128)
    same = (s[:, None] // 64) == (s[None, :] // 64)
    c[:, 128:256] = (same & (s[:, None] > s[None, :])).astype(np.float32)
    c[:, 256] = (s < 64)
    c[:, 257] = (s >= 64)
    c[:, 258] = 1.0
    c[:, 272:288] = np.arange(16, dtype=np.float32)[None, :]
    c[0:4, 384:512] = 1.0
    c[:, 512:640] = (same & (s[:, None] <= s[None, :])).astype(np.float32)
    return c


def emit_peer(K, NT, io, pfx, xkey, ykey, block, final, NBUF=5):
    nc = K.nc
    x_d = io["x"]
    g_d = io["g"]
    gf_d = io["gf"]
    wq_d = io["wq"]
    sk_d = io["skT"]
    uv_d = io["uv"]
    c_d = io["consts"]
    y_d = io["y"]
    with ExitStack() as st:
        K.st = st
        P = K.P
        sb = lambda n_, s_, dt_=F32: K.sb(pfx + n_, s_, dt_)
        wq = sb("wq_s", [128, 8, 2048])
        skT = sb("skT_s", [128, 16, 128])
        cst = sb("cst", [128, 768])
        gb = sb("gb", [128, D])
        gfb = sb("gfb", [128, D if final else 1])
        xts = [sb("xt%d" % i, [128, D]) for i in range(3)]
        xns = [sb("xn%d" % i, [128, D]) for i in range(2)]
        idxs = [sb("idx%d" % i, [128, 128], I32) for i in range(3)]
        wgts = [sb("wgt%d" % i, [128, 128]) for i in range(2)]
        ms = sb("ms", [128, 1])
        rstd = sb("rstd", [128, 1])
        xnT = sb("xnT", [128, 8, 128])
        qT = sb("qT", [128, 16, 128])
        sc = sb("sc", [128, 16, 128])
        sc2 = sb("sc2", [128, 128])
        stop = sb("stop", [128, 16, 16])
        itop = sb("itop", [128, 16, 16], U32)
        itopf = sb("itopf", [128, 16, 16])
        cand = sb("cand", [128, 8, 256])
        cand2 = sb("cand2", [128, 256])
        best = sb("best", [128, 8, 16])
        pos = sb("pos", [128, 8, 16], U32)
        pa = sb("pa", [128, 8, 16], U32)
        pbb = sb("pbb", [128, 8, 16], U32)
        paf = sb("paf", [128, 8, 16])
        pbf = sb("pbf", [128, 8, 16])
        isel = sb("isel", [128, 8, 16])
        jsel = sb("jsel", [128, 8, 16])
        idxf = sb("idxf", [128, 128])
        gats = [sb("gat%d" % i, [128, 8, 16]) for i in range(2)]
        gsum = sb("gsum", [128, 8])
        apre = sb("apre", [128, 128])
        uvb = [sb("uvb%d" % i, [128, 2 * D]) for i in range(NBUF)]
        gl = sb("gl", [128, 128])
        NDG = 8
        dg = [sb("dg%d" % i, [128, 128]) for i in range(NDG)]
        junks = [sb("jnk%d" % i, [128, D]) for i in range(2)]
        junk2 = sb("junk2", [128, D])
        yo = sb("yo", [128, D])
        ms2 = sb("ms2", [128, 1])
        rstd2 = sb("rstd2", [128, 1])

        ident = cst[:, 0:128]
        iota16 = cst[:, 272:288]
        rs_d = io.get("rowsel")
        if rs_d is not None:
            rsel = sb("rsel", [128, NT], I32)
            K.load(rsel[:], rs_d, ["const"])
        ob_d = io.get("oob")
        if ob_d is not None:
            oobt = sb("oobt", [128, NT])
            K.load(oobt[:], ob_d, ["const"])

        K.load(cst[:], c_d, ["const"])
        K.load(gb[:], g_d[0:1, :].broadcast_to([128, D]), ["const"])
        if final:
            K.load(gfb[:], gf_d[0:1, :].broadcast_to([128, D]), ["const"])
        K.load(skT[:].rearrange("p a b -> p (a b)"), sk_d, ["const"])
        for c in range(8):
            K.load(wq[:, c, :], wq_d[c * 128:(c + 1) * 128, :], ["const"])

        oobreg = []

        def oobkw_(e):
            if ob_d is None:
                return {}
            if not oobreg:
                oobreg.append(e.to_reg(16383))
            return {"bounds_check": oobreg[0], "oob_is_err": False}
        abank = [0]

        def bankA():
            b = abank[0]
            abank[0] = (abank[0] + 1) % 6
            return K.banks[b], ("pb", b)

        def top16(src_ap, ksrc, scratch_ap, kscr, vals_ap, idx_ap, kout):
            P.op("dve", lambda e: e.max(out=vals_ap[:, 0:8], in_=src_ap), reads=[ksrc], writes=[kout])
            P.op("dve", lambda e: e.max_index(out=idx_ap[:, 0:8], in_max=vals_ap[:, 0:8], in_values=src_ap),
                 reads=[ksrc, kout], writes=[kout + "i"])
            P.op("dve", lambda e: e.match_replace(out=scratch_ap, in_to_replace=vals_ap[:, 0:8], in_values=src_ap,
                                                  imm_value=NEG), reads=[ksrc, kout], writes=[kscr])
            P.op("dve", lambda e: e.max(out=vals_ap[:, 8:16], in_=scratch_ap), reads=[kscr], writes=[kout])
            P.op("dve", lambda e: e.max_index(out=idx_ap[:, 8:16], in_max=vals_ap[:, 8:16], in_values=scratch_ap),
                 reads=[kscr, kout], writes=[kout + "i"])

        def stageA(it):
            xt = xts[it % 3]
            kxt = "xt%d" % (it % 3)
            xn = xns[it % 2]
            kxn = "xn%d" % (it % 2)
            idx = idxs[it % 3]
            kidx = "idx%d" % (it % 3)
            gat = gats[it % 2]
            kgat = "gat%d" % (it % 2)
            rows = slice(it * 128, (it + 1) * 128)
            if rs_d is None:
                K.load(xt[:], x_d[rows, :], [kxt], reads=[(xkey, it)])
            else:
                P.dma("pool", lambda e, it=it: e.indirect_dma_start(
                    out=xt[:, :], out_offset=None, in_=x_d[:, :],
                    in_offset=bass.IndirectOffsetOnAxis(ap=rsel[:, it:it + 1], axis=0)),
                    reads=["const"] + [(xkey, j_) for j_ in range(x_d.shape[0] // 128)] if it == 0 else ["const"], writes=[kxt])
            yield
            K.rmsnorm(xt, gb, xn, junk2, ms, rstd, kxt, kxn, "A")
            yield
            for g0 in range(0, 8, 4):
                pb, kb = bankA()
                for j in range(4):
                    c = g0 + j
                    P.op("pe", lambda e, c=c, j=j, pb=pb: e.transpose(out=pb[:, j * 128:(j + 1) * 128], in_=xn[:, c * 128:(c + 1) * 128],
                                                                       identity=ident), reads=[kxn, "const"], writes=[kb])
                P.op("act", lambda e, g0=g0, pb=pb: e.copy(out=xnT[:, g0:g0 + 4, :], in_=pb[:].rearrange("p (a b) -> p a b", b=128)),
                     reads=[kb], writes=["xnT"])
                yield
            for g0 in range(0, 16, 4):
                pb, kb = bankA()
                for j in range(4):
                    g = g0 + j
                    for c in range(8):
                        P.op("pe", lambda e, g=g, j=j, c=c, pb=pb: e.matmul(
                            pb[:, j * 128:(j + 1) * 128], lhsT=wq[:, c, g * 128:(g + 1) * 128], rhs=xnT[:, c, :],
                            start=(c == 0), stop=(c == 7)), reads=["const", "xnT"], writes=[kb])
                    yield
                P.op("act", lambda e, g0=g0, pb=pb: e.copy(out=qT[:, g0:g0 + 4, :].rearrange("p a b -> p (a b)"), in_=pb[:]),
                     reads=[kb], writes=["qT"])
            for g0 in range(0, 16, 4):
                pb, kb = bankA()
                for j in range(4):
                    g = g0 + j
                    P.op("pe", lambda e, g=g, j=j, pb=pb: e.matmul(
                        pb[:, j * 128:(j + 1) * 128], lhsT=qT[:, g, :], rhs=skT[:, g, :], start=True, stop=True),
                        reads=["qT", "const"], writes=[kb])
                P.op("act", lambda e, g0=g0, pb=pb: e.copy(out=sc[:, g0:g0 + 4, :].rearrange("p a b -> p (a b)"), in_=pb[:]),
                     reads=[kb], writes=["sc"])
                yield
            for g in range(16):
                top16(sc[:, g, :], "sc", sc2[:], "sc2", stop[:, g, :], itop[:, g, :], "stop")
                yield
            P.op("dve", lambda e: e.tensor_copy(out=itopf[:], in_=itop[:]), reads=["stopi"], writes=["itopf"])
            s4 = stop[:].rearrange("p (h two) k -> p h two k", two=2)
            i4 = itopf[:].rearrange("p (h two) k -> p h two k", two=2)
            P.op("dve", lambda e: e.tensor_tensor(
                out=cand[:].rearrange("p h (a b) -> p h a b", b=16),
                in0=s4[:, :, 0, :].unsqueeze(3).broadcast_to([128, 8, 16, 16]),
                in1=s4[:, :, 1, :].unsqueeze(2).broadcast_to([128, 8, 16, 16]), op=ALU.add),
                reads=["stop"], writes=["cand"])
            yield
            for h in range(8):
                top16(cand[:, h, :], "cand", cand2[:], "cand2", best[:, h, :], pos[:, h, :], "best")
                yield
            P.op("dve", lambda e: e.tensor_single_scalar(out=pa[:], in_=pos[:], scalar=4, op=ALU.logical_shift_right),
                 reads=["besti"], writes=["pa"])
            P.op("dve", lambda e: e.tensor_single_scalar(out=pbb[:], in_=pos[:], scalar=15, op=ALU.bitwise_and),
                 reads=["besti"], writes=["pbb"])
            P.op("dve", lambda e: e.tensor_copy(out=paf[:], in_=pa[:]), reads=["pa"], writes=["paf"])
            P.op("dve", lambda e: e.tensor_copy(out=pbf[:], in_=pbb[:]), reads=["pbb"], writes=["pbf"])
            yield
            io4 = iota16.unsqueeze(1).unsqueeze(1).broadcast_to([128, 8, 16, 16])
            eq = cand[:].rearrange("p h (a b) -> p h a b", b=16)
            for (pf, kpf, two, sel, ksel) in ((paf, "paf", 0, isel, "isel"), (pbf, "pbf", 1, jsel, "jsel")):
                P.op("dve", lambda e, pf=pf: e.tensor_tensor(out=eq, in0=pf[:].unsqueeze(3).broadcast_to([128, 8, 16, 16]),
                                                             in1=io4, op=ALU.is_equal), reads=[kpf, "const"], writes=["cand"])
                P.op("dve", lambda e, two=two: e.tensor_tensor(out=eq, in0=eq,
                                                               in1=i4[:, :, two, :].unsqueeze(2).broadcast_to([128, 8, 16, 16]),
                                                               op=ALU.mult), reads=["cand", "itopf"], writes=["cand"])
                P.op("dve", lambda e, sel=sel: e.tensor_reduce(out=sel[:], in_=eq, axis=AX.X, op=ALU.add),
                     reads=["cand"], writes=[ksel])
                yield
            P.op("dve", lambda e: e.scalar_tensor_tensor(out=idxf[:], in0=isel[:].rearrange("p h k -> p (h k)"), scalar=128.0,
                                                         in1=jsel[:].rearrange("p h k -> p (h k)"), op0=ALU.mult, op1=ALU.add),
                 reads=["isel", "jsel"], writes=["idxf"])
            if ob_d is not None:
                P.op("dve", lambda e, it=it: e.tensor_scalar(out=idxf[:], in0=idxf[:], scalar1=oobt[:, it:it + 1], scalar2=None, op0=ALU.add),
                     reads=["idxf", "const"], writes=["idxf"])
            P.op("dve", lambda e: e.tensor_copy(out=idx[:], in_=idxf[:]), reads=["idxf"], writes=[kidx])
            yield
            P.op("dve", lambda e: e.tensor_tensor(out=gat[:], in0=best[:], in1=best[:, :, 0:1].broadcast_to([128, 8, 16]),
                                                  op=ALU.subtract), reads=["best"], writes=[kgat])
            P.op("act", lambda e: e.activation(out=gat[:], in_=gat[:], func=AF.Exp), reads=[kgat], writes=[kgat])
            P.op("dve", lambda e: e.tensor_reduce(out=gsum[:], in_=gat[:], axis=AX.X, op=ALU.add), reads=[kgat], writes=["gsum"])
            P.op("dve", lambda e: e.reciprocal(out=gsum[:], in_=gsum[:]), reads=["gsum"], writes=["gsum"])
            P.op("dve", lambda e: e.tensor_tensor(out=gat[:], in0=gat[:], in1=gsum[:].unsqueeze(2).broadcast_to([128, 8, 16]),
                                                  op=ALU.mult), reads=[kgat, "gsum"], writes=[kgat])
            yield

        def stageBC(it):
            xt = xts[it % 3]
            kxt = "xt%d" % (it % 3)
            xn = xns[it % 2]
            kxn = "xn%d" % (it % 2)
            idx = idxs[it % 3]
            kidx = "idx%d" % (it % 3)
            gat = gats[it % 2]
            kgat = "gat%d" % (it % 2)
            gat2 = gat[:].rearrange("p h k -> p (h k)")
            py0, ky0 = K.banks[6], ("pb", 6)
            py1, ky1 = K.banks[7], ("pb", 7)
            for s in range(128):
                b = uvb[s % NBUF]
                kb_ = "uvb%d" % (s % NBUF)
                d_ = dg[s % NDG]
                kd = "dg%d" % (s % NDG)
                P.dma("pool", lambda e, b=b, s=s: e.indirect_dma_start(
                    out=b[:, :], out_offset=None, in_=uv_d[:, :],
                    in_offset=bass.IndirectOffsetOnAxis(ap=idx[:, s:s + 1], axis=0), **oobkw_(e)), reads=[kidx], writes=[kb_])
                P.op("dve", lambda e, b=b, s=s: e.scalar_tensor_tensor(
                    out=junks[s % 2][:], in0=b[:, 0:D], scalar=1.0, in1=xn[:], op0=ALU.mult, op1=ALU.mult, accum_out=apre[:, s:s + 1]),
                    reads=[kb_, kxn], writes=[("apre", s), "junk%d" % (s % 2)])
                P.op("act", lambda e, s=s: e.activation(out=gl[:, s:s + 1], in_=apre[:, s:s + 1], func=AF.Gelu),
                     reads=[("apre", s)], writes=[("gl", s)])
                P.op("dve", lambda e, d_=d_, s=s: e.tensor_scalar(out=d_[:], in0=ident, scalar1=gl[:, s:s + 1], scalar2=gat2[:, s:s + 1],
                                                                  op0=ALU.mult, op1=ALU.mult),
                     reads=[("gl", s), kgat, "const"], writes=[kd])
                P.op("pe", lambda e, d_=d_, b=b, s=s: e.matmul(py0[:], lhsT=d_[:], rhs=b[:, D:D + 512], start=(s == 0), stop=(s == 127)),
                     reads=[kd, kb_], writes=[ky0])
                P.op("pe", lambda e, d_=d_, b=b, s=s: e.matmul(py1[:], lhsT=d_[:], rhs=b[:, D + 512:2 * D], start=(s == 0), stop=(s == 127)),
                     reads=[kd, kb_], writes=[ky1])
                yield
            P.op("dve", lambda e: e.tensor_tensor(out=yo[:, 0:512], in0=py0[:], in1=xt[:, 0:512], op=ALU.add),
                 reads=[ky0, kxt], writes=["yo"])
            P.op("dve", lambda e: e.tensor_tensor(out=yo[:, 512:1024], in0=py1[:], in1=xt[:, 512:1024], op=ALU.add),
                 reads=[ky1, kxt], writes=["yo"])
            if final:
                K.rmsnorm(yo, gfb, yo, junk2, ms2, rstd2, "yo", "yo", "C")
            rows = slice(it * 128, (it + 1) * 128)
            P.dma("sp", lambda e, rows=rows: e.dma_start(out=y_d[rows, :], in_=yo[:]), reads=["yo"], writes=[(ykey, it)])
            yield

        def drain(gens, weights):
            alive = [g is not None for g in gens]
            while any(alive):
                for gi, g in enumerate(gens):
                    if not alive[gi]:
                        continue
                    for _ in range(weights[gi]):
                        try:
                            next(g)
                        except StopIteration:
                            alive[gi] = False
                            break

        for k in range(-1, NT):
            gens = [stageBC(k) if 0 <= k < NT else None,
                    stageA(k + 1) if 0 <= k + 1 < NT else None]
            drain(gens, [2, 1])
        P.barrier()
        P.emit(block)


def peer_inputs(x_rows, g, gf, wq, sub_keys, u, v):
    skT = np.ascontiguousarray(sub_keys.reshape(16, 128, 128).transpose(2, 0, 1)).reshape(128, 2048)
    return {"x": np.ascontiguousarray(x_rows), "g": g.reshape(1, D), "gf": gf.reshape(1, D), "wq": wq, "skT": skT,
            "uv": np.concatenate([u, v], axis=1), "consts": _consts()}


def emit_gla(K, NT, io, pfx, xkey, ykey, block):
    nc = K.nc
    x_d = io["x"]
    g_d = io["g"]
    win_d = io["win"]
    wgu_d = io["wgu"]
    bg_d = io["bg"]
    gh_d = io["gh"]
    wout_d = io["wout"]
    c_d = io["consts"]
    y_d = io["y"]
    with ExitStack() as st:
        K.st = st
        P = K.P
        sb = lambda n_, s_, dt_=F32: K.sb(pfx + n_, s_, dt_)
        win = sb("win_s", [128, 8, 3088])
        wout = sb("wout_s", [128, 8, D])
        wgu = sb("wgu_s", [16, 512])
        bg = sb("bg_s", [1, 512])
        cst = sb("cst", [128, 768])
        gb = sb("gb", [128, D])
        ghb = sb("ghb", [128, D])
        xt = sb("xt", [128, D])
        sq = sb("sq", [128, D])
        ms = sb("ms", [128, 1])
        rstd = sb("rstd", [128, 1])
        xn = sb("xn", [128, D])
        xnT = sb("xnT", [128, 8, 128])
        glowT = sb("glowT", [16, 128])
        ta = sb("ta", [128, 512])
        ls = sb("ls", [128, 512])
        expD = sb("expD", [128, 512])
        etot = sb("etot", [128, 8])
        kdec = sb("kdec", [128, 512])
        vv = sb("vv", [128, D])
        gsil = sb("gsil", [128, D])
        qTm = sb("qTm", [128, 2, 512])
        S = sb("S", [128, 4, 256])
        ss = sb("ss", [128, 4])
        og = sb("og", [128, D])
        ogT = sb("ogT", [128, 8, 128])
        yo = sb("yo", [128, D])
        ident = cst[:, 0:128]
        ustrict = cst[:, 128:256]
        chunkind = cst[:, 256:258]
        ones_row = cst[0:1, 384:512]

        K.load(cst[:], c_d, ["const"])
        K.load(gb[:], g_d[0:1, :].broadcast_to([128, D]), ["const"])
        K.load(ghb[:], gh_d[0:1, :].broadcast_to([128, D]), ["const"])
        K.load(wgu[:], wgu_d, ["const"])
        K.load(bg[:], bg_d, ["const"])
        for c in range(8):
            K.load(win[:, c, :], win_d[c * 128:(c + 1) * 128, :], ["const"])
            K.load(wout[:, c, :], wout_d[c * 128:(c + 1) * 128, :], ["const"])
        P.op("dve", lambda e: e.memset(S[:], 0.0), writes=["S"])
        P.op("dve", lambda e: e.memset(qTm[:], 0.0), writes=["qT"])

        def proj(col0, ncol, pb, kb, pcol=0):
            for c in range(8):
                P.op("pe", lambda e, c=c: e.matmul(pb[:, pcol:pcol + ncol], lhsT=xnT[:, c, :], rhs=win[:, c, col0:col0 + ncol],
                                                   start=(c == 0), stop=(c == 7)), reads=["xnT", "const"], writes=[kb])

        for it in range(NT):
            rows = slice(it * 128, (it + 1) * 128)
            K.load(xt[:], x_d[rows, :], ["xt"], reads=[(xkey, it)])
            K.rmsnorm(xt, gb, xn, sq, ms, rstd, "xt", "xn", "")
            K.transpose_cols(xn, "xn", xnT, "xnT", 8, ident)
            pb, kb = K.bank()
            for c in range(8):
                P.op("pe", lambda e, c=c, pb=pb: e.matmul(pb[0:16, 0:128], lhsT=win[:, c, 3072:3088], rhs=xnT[:, c, :],
                                                         start=(c == 0), stop=(c == 7)), reads=["xnT", "const"], writes=[kb])
            P.op("act", lambda e, pb=pb: e.copy(out=glowT[:], in_=pb[0:16, 0:128]), reads=[kb], writes=["glowT"])
            pz, kz = K.bank()
            P.op("pe", lambda e: e.matmul(pz[:], lhsT=glowT[:], rhs=wgu[:], start=True, stop=False),
                 reads=["glowT", "const"], writes=[kz])
            P.op("pe", lambda e: e.matmul(pz[:], lhsT=ones_row, rhs=bg[:], start=False, stop=True),
                 reads=["const"], writes=[kz])
            P.op("act", lambda e: e.copy(out=ls[:], in_=pz[:]), reads=[kz], writes=["ls"])
            P.op("dve", lambda e: e.scalar_tensor_tensor(out=ta[:], in0=pz[:], scalar=-1.0, in1=ls[:], op0=ALU.mult, op1=ALU.min),
                 reads=[kz, "ls"], writes=["ta"])
            P.op("act", lambda e: e.activation(out=ta[:], in_=ta[:], func=AF.Exp), reads=["ta"], writes=["ta"])
            P.op("act", lambda e: e.activation(out=ta[:], in_=ta[:], func=AF.Ln, bias=1.0), reads=["ta"], writes=["ta"])
            P.op("dve", lambda e: e.scalar_tensor_tensor(out=ls[:], in0=pz[:], scalar=0.0, in1=ta[:], op0=ALU.min, op1=ALU.subtract),
                 reads=[kz, "ta"], writes=["ls"])
            pd, kd = K.bank()
            P.op("pe", lambda e: e.matmul(pd[:], lhsT=ustrict, rhs=ls[:], start=True, stop=True), reads=["ls", "const"], writes=[kd])
            P.op("act", lambda e: e.activation(out=expD[:], in_=pd[:], func=AF.Exp, scale=1.0 / 16.0), reads=[kd], writes=["expD"])
            pt, kt = K.bank()
            for h in range(4):
                P.op("pe", lambda e, h=h: e.matmul(pt[:, h * 2:(h + 1) * 2], lhsT=ls[:, h * 128:(h + 1) * 128], rhs=chunkind,
                                                   start=True, stop=True), reads=["ls", "const"], writes=[kt])
            P.op("act", lambda e: e.activation(out=etot[:], in_=pt[:, 0:8], func=AF.Exp, scale=1.0 / 16.0), reads=[kt], writes=["etot"])
            pk, kk = K.bank()
            proj(512, 512, pk, kk)
            P.op("dve", lambda e: e.tensor_tensor(out=kdec[:], in0=pk[:], in1=expD[:], op=ALU.mult), reads=[kk, "expD"], writes=["kdec"])
            for hf in range(2):
                pv, kv = K.bank()
                proj(1024 + hf * 512, 512, pv, kv)
                P.op("act", lambda e, hf=hf, pv=pv: e.copy(out=vv[:, hf * 512:(hf + 1) * 512], in_=pv[:]), reads=[kv], writes=["vv"])
            for hf in range(2):
                pr, kr = K.bank()
                proj(2048 + hf * 512, 512, pr, kr)
                P.op("act", lambda e, hf=hf, pr=pr: e.activation(out=gsil[:, hf * 512:(hf + 1) * 512], in_=pr[:], func=AF.Silu),
                     reads=[kr], writes=["gsil"])
            P.op("dve", lambda e: e.tensor_tensor(out=gsil[:], in0=gsil[:], in1=ghb[:], op=ALU.mult), reads=["gsil", "const"], writes=["gsil"])
            pq, kq = K.bank()
            for h in range(4):
                for c in range(8):
                    P.op("pe", lambda e, h=h, c=c: e.matmul(pq[:, h * 128:(h + 1) * 128], lhsT=win[:, c, h * 128:(h + 1) * 128],
                                                            rhs=xnT[:, c, :], start=(c == 0), stop=(c == 7)),
                         reads=["xnT", "const"], writes=[kq])
            for j in range(2):
                P.op("act", lambda e, j=j: e.mul(out=qTm[:, j, :].rearrange("p (h t) -> p h t", t=128)[:, :, j * 64:(j + 1) * 64],
                                                 in_=pq[:].rearrange("p (h t) -> p h t", t=128)[:, :, j * 64:(j + 1) * 64],
                                                 mul=128.0 ** -0.5), reads=[kq], writes=["qT"])
            po = [K.bank() for _ in range(4)]
            for j in range(2):
                tk = slice(j * 64, (j + 1) * 64)
                for hp in range(2):
                    pS, kS = K.bank()
                    for hh in range(2):
                        h = hp * 2 + hh
                        P.op("pe", lambda e, h=h, hh=hh, pS=pS, tk=tk: e.matmul(pS[:, hh * 256:(hh + 1) * 256], lhsT=kdec[tk, h * 128:(h + 1) * 128],
                                                                         rhs=vv[tk, h * 256:(h + 1) * 256], start=True, stop=True),
                             reads=["kdec", "vv"], writes=[kS])
                    for hh in range(2):
                        h = hp * 2 + hh
                        P.op("dve", lambda e, h=h, hh=hh, pS=pS, j=j: e.scalar_tensor_tensor(
                            out=S[:, h, :], in0=S[:, h, :], scalar=etot[:, h * 2 + j:h * 2 + j + 1], in1=pS[:, hh * 256:(hh + 1) * 256],
                            op0=ALU.mult, op1=ALU.add), reads=[kS, "etot", ("S", h)], writes=[("S", h)])
                for h in range(4):
                    pb_, kb_ = po[h]
                    P.op("pe", lambda e, h=h, pb_=pb_, j=j: e.matmul(pb_[:, 0:256], lhsT=qTm[:, j, h * 128:(h + 1) * 128],
                                                               rhs=S[:, h, :], start=(j == 0), stop=(j == 1)), reads=["qT", ("S", h), "S"], writes=[kb_])
            for h in range(4):
                pb_, kb_ = po[h]
                P.op("act", lambda e, h=h, pb_=pb_: e.activation(out=sq[:, 0:256], in_=pb_[:, 0:256], func=AF.Square,
                                                               accum_out=ss[:, h:h + 1]), reads=[kb_], writes=["sq", ("ss", h)])
            P.op("act", lambda e: e.activation(out=ss[:], in_=ss[:], func=AF.Sqrt, scale=1.0 / 256.0, bias=EPS),
                 reads=[("ss", h) for h in range(4)], writes=[("ss", h) for h in range(4)])
            P.op("dve", lambda e: e.reciprocal(out=ss[:], in_=ss[:]), reads=[("ss", h) for h in range(4)], writes=[("ss", h) for h in range(4)])
            for h in range(4):
                pb_, kb_ = po[h]
                P.op("dve", lambda e, h=h, pb_=pb_: e.scalar_tensor_tensor(
                    out=og[:, h * 256:(h + 1) * 256], in0=pb_[:, 0:256], scalar=ss[:, h:h + 1],
                    in1=gsil[:, h * 256:(h + 1) * 256], op0=ALU.mult, op1=ALU.mult), reads=[kb_, ("ss", h), "gsil"], writes=["og"])
            K.transpose_cols(og, "og", ogT, "ogT", 8, ident)
            for hf in range(2):
                pm, km = K.bank()
                for c in range(8):
                    P.op("pe", lambda e, c=c, hf=hf, pm=pm: e.matmul(pm[:], lhsT=ogT[:, c, :], rhs=wout[:, c, hf * 512:(hf + 1) * 512],
                                                                     start=(c == 0), stop=(c == 7)), reads=["ogT", "const"], writes=[km])
                P.op("dve", lambda e, hf=hf, pm=pm: e.tensor_tensor(out=yo[:, hf * 512:(hf + 1) * 512], in0=pm[:], in1=xt[:, hf * 512:(hf + 1) * 512],
                                                                    op=ALU.add), reads=[km, "xt"], writes=["yo"])
            P.dma("sp", lambda e, rows=rows: e.dma_start(out=y_d[rows, :], in_=yo[:]), reads=["yo"], writes=[(ykey, it)])
        P.barrier()
        P.emit(block)


def gla_inputs(x_rows, g, win, wgu, bg, gh, wout):
    return {"x": np.ascontiguousarray(x_rows), "g": g.reshape(1, D), "win": win, "wgu": wgu, "bg": bg.reshape(1, 512),
            "gh": gh.reshape(1, D), "wout": wout, "consts": _consts()}


DH = 512


def emit_mlstm(K, NT, io, pfx, xkey, ykey, block, half=False):
    nc = K.nc
    x_d = io["x"]
    g_d = io["g"]
    winl_d = io["winl"]
    wg_d = io["wg"]
    bif_d = io["bif"]
    cw_d = io["cw"]
    cb_d = io["cb"]
    bd_d = io["bd"]
    sk_d = io["skip"]
    gh_d = io["gh"]
    wout_d = io["wout"]
    c_d = io["consts"]
    y_d = io["y"]
    with ExitStack() as st:
        K.st = st
        P = K.P
        sb = lambda n_, s_, dt_=F32: K.sb(pfx + n_, s_, dt_)
        cst = sb("cst", [128, 768])
        gb = sb("gb", [128, D])
        wg = sb("wg_s", [128, 8, 8])
        bif = sb("bif_s", [128, 8])
        cw = sb("cw_s", [128, 4, 16])
        cb = sb("cb_s", [128, 16])
        skp = sb("skp_s", [128, 16])
        ghd = sb("ghd_s", [128, 16])
        xt = sb("xt", [128, D])
        sq = sb("sq", [128, D])
        ms = sb("ms", [128, 1])
        rstd = sb("rstd", [128, 1])
        xn = sb("xn", [128, D])
        xnT = sb("xnT", [128, 8, 128])
        wch = [sb("wch%d" % i, [128, 8, 128]) for i in range(4)]
        bdc = [sb("bdc%d" % i, [128, 128]) for i in range(6)]
        woc = [sb("woc%d" % i, [128, D]) for i in range(3)]
        xmT = sb("xmT", [128, 16, 131])
        xcT = sb("xcT", [128, 16, 128])
        tmpc = sb("tmpc", [128, 16, 128])
        szT = sb("szT", [128, 16, 128])
        qTm = sb("qTm", [128, 2, 16, 128])
        kw = sb("kw", [128, 2048])
        vv = sb("vv", [128, 2048])
        hh = sb("hh", [128, 2048])
        fin = sb("fin", [128, 16, 128])
        C = sb("C", [128, 16, 512])
        nst = sb("nst", [128, 16])
        gtk = sb("gtk", [128, 8])
        gta = sb("gta", [128, 4])
        lf = sb("lf", [128, 4])
        sm4 = sb("sm4", [4, 128])
        cumT = sb("cumT", [4, 128])
        dT = sb("dT", [4, 128])
        wlT = sb("wlT", [4, 128])
        wexT = sb("wexT", [4, 128])
        mst = sb("mst", [4, 1])
        t1 = sb("t1", [4, 1])
        mnew = sb("mnew", [4, 1])
        nm = sb("nm", [4, 1])
        Mx = sb("Mx", [4, 1])
        smq = sb("smq", [4, 4])
        msk = sb("msk", [4, 4, 4])
        bc = sb("bc", [128, 16])
        wtok = sb("wtok", [128, 4])
        qn = sb("qn", [128, 4])
        en = sb("en", [128, 4])
        den = sb("den", [128, 4])
        stats = sb("stats", [128, 4, 6])
        mv = sb("mv", [128, 4, 2])
        rs4 = sb("rs4", [128, 4])
        yo = sb("yo", [128, D])
        ident = cst[:, 0:128]
        ones_col = cst[:, 258:259]
        ones4 = cst[0:4, 384:512]
        tincl = cst[:, 512:640]
        I4 = cst[0:4, 0:4]

        K.load(cst[:], c_d, ["const"])
        K.load(gb[:], g_d[0:1, :].broadcast_to([128, D]), ["const"])
        K.load(wg[:].rearrange("p a b -> p (a b)"), wg_d, ["const"])
        K.load(bif[:], bif_d[0:1, :].broadcast_to([128, 8]), ["const"])
        K.load(cw[:].rearrange("p a b -> p (a b)"), cw_d, ["const"])
        K.load(cb[:], cb_d, ["const"])
        K.load(skp[:], sk_d, ["const"])
        K.load(ghd[:], gh_d, ["const"])
        P.op("dve", lambda e: e.memset(C[:], 0.0), writes=["C"])
        P.op("dve", lambda e: e.memset(nst[:], 0.0), writes=["nst"])
        P.op("dve", lambda e: e.memset(mst[:], 0.0), writes=["mst"])
        P.op("dve", lambda e: e.memset(qTm[:], 0.0), writes=["qTm"])
        P.op("dve", lambda e: e.memset(xmT[:], 0.0), writes=["xmT"])
        nw = [0]
        nb = [0]
        nwo = [0]

        fl_d = io.get("flag")
        rs_d = io.get("rowsel")
        if half:
            rsel = sb("rsel", [128, NT // 2], I32)
            flg = sb("flg", [128, 1])
            K.load(rsel[:], rs_d, ["const"])
            K.load(flg[:], fl_d, ["const"])
            plan = [(it, False, None, None) for it in range(NT // 2)] + [None] + [(None, True, j, j) for j in range(NT // 2)]
        else:
            plan = [(it, True, it, None) for it in range(NT)]
        for step in plan:
            if step is None:
                P.op("dve", lambda e: e.tensor_scalar(out=C[:], in0=C[:], scalar1=flg[:, 0:1], scalar2=None, op0=ALU.mult),
                     reads=["const", "C"] + [("C", i_) for i_ in range(16)], writes=["C"] + [("C", i_) for i_ in range(16)])
                P.op("dve", lambda e: e.tensor_scalar(out=nst[:], in0=nst[:], scalar1=flg[:, 0:1], scalar2=None, op0=ALU.mult),
                     reads=["const", "nst"], writes=["nst"])
                P.op("dve", lambda e: e.tensor_scalar(out=mst[:], in0=mst[:], scalar1=flg[0:4, 0:1], scalar2=None, op0=ALU.mult),
                     reads=["const", "mst"], writes=["mst"])
                P.op("dve", lambda e: e.tensor_scalar(out=xmT[:, :, 0:3], in0=xmT[:, :, 0:3], scalar1=flg[:, 0:1], scalar2=None, op0=ALU.mult),
                     reads=["const", "xmT"], writes=["xmT"])
                continue
            it, full, slot, rcol = step
            if rcol is None:
                K.load(xt[:], x_d[it * 128:(it + 1) * 128, :], ["xt"], reads=[(xkey, it)])
            else:
                P.dma("pool", lambda e, rcol=rcol: e.indirect_dma_start(
                    out=xt[:, :], out_offset=None, in_=x_d[:, :],
                    in_offset=bass.IndirectOffsetOnAxis(ap=rsel[:, rcol:rcol + 1], axis=0)),
                    reads=["const"] + [(xkey, j_) for j_ in range(NT)], writes=["xt"])
            if full:
                rows = slice(slot * 128, (slot + 1) * 128)
            K.rmsnorm(xt, gb, xn, sq, ms, rstd, "xt", "xn", "")
            K.transpose_cols(xn, "xn", xnT, "xnT", 8, ident)
            pg, kg = K.bank()
            for c in range(8):
                P.op("pe", lambda e, c=c, pg=pg: e.matmul(pg[:, 0:8], lhsT=xnT[:, c, :], rhs=wg[:, c, :], start=(c == 0), stop=(c == 7)),
                     reads=["xnT", "const"], writes=[kg])
            P.op("dve", lambda e, pg=pg: e.tensor_tensor(out=gtk[:], in0=pg[:, 0:8], in1=bif[:], op=ALU.add), reads=[kg, "const"], writes=["gtk"])
            P.op("dve", lambda e: e.scalar_tensor_tensor(out=gta[:], in0=gtk[:, 4:8], scalar=-1.0, in1=gtk[:, 4:8], op0=ALU.mult, op1=ALU.min),
                 reads=["gtk"], writes=["gta"])
            P.op("act", lambda e: e.activation(out=gta[:], in_=gta[:], func=AF.Exp), reads=["gta"], writes=["gta"])
            P.op("act", lambda e: e.activation(out=gta[:], in_=gta[:], func=AF.Ln, bias=1.0), reads=["gta"], writes=["gta"])
            P.op("dve", lambda e: e.scalar_tensor_tensor(out=lf[:], in0=gtk[:, 4:8], scalar=0.0, in1=gta[:], op0=ALU.min, op1=ALU.subtract),
                 reads=["gtk", "gta"], writes=["lf"])
            pc, kc_ = K.bank()
            P.op("pe", lambda e, pc=pc: e.matmul(pc[0:4, 0:128], lhsT=lf[:], rhs=tincl, start=True, stop=True), reads=["lf", "const"], writes=[kc_])
            P.op("pe", lambda e, pc=pc: e.matmul(pc[0:4, 128:256], lhsT=gtk[:, 0:4], rhs=ident, start=True, stop=True), reads=["gtk", "const"], writes=[kc_])
            P.op("act", lambda e, pc=pc: e.copy(out=cumT[:], in_=pc[0:4, 0:128]), reads=[kc_], writes=["cumT"])
            P.op("dve", lambda e, pc=pc: e.tensor_tensor(out=dT[:], in0=pc[0:4, 128:256], in1=cumT[:], op=ALU.subtract), reads=[kc_, "cumT"], writes=["dT"])
            for j in range(2):
                tk = slice(j * 64, (j + 1) * 64)
                tot = cumT[:, j * 64 + 63:j * 64 + 64]
                P.op("dve", lambda e, tk=tk, tot=tot: e.tensor_scalar(out=wlT[:, tk], in0=dT[:, tk], scalar1=tot, scalar2=None, op0=ALU.add),
                     reads=["dT", "cumT"], writes=["wlT"])
                P.op("dve", lambda e, tk=tk: e.tensor_reduce(out=Mx[:], in_=wlT[:, tk], axis=AX.X, op=ALU.max), reads=["wlT"], writes=["Mx"])
                P.op("dve", lambda e, tot=tot: e.tensor_tensor(out=t1[:], in0=mst[:], in1=tot, op=ALU.add), reads=["mst", "cumT"], writes=["t1"])
                P.op("dve", lambda e: e.tensor_tensor(out=mnew[:], in0=t1[:], in1=Mx[:], op=ALU.max), reads=["t1", "Mx"], writes=["mnew"])
                P.op("dve", lambda e: e.tensor_scalar(out=nm[:], in0=mnew[:], scalar1=-1.0, scalar2=None, op0=ALU.mult), reads=["mnew"], writes=["nm"])
                P.op("act", lambda e, j=j: e.activation(out=smq[:, j:j + 1], in_=t1[:], func=AF.Exp, bias=nm[:, 0:1]), reads=["t1", "nm"], writes=["smq"])
                P.op("act", lambda e, j=j: e.activation(out=smq[:, 2 + j:3 + j], in_=nm[:], func=AF.Exp), reads=["nm"], writes=["smq"])
                P.op("act", lambda e, tk=tk: e.activation(out=wexT[:, tk], in_=wlT[:, tk], func=AF.Exp, bias=nm[:, 0:1]), reads=["wlT", "nm"], writes=["wexT"])
                P.op("dve", lambda e: e.tensor_copy(out=mst[:], in_=mnew[:]), reads=["mnew"], writes=["mst"])
            P.op("dve", lambda e: e.tensor_tensor(out=msk[:], in0=smq[:].unsqueeze(2).broadcast_to([4, 4, 4]),
                                                  in1=I4.unsqueeze(1).broadcast_to([4, 4, 4]), op=ALU.mult), reads=["smq", "const"], writes=["msk"])
            pbc, kbc = K.bank()
            P.op("pe", lambda e, pbc=pbc: e.matmul(pbc[:, 0:16], lhsT=ones4, rhs=msk[:].rearrange("p a b -> p (a b)"), start=True, stop=True),
                 reads=["msk", "const"], writes=[kbc])
            P.op("pe", lambda e, pbc=pbc: e.transpose(out=pbc[:, 16:20], in_=wexT[:], identity=I4), reads=["wexT", "const"], writes=[kbc])
            P.op("act", lambda e, pbc=pbc: e.copy(out=bc[:], in_=pbc[:, 0:16]), reads=[kbc], writes=["bc"])
            P.op("act", lambda e, pbc=pbc: e.copy(out=wtok[:], in_=pbc[:, 16:20]), reads=[kbc], writes=["wtok"])
            if full:
                for j in range(2):
                    tk = slice(j * 64, (j + 1) * 64)
                    P.op("act", lambda e, j=j, tk=tk: e.copy(out=en[tk, :], in_=bc[tk, (2 + j) * 4:(3 + j) * 4]), reads=["bc"], writes=["en"])
            for g0 in range(0, 16, 4):
                pb, kb = K.bank()
                for jj in range(4):
                    cc = g0 + jj
                    wb = wch[nw[0] % 4]
                    kw_ = "wch%d" % (nw[0] % 4)
                    nw[0] += 1
                    K.load(wb[:].rearrange("p a b -> p (a b)"), winl_d[cc, :, :], [kw_])
                    for c in range(8):
                        P.op("pe", lambda e, c=c, jj=jj, pb=pb, wb=wb: e.matmul(pb[:, jj * 128:(jj + 1) * 128], lhsT=wb[:, c, :], rhs=xnT[:, c, :],
                                                                                  start=(c == 0), stop=(c == 7)), reads=["xnT", kw_], writes=[kb])
                P.op("act", lambda e, g0=g0, pb=pb: e.copy(out=xmT[:, g0:g0 + 4, 3:131], in_=pb[:].rearrange("p (a b) -> p a b", b=128)),
                     reads=[kb], writes=["xmT"])
            for w_ in range(4):
                cwb = cw[:, w_, :].unsqueeze(2).broadcast_to([128, 16, 128])
                if w_ == 0:
                    P.op("dve", lambda e, cwb=cwb: e.tensor_tensor(out=xcT[:], in0=xmT[:, :, 0:128], in1=cwb, op=ALU.mult),
                         reads=["xmT", "const"], writes=["xcT"])
                else:
                    P.op("dve", lambda e, cwb=cwb, w_=w_: e.tensor_tensor(out=tmpc[:], in0=xmT[:, :, w_:w_ + 128], in1=cwb, op=ALU.mult),
                         reads=["xmT", "const"], writes=["tmpc"])
                    P.op("dve", lambda e: e.tensor_tensor(out=xcT[:], in0=xcT[:], in1=tmpc[:], op=ALU.add), reads=["xcT", "tmpc"], writes=["xcT"])
            P.op("dve", lambda e: e.tensor_tensor(out=xcT[:], in0=xcT[:], in1=cb[:].unsqueeze(2).broadcast_to([128, 16, 128]), op=ALU.add),
                 reads=["xcT", "const"], writes=["xcT"])
            P.op("act", lambda e: e.activation(out=xcT[:], in_=xcT[:], func=AF.Silu), reads=["xcT"], writes=["xcT"])
            if full:
                for g0 in range(0, 16, 4):
                    pb, kb = K.bank()
                    for jj in range(4):
                        cc = g0 + jj
                        wb = wch[nw[0] % 4]
                        kw_ = "wch%d" % (nw[0] % 4)
                        nw[0] += 1
                        K.load(wb[:].rearrange("p a b -> p (a b)"), winl_d[16 + cc, :, :], [kw_])
                        for c in range(8):
                            P.op("pe", lambda e, c=c, jj=jj, pb=pb, wb=wb: e.matmul(pb[:, jj * 128:(jj + 1) * 128], lhsT=wb[:, c, :], rhs=xnT[:, c, :],
                                                                                      start=(c == 0), stop=(c == 7)), reads=["xnT", kw_], writes=[kb])
                    P.op("act", lambda e, g0=g0, pb=pb: e.activation(out=szT[:, g0:g0 + 4, :], in_=pb[:].rearrange("p (a b) -> p a b", b=128), func=AF.Silu),
                         reads=[kb], writes=["szT"])
            def bdload(which, cc):
                b_ = bdc[nb[0] % 6]
                k_ = "bdc%d" % (nb[0] % 6)
                nb[0] += 1
                K.load(b_[:], bd_d[which, cc, :, :], [k_])
                return b_, k_
            if full:
                for g0 in range(0, 16, 4):
                    pb, kb = K.bank()
                    for jj in range(4):
                        cc = g0 + jj
                        b_, k_ = bdload(0, cc)
                        P.op("pe", lambda e, jj=jj, cc=cc, pb=pb, b_=b_: e.matmul(pb[:, jj * 128:(jj + 1) * 128], lhsT=b_[:], rhs=xcT[:, cc, :], start=True, stop=True),
                             reads=["xcT", k_], writes=[kb])
                    for j in range(2):
                        P.op("act", lambda e, j=j, g0=g0, pb=pb: e.copy(out=qTm[:, j, g0:g0 + 4, j * 64:(j + 1) * 64],
                                                                        in_=pb[:].rearrange("p (a b) -> p a b", b=128)[:, :, j * 64:(j + 1) * 64]),
                             reads=[kb], writes=["qTm"])
            for h in range(4):
                pb, kb = K.bank()
                for jj in range(4):
                    cc = h * 4 + jj
                    b_, k_ = bdload(1, cc)
                    P.op("pe", lambda e, jj=jj, cc=cc, pb=pb, b_=b_: e.matmul(pb[:, jj * 128:(jj + 1) * 128], lhsT=xcT[:, cc, :], rhs=b_[:], start=True, stop=True),
                         reads=["xcT", k_], writes=[kb])
                P.op("dve", lambda e, h=h, pb=pb: e.tensor_scalar(out=kw[:, h * 512:(h + 1) * 512], in0=pb[:], scalar1=wtok[:, h:h + 1], scalar2=float(DH) ** -0.5,
                                                                  op0=ALU.mult, op1=ALU.mult), reads=[kb, "wtok"], writes=["kw"])
            for h in range(4):
                pb, kb = K.bank()
                for jj in range(4):
                    cc = h * 4 + jj
                    b_, k_ = bdload(2, cc)
                    P.op("pe", lambda e, jj=jj, cc=cc, pb=pb, b_=b_: e.matmul(pb[:, jj * 128:(jj + 1) * 128], lhsT=xmT[:, cc, 3:131], rhs=b_[:], start=True, stop=True),
                         reads=["xmT", k_], writes=[kb])
                P.op("act", lambda e, h=h, pb=pb: e.copy(out=vv[:, h * 512:(h + 1) * 512], in_=pb[:]), reads=[kb], writes=["vv"])
            P.op("dve", lambda e: e.tensor_copy(out=xmT[:, :, 0:3], in_=xmT[:, :, 128:131]), reads=["xmT"], writes=["xmT"])
            K.bi = 0
            pnum = [(K.banks[h], ("pb", h)) for h in range(4)]
            pqn, kqn = K.banks[4], ("pb", 4)
            pdn, kdn = K.banks[5], ("pb", 5)
            ci = 0
            for j in range(2):
                tk = slice(j * 64, (j + 1) * 64)
                for h in range(4):
                    abc = bc[:, j * 4 + h:j * 4 + h + 1]
                    for dk in range(4):
                        pC, kC = K.banks[6 + ci % 2], ("pb", 6 + ci % 2)
                        ci += 1
                        P.op("pe", lambda e, h=h, dk=dk, tk=tk, pC=pC: e.matmul(pC[:], lhsT=kw[tk, h * 512 + dk * 128:h * 512 + (dk + 1) * 128],
                                                                             rhs=vv[tk, h * 512:(h + 1) * 512], start=True, stop=True),
                             reads=["kw", "vv"], writes=[kC])
                        P.op("dve", lambda e, h=h, dk=dk, pC=pC, abc=abc: e.scalar_tensor_tensor(
                            out=C[:, h * 4 + dk, :], in0=C[:, h * 4 + dk, :], scalar=abc, in1=pC[:], op0=ALU.mult, op1=ALU.add),
                            reads=[kC, "bc", ("C", h * 4 + dk), "C"], writes=[("C", h * 4 + dk)])
                for h in range(4):
                    for dk in range(4):
                        P.op("pe", lambda e, h=h, dk=dk, tk=tk: e.matmul(pdn[:, h * 4 + dk:h * 4 + dk + 1], lhsT=kw[tk, h * 512 + dk * 128:h * 512 + (dk + 1) * 128],
                                                                        rhs=ones_col[tk, :], start=True, stop=True), reads=["kw", "const"], writes=[kdn])
                for h in range(4):
                    abc = bc[:, j * 4 + h:j * 4 + h + 1]
                    P.op("dve", lambda e, h=h, abc=abc: e.scalar_tensor_tensor(out=nst[:, h * 4:(h + 1) * 4], in0=nst[:, h * 4:(h + 1) * 4], scalar=abc,
                                                                              in1=pdn[:, h * 4:(h + 1) * 4], op0=ALU.mult, op1=ALU.add),
                         reads=[kdn, "bc", "nst"], writes=["nst"])
                if full:
                    for h in range(4):
                        pb_, kb_ = pnum[h]
                        for dk in range(4):
                            P.op("pe", lambda e, h=h, dk=dk, j=j, pb_=pb_: e.matmul(pb_[:], lhsT=qTm[:, j, h * 4 + dk, :], rhs=C[:, h * 4 + dk, :],
                                                                                 start=(j == 0 and dk == 0), stop=(j == 1 and dk == 3)),
                                 reads=["qTm", ("C", h * 4 + dk), "C"], writes=[kb_])
                    for h in range(4):
                        for dk in range(4):
                            P.op("pe", lambda e, h=h, dk=dk, j=j: e.matmul(pqn[:, j * 4 + h:j * 4 + h + 1], lhsT=qTm[:, j, h * 4 + dk, :], rhs=nst[:, h * 4 + dk:h * 4 + dk + 1],
                                                                          start=(dk == 0), stop=(dk == 3)), reads=["qTm", "nst"], writes=[kqn])
            if full:
                P.op("act", lambda e: e.copy(out=qn[:], in_=pqn[:, 0:4]), reads=[kqn], writes=["qn"])
                P.op("dve", lambda e: e.tensor_tensor(out=qn[:], in0=qn[:], in1=pqn[:, 4:8], op=ALU.add), reads=[kqn, "qn"], writes=["qn"])
                P.op("dve", lambda e: e.scalar_tensor_tensor(out=den[:], in0=qn[:], scalar=-1.0, in1=qn[:], op0=ALU.mult, op1=ALU.max), reads=["qn"], writes=["den"])
                P.op("dve", lambda e: e.tensor_tensor(out=den[:], in0=den[:], in1=en[:], op=ALU.max), reads=["den", "en"], writes=["den"])
                P.op("dve", lambda e: e.reciprocal(out=den[:], in_=den[:]), reads=["den"], writes=["den"])
                for h in range(4):
                    pb_, kb_ = pnum[h]
                    P.op("dve", lambda e, h=h, pb_=pb_: e.tensor_scalar(out=hh[:, h * 512:(h + 1) * 512], in0=pb_[:], scalar1=den[:, h:h + 1], scalar2=None, op0=ALU.mult),
                         reads=[kb_, "den"], writes=[("hh", h)])
                    P.op("dve", lambda e, h=h: e.bn_stats(out=stats[:, h, :], in_=hh[:, h * 512:(h + 1) * 512]), reads=[("hh", h)], writes=[("stats", h)])
                    P.op("dve", lambda e, h=h: e.bn_aggr(out=mv[:, h, :], in_=stats[:, h, :]), reads=[("stats", h)], writes=[("mv", h)])
                P.op("act", lambda e: e.activation(out=rs4[:], in_=mv[:, :, 1], func=AF.Sqrt, bias=EPS), reads=[("mv", h) for h in range(4)], writes=["rs4"])
                P.op("dve", lambda e: e.reciprocal(out=rs4[:], in_=rs4[:]), reads=["rs4"], writes=["rs4"])
                for h in range(4):
                    P.op("dve", lambda e, h=h: e.tensor_scalar(out=hh[:, h * 512:(h + 1) * 512], in0=hh[:, h * 512:(h + 1) * 512], scalar1=mv[:, h, 0:1],
                                                               scalar2=rs4[:, h:h + 1], op0=ALU.subtract, op1=ALU.mult),
                         reads=[("hh", h), ("mv", h), "rs4"], writes=[("hh", h)])
                for g0 in range(0, 16, 4):
                    pb, kb = K.bank()
                    for jj in range(4):
                        cc = g0 + jj
                        P.op("pe", lambda e, jj=jj, cc=cc, pb=pb: e.transpose(out=pb[:, jj * 128:(jj + 1) * 128], in_=hh[:, cc * 128:(cc + 1) * 128], identity=ident),
                             reads=[("hh", cc // 4), "const"], writes=[kb])
                    gsl = slice(g0, g0 + 4)
                    P.op("dve", lambda e, gsl=gsl, pb=pb: e.tensor_tensor(out=fin[:, gsl, :], in0=pb[:].rearrange("p (a b) -> p a b", b=128),
                                                                           in1=ghd[:, gsl].unsqueeze(2).broadcast_to([128, 4, 128]), op=ALU.mult),
                         reads=[kb, "const"], writes=[("fin", g0)])
                    P.op("dve", lambda e, gsl=gsl: e.tensor_tensor(out=tmpc[:, gsl, :], in0=xcT[:, gsl, :], in1=skp[:, gsl].unsqueeze(2).broadcast_to([128, 4, 128]), op=ALU.mult),
                         reads=["xcT", "const"], writes=["tmpc"])
                    P.op("dve", lambda e, gsl=gsl: e.tensor_tensor(out=fin[:, gsl, :], in0=fin[:, gsl, :], in1=tmpc[:, gsl, :], op=ALU.add),
                         reads=[("fin", g0), "tmpc"], writes=[("fin", g0)])
                    P.op("dve", lambda e, gsl=gsl: e.tensor_tensor(out=fin[:, gsl, :], in0=fin[:, gsl, :], in1=szT[:, gsl, :], op=ALU.mult),
                         reads=[("fin", g0), "szT"], writes=[("fin", g0)])
                pm0, km0 = K.bank()
                pm1, km1 = K.bank()
                for cc in range(16):
                    wb = woc[nwo[0] % 3]
                    kwo = "woc%d" % (nwo[0] % 3)
                    nwo[0] += 1
                    K.load(wb[:], wout_d[cc * 128:(cc + 1) * 128, :], [kwo])
                    P.op("pe", lambda e, cc=cc, wb=wb: e.matmul(pm0[:], lhsT=fin[:, cc, :], rhs=wb[:, 0:512], start=(cc == 0), stop=(cc == 15)),
                         reads=[("fin", (cc // 4) * 4), kwo], writes=[km0])
                    P.op("pe", lambda e, cc=cc, wb=wb: e.matmul(pm1[:], lhsT=fin[:, cc, :], rhs=wb[:, 512:1024], start=(cc == 0), stop=(cc == 15)),
                         reads=[("fin", (cc // 4) * 4), kwo], writes=[km1])
                P.op("dve", lambda e: e.tensor_tensor(out=yo[:, 0:512], in0=pm0[:], in1=xt[:, 0:512], op=ALU.add), reads=[km0, "xt"], writes=["yo"])
                P.op("dve", lambda e: e.tensor_tensor(out=yo[:, 512:1024], in0=pm1[:], in1=xt[:, 512:1024], op=ALU.add), reads=[km1, "xt"], writes=["yo"])
                P.dma("sp", lambda e, rows=rows: e.dma_start(out=y_d[rows, :], in_=yo[:]), reads=["yo"], writes=[(ykey, slot)])
        P.barrier()
        P.emit(block)


def mlstm_inputs(x_rows, g, win, b_i, b_f, conv_w, conv_b, w_q, w_k, w_v, skip, gh, wout):
    winl = np.ascontiguousarray(win[:, 0:4096].reshape(8, 128, 32, 128).transpose(2, 1, 0, 3)).reshape(32, 128, 1024)
    wg = np.ascontiguousarray(win[:, 4096:4104].reshape(8, 128, 8).transpose(1, 0, 2)).reshape(128, 64)
    bif = np.concatenate([b_i.reshape(-1), b_f.reshape(-1)]).reshape(1, 8).astype(np.float32)
    cw = np.ascontiguousarray(conv_w.reshape(4, 16, 128).transpose(2, 0, 1)).reshape(128, 64)
    cb = np.ascontiguousarray(conv_b.reshape(16, 128).T)
    bd = np.zeros((3, 16, 128, 128), np.float32)
    for a, w_ in enumerate((w_q, w_k, w_v)):
        wr = w_.reshape(16, 32, 4, 4)
        for n in range(32):
            bd[a, :, n * 4:(n + 1) * 4, n * 4:(n + 1) * 4] = wr[:, n]
    return {"x": np.ascontiguousarray(x_rows), "g": g.reshape(1, D), "winl": winl, "wg": wg, "bif": bif, "cw": cw, "cb": cb, "bd": bd,
            "skip": np.ascontiguousarray(skip.reshape(16, 128).T), "gh": np.ascontiguousarray(gh.reshape(16, 128).T), "wout": wout,
            "consts": _consts()}


_IO = {
    "peer": [("g", [1, D]), ("gf", [1, D]), ("wq", [D, 2048]), ("skT", [128, 2048]), ("uv", [16384, 2 * D])],
    "gla": [("g", [1, D]), ("win", [D, 3088]), ("wgu", [16, 512]), ("bg", [1, 512]), ("gh", [1, D]), ("wout", [D, D])],
    "mlstm": [("g", [1, D]), ("winl", [32, 128, 1024]), ("wg", [128, 64]), ("bif", [1, 8]), ("cw", [128, 64]), ("cb", [128, 16]),
              ("bd", [3, 16, 128, 128]), ("skip", [128, 16]), ("gh", [128, 16]), ("wout", [2048, D])],
}
_EMIT = {"peer": emit_peer, "gla": emit_gla, "mlstm": emit_mlstm}


def build_program(NT, phases, split_last=False):
    nc = bass.Bass("TRN2", target_bir_lowering=False)
    dr = lambda n, s, kind="ExternalInput": nc.dram_tensor(n, s, F32, kind=kind).ap()
    x_d = dr("x", [NT * 128, D])
    c_d = dr("consts", [128, 768])
    NTO = NT // 2 if split_last else NT
    y_d = dr("y", [NTO * 128, D], kind="ExternalOutput")
    scratch = [dr("xa", [NT * 128, D], kind="Internal"), dr("xb", [NT * 128, D], kind="Internal")] if len(phases) > 1 else []
    ios = []
    for kind, pfx, extra in phases:
        io = {n: dr(pfx + n, shp) for n, shp in _IO[kind]}
        io["consts"] = c_d
        ios.append(io)
    if split_last:
        assert phases[-2][0] == "mlstm" and phases[-1][0] == "peer"
        ios[-2]["rowsel"] = nc.dram_tensor("rowsel", [128, NT // 2], I32, kind="ExternalInput").ap()
        ios[-2]["flag"] = nc.dram_tensor("flag", [128, 1], F32, kind="ExternalInput").ap()
        for i_, (kind_, _, _) in enumerate(phases[:-1]):
            if kind_ == "peer":
                ios[i_]["oob"] = nc.dram_tensor("oob", [128, NT], F32, kind="ExternalInput").ap()
    with ExitStack() as st0:
        K = Ctx(nc, st0)
        block = st0.enter_context(nc.Block())
        src, skey = x_d, "x_in"
        for i, (kind, pfx, extra) in enumerate(phases):
            last = i == len(phases) - 1
            dst, dkey = (y_d, "y_out") if last else (scratch[i % 2], "xs%d" % i)
            ios[i]["x"] = src
            ios[i]["y"] = dst
            if split_last and i == len(phases) - 2:
                extra = extra + (True,)
            _EMIT[kind](K, NTO if (last and split_last) else NT, ios[i], pfx, skey, dkey, block, *extra)
            src, skey = dst, dkey
        K.P.wait_all("sp", [("y_out", it) for it in range(NTO)])
        K.P.emit(block)
    return nc


def build_peer(NT, final):
    return build_program(NT, [("peer", "", (final,))])


def build_gla(NT):
    return build_program(NT, [("gla", "", ())])


def build_mlstm(NT):
    return build_program(NT, [("mlstm", "", ())])


_PHASES = [("gla", "g0_", ()), ("peer", "p0_", (False,)), ("mlstm", "m1_", ()), ("peer", "p1_", (True,))]
_NC_CACHE = {}


def kernel(x, norm_mix_g, gla_w_in, gla_w_gate_up, gla_b_gate, gla_g_head, gla_w_out,
           mlstm_w_in, mlstm_b_i, mlstm_b_f, mlstm_conv_w, mlstm_conv_b, mlstm_w_q, mlstm_w_k,
           mlstm_w_v, mlstm_skip, mlstm_g_head, mlstm_w_out, norm_ffn_g, peer_w_query,
           peer_sub_keys, peer_u, peer_v, norm_final_g):
    f = lambda a: np.ascontiguousarray(np.asarray(a, dtype=np.float32))
    x = f(x)
    B, S, _ = x.shape
    NT = S // 128
    dummy = np.zeros((1, D), np.float32)
    parts = [
        ("g0_", gla_inputs(dummy, f(norm_mix_g)[0], f(gla_w_in)[0], f(gla_w_gate_up)[0], f(gla_b_gate)[0], f(gla_g_head)[0], f(gla_w_out)[0])),
        ("p0_", peer_inputs(dummy, f(norm_ffn_g)[0], f(norm_final_g), f(peer_w_query)[0], f(peer_sub_keys)[0], f(peer_u)[0], f(peer_v)[0])),
        ("m1_", mlstm_inputs(dummy, f(norm_mix_g)[1], f(mlstm_w_in)[0], f(mlstm_b_i)[0], f(mlstm_b_f)[0], f(mlstm_conv_w)[0], f(mlstm_conv_b)[0],
                             f(mlstm_w_q)[0], f(mlstm_w_k)[0], f(mlstm_w_v)[0], f(mlstm_skip)[0], f(mlstm_g_head)[0], f(mlstm_w_out)[0])),
        ("p1_", peer_inputs(dummy, f(norm_ffn_g)[1], f(norm_final_g), f(peer_w_query)[1], f(peer_sub_keys)[1], f(peer_u)[1], f(peer_v)[1])),
    ]
    shared = {"consts": _consts()}
    for pfx, d in parts:
        for k, v in d.items():
            if k not in ("x", "consts"):
                shared[pfx + k] = v
    def rowsel(half):
        t = np.arange(NT // 2)[None, :] + half * (NT // 2)
        return np.ascontiguousarray((t * 128 + np.arange(128)[:, None]).astype(np.int32))
    def oob(half):
        o = np.zeros((128, NT), np.float32)
        if half == 0:
            o[:, NT // 2:] = 1.0e6
        return o
    in_maps = [dict(shared, x=x[c // 2], rowsel=rowsel(c % 2), flag=np.full((128, 1), float(c % 2), np.float32), oob=oob(c % 2))
               for c in range(8)]
    if NT not in _NC_CACHE:
        _NC_CACHE[NT] = build_program(NT, _PHASES, split_last=True)
    res = run_bass_kernel_spmd(_NC_CACHE[NT], in_maps, core_ids=list(range(8)))
    out = np.empty((B, S, D), np.float32)
    h = S // 2
    for c in range(8):
        out[c // 2, (c % 2) * h:(c % 2 + 1) * h] = res.results[c]["y"]
    return out
```
